# Optimizing a Trainium2 kernel written in Bass

```python
import jax, jax.numpy as jnp
from jax import lax
import numpy as np

D_MODEL = 1024
BATCH = 4
SEQ = 4096
DEPTH = 4
DEC_BATCH = 16
DEC_SEQ = 64
PAST_LEN = 2048

CHUNK = 64
QBLK = 128
D_MIX = D_MODEL
N_MIXERS = 4
G_WIDTH = D_MIX // N_MIXERS
HEAD_DIM = 64
N_HEADS = G_WIDTH // HEAD_DIM
CONV_W = 31
N_MEM = 256
MEM_HEADS = 4
MEM_HEAD_DIM = D_MODEL // MEM_HEADS
D_FF = 4 * D_MODEL
EPS = 1e-6
NEG_BIG = -1e30
SPLIT_SIZES = (G_WIDTH, G_WIDTH, G_WIDTH, G_WIDTH, G_WIDTH, N_HEADS, G_WIDTH, G_WIDTH, G_WIDTH, G_WIDTH, G_WIDTH, G_WIDTH, G_WIDTH)
P_IN = 12 * G_WIDTH + N_HEADS

kernel_name = 'hybrid_streaming_encoder_step'


def _rmsnorm(x, g):
    xf = x.astype(jnp.float32)
    y = xf * lax.rsqrt(jnp.mean(xf * xf, axis=-1, keepdims=True) + EPS)
    return (y * g.astype(jnp.float32)).astype(x.dtype)


def _heads(t):
    return t.reshape(t.shape[:-1] + (N_HEADS, HEAD_DIM))


def _flat(t):
    return t.reshape(t.shape[:2] + (G_WIDTH,))


def _blocks(a, n):
    b, s = a.shape[0], a.shape[1]
    return jnp.moveaxis(a.reshape((b, s // n, n) + a.shape[2:]), 1, 0)


def _unblocks(a):
    a = jnp.moveaxis(a, 0, 1)
    return a.reshape((a.shape[0], a.shape[1] * a.shape[2]) + a.shape[3:])


def _conv_module(a, g, hist, w, bias, ln_g, ln_b):
    u = a * jax.nn.sigmoid(g)
    ext = jnp.concatenate([hist.astype(u.dtype), u], axis=1)
    y = lax.conv_general_dilated(ext, w[:, None, :].astype(u.dtype), (1,), 'VALID',
                                 dimension_numbers=('NWC', 'WIO', 'NWC'),
                                 feature_group_count=G_WIDTH)
    yf = y.astype(jnp.float32) + bias.astype(jnp.float32)
    mu = jnp.mean(yf, axis=-1, keepdims=True)
    var = jnp.mean(jnp.square(yf - mu), axis=-1, keepdims=True)
    yn = (yf - mu) * lax.rsqrt(var + EPS) * ln_g.astype(jnp.float32) + ln_b.astype(jnp.float32)
    return jax.nn.silu(yn).astype(a.dtype), ext[:, -(CONV_W - 1):]


def _fox_attend(q, cq, qpos, k, v, ck, kpos):
    s = jnp.einsum('bqhd,bkhd->bhqk', q, k).astype(jnp.float32) * (HEAD_DIM ** -0.5)
    s = s + jnp.swapaxes(cq, 1, 2)[..., :, None] - jnp.swapaxes(ck, 1, 2)[..., None, :]
    s = jnp.where(kpos[None, :] <= qpos[:, None], s, NEG_BIG)
    p = jax.nn.softmax(s, axis=-1)
    return jnp.einsum('bhqk,bkhd->bqhd', p.astype(v.dtype), v)


def _sb_attend(q, qpos, k, v, kpos):
    z = jnp.einsum('bqhd,bkhd->bhqk', q, k).astype(jnp.float32) * (HEAD_DIM ** -0.5)
    m = kpos[None, :] < qpos[:, None]
    u = jnp.where(m, jax.nn.log_sigmoid(-z), 0.0)
    rest = lax.cumsum(u, axis=3, reverse=True) - u
    a = jnp.where(m, jnp.exp(jnp.where(m, jax.nn.log_sigmoid(z) + rest, 0.0)), 0.0)
    return jnp.einsum('bhqk,bkhd->bqhd', a.astype(v.dtype), v)


def _hgrn_chunk(s0, q, k, v, logf):
    b = jnp.cumsum(logf, axis=1)
    n = q.shape[1]
    m = (jnp.arange(n)[:, None] >= jnp.arange(n)[None, :])[None, :, :, None, None]
    decay = jnp.where(m, jnp.exp(jnp.minimum(b[:, :, None] - b[:, None, :], 0.0)), 0.0)
    qf = q.astype(jnp.float32)
    vf = v.astype(jnp.float32)
    scores = jnp.einsum('bthk,bshk,btshk->bhts', qf, k, decay)
    o = jnp.einsum('bthk,bhkv->bthv', qf * jnp.exp(b), s0) + jnp.einsum('bhts,bshv->bthv', scores, vf)
    b_last = b[:, -1]
    s_new = jnp.exp(b_last)[..., None] * s0 + jnp.einsum('bshk,bshv->bhkv', k * jnp.exp(b_last[:, None] - b), vf)
    return s_new, o


def _mem_attend(h, mk, mv, wq, wo):
    q = (h @ wq).reshape(h.shape[:2] + (MEM_HEADS, MEM_HEAD_DIM))
    s = jnp.einsum('bqhd,bkhd->bhqk', q, mk.astype(q.dtype)).astype(jnp.float32) * (MEM_HEAD_DIM ** -0.5)
    p = jax.nn.softmax(s, axis=-1)
    o = jnp.einsum('bhqk,bkhd->bqhd', p.astype(mv.dtype), mv)
    return o.reshape(h.shape[:2] + (MEM_HEADS * MEM_HEAD_DIM,)) @ wo


def setup_inputs(seed: int = 0) -> dict:
    key = jax.random.key(seed)
    ks = jax.random.split(key, 32)

    def nrm(i, shape, scale=1.0):
        return scale * jax.random.normal(ks[i], shape, jnp.float32)

    H, HD, G = N_HEADS, HEAD_DIM, G_WIDTH
    return {
        'x_prompt': nrm(0, (BATCH, SEQ, D_MODEL)),
        'x_sample': nrm(1, (DEC_BATCH, DEC_SEQ, D_MODEL)),
        'mem_prompt': nrm(2, (BATCH, N_MEM, D_MODEL)),
        'cache_conv': nrm(3, (DEPTH, DEC_BATCH, CONV_W - 1, G), 0.5),
        'cache_fox_k': nrm(4, (DEPTH, DEC_BATCH, PAST_LEN, H, HD)),
        'cache_fox_v': nrm(5, (DEPTH, DEC_BATCH, PAST_LEN, H, HD)),
        'cache_fox_logf': jax.nn.log_sigmoid(2.5 + nrm(6, (DEPTH, DEC_BATCH, PAST_LEN, H), 0.5)),
        'state_hgrn': nrm(7, (DEPTH, DEC_BATCH, H, HD, HD), 0.3),
        'cache_sb_k': nrm(8, (DEPTH, DEC_BATCH, PAST_LEN, H, HD)),
        'cache_sb_v': nrm(9, (DEPTH, DEC_BATCH, PAST_LEN, H, HD)),
        'cache_mem_k': nrm(10, (DEPTH, DEC_BATCH, N_MEM, MEM_HEADS, MEM_HEAD_DIM)),
        'cache_mem_v': nrm(11, (DEPTH, DEC_BATCH, N_MEM, MEM_HEADS, MEM_HEAD_DIM)),
        'norm_mix': 1.0 + nrm(12, (DEPTH, D_MODEL), 0.02),
        'w_in': nrm(13, (DEPTH, D_MODEL, P_IN), D_MODEL ** -0.5),
        'b_fox_f': 2.5 + nrm(14, (DEPTH, H), 0.5),
        'conv_w': nrm(15, (DEPTH, CONV_W, G), CONV_W ** -0.5),
        'conv_b': nrm(16, (DEPTH, G), 0.02),
        'conv_ln_g': 1.0 + nrm(17, (DEPTH, G), 0.02),
        'conv_ln_b': nrm(18, (DEPTH, G), 0.02),
        'hgrn_lb': nrm(19, (DEPTH, G), 0.5),
        'hgrn_norm': 1.0 + nrm(20, (DEPTH, HD), 0.02),
        'w_out': nrm(21, (DEPTH, D_MIX, D_MODEL), D_MIX ** -0.5),
        'norm_mem': 1.0 + nrm(22, (DEPTH, D_MODEL), 0.02),
        'w_mq': nrm(23, (DEPTH, D_MODEL, MEM_HEADS * MEM_HEAD_DIM), D_MODEL ** -0.5),
        'w_mk': nrm(24, (DEPTH, D_MODEL, MEM_HEADS * MEM_HEAD_DIM), D_MODEL ** -0.5),
        'w_mv': nrm(25, (DEPTH, D_MODEL, MEM_HEADS * MEM_HEAD_DIM), D_MODEL ** -0.5),
        'w_mo': nrm(26, (DEPTH, MEM_HEADS * MEM_HEAD_DIM, D_MODEL), (MEM_HEADS * MEM_HEAD_DIM) ** -0.5),
        'norm_ffn': 1.0 + nrm(27, (DEPTH, D_MODEL), 0.02),
        'w_up': nrm(28, (DEPTH, D_MODEL, D_FF), D_MODEL ** -0.5),
        'w_down': nrm(29, (DEPTH, D_FF, D_MODEL), D_FF ** -0.5),
        'norm_final': 1.0 + nrm(30, (D_MODEL,), 0.02),
    }


def reference(x_prompt, x_sample, mem_prompt, cache_conv, cache_fox_k, cache_fox_v, cache_fox_logf,
              state_hgrn, cache_sb_k, cache_sb_v, cache_mem_k, cache_mem_v,
              norm_mix, w_in, b_fox_f, conv_w, conv_b, conv_ln_g, conv_ln_b, hgrn_lb, hgrn_norm,
              w_out, norm_mem, w_mq, w_mk, w_mv, w_mo, norm_ffn, w_up, w_down, norm_final):
    f32 = jnp.float32
    split_idx = np.cumsum(SPLIT_SIZES)[:-1].tolist()
    sm = jax.nn.softmax(hgrn_lb.astype(f32), axis=0)
    lower = jnp.clip(jnp.cumsum(sm, axis=0) - sm[0], 0.0, 1.0 - 1e-6).reshape(DEPTH, N_HEADS, HEAD_DIM)

    def project(x, l):
        p = _rmsnorm(x, norm_mix[l]) @ w_in[l]
        cv_a, cv_g, fq, fk, fv, ff, hq, hf, hi, hgate, sq, sk, sv = jnp.split(p, split_idx, axis=-1)
        f_logf = jax.nn.log_sigmoid((ff + b_fox_f[l]).astype(f32))
        hz = _heads(hf).astype(f32)
        lb = lower[l]
        pos_lb = lb > 0.0
        lb_safe = jnp.where(pos_lb, lb, 1.0)
        base = jnp.log1p(-lb) + jax.nn.log_sigmoid(hz)
        h_logf = jnp.where(pos_lb, jnp.logaddexp(jnp.log(lb_safe), base), base)
        h_k = (1.0 - lb) * jax.nn.sigmoid(-hz)
        return (cv_a, cv_g, _heads(fq), _heads(fk), _heads(fv), f_logf,
                _heads(hq), h_k, _heads(hi), h_logf, hgate, _heads(sq), _heads(sk), _heads(sv))

    def finish(x, l, conv_o, fox_o, hg_o, hgate, sb_o, mk, mv):
        hg = _rmsnorm(hg_o, hgrn_norm[l]) * jax.nn.silu(_heads(hgate).astype(f32))
        mix = jnp.concatenate([conv_o.astype(x.dtype), _flat(fox_o).astype(x.dtype),
                               _flat(hg).astype(x.dtype), _flat(sb_o).astype(x.dtype)], axis=-1)
        x = x + mix @ w_out[l]
        x = x + _mem_attend(_rmsnorm(x, norm_mem[l]), mk, mv, w_mq[l], w_mo[l])
        h = _rmsnorm(x, norm_ffn[l])
        return x + jnp.square(jax.nn.relu(h @ w_up[l])) @ w_down[l]

    bp, sp = x_prompt.shape[0], x_prompt.shape[1]
    pos_p = jnp.arange(sp)
    qpos_blocks = pos_p.reshape(-1, QBLK)
    x = x_prompt
    conv_pl, fk_pl, fv_pl, flf_pl, hg_pl, sk_pl, sv_pl, mk_pl, mv_pl = [], [], [], [], [], [], [], [], []
    for l in range(DEPTH):
        cv_a, cv_g, fq, fk, fv, flf, hq, hk, hv, hlf, hgate, sq, sk, sv = project(x, l)
        conv_o, conv_h = _conv_module(cv_a, cv_g, jnp.zeros((bp, CONV_W - 1, G_WIDTH), cv_a.dtype),
                                      conv_w[l], conv_b[l], conv_ln_g[l], conv_ln_b[l])
        fc = jnp.cumsum(flf, axis=1)
        fox_o = _unblocks(lax.map(lambda a: _fox_attend(a[0], a[1], a[2], fk, fv, fc, pos_p),
                                  (_blocks(fq, QBLK), _blocks(fc, QBLK), qpos_blocks)))
        s_fin, hg_o = lax.scan(lambda st, c: _hgrn_chunk(st, c[0], c[1], c[2], c[3]),
                               jnp.zeros((bp, N_HEADS, HEAD_DIM, HEAD_DIM), f32),
                               (_blocks(hq, CHUNK), _blocks(hk, CHUNK), _blocks(hv, CHUNK), _blocks(hlf, CHUNK)))
        hg_o = _unblocks(hg_o)
        sb_o = _unblocks(lax.map(lambda a: _sb_attend(a[0], a[1], sk, sv, pos_p),
                                 (_blocks(sq, QBLK), qpos_blocks)))
        mk = (mem_prompt @ w_mk[l]).reshape(bp, N_MEM, MEM_HEADS, MEM_HEAD_DIM)
        mv = (mem_prompt @ w_mv[l]).reshape(bp, N_MEM, MEM_HEADS, MEM_HEAD_DIM)
        x = finish(x, l, conv_o, fox_o, hg_o, hgate, sb_o, mk, mv)
        conv_pl.append(conv_h); fk_pl.append(fk); fv_pl.append(fv); flf_pl.append(flf)
        hg_pl.append(s_fin); sk_pl.append(sk); sv_pl.append(sv); mk_pl.append(mk); mv_pl.append(mv)
    y_prompt = _rmsnorm(x, norm_final)

    past = cache_fox_k.shape[2]
    ls = x_sample.shape[1]
    qpos = past + jnp.arange(ls)
    kpos = jnp.arange(past + ls)
    x = x_sample
    conv_sl, fk_sl, fv_sl, flf_sl, hg_sl, sk_sl, sv_sl = [], [], [], [], [], [], []
    for l in range(DEPTH):
        cv_a, cv_g, fq, fk, fv, flf, hq, hk, hv, hlf, hgate, sq, sk, sv = project(x, l)
        conv_o, conv_h = _conv_module(cv_a, cv_g, cache_conv[l], conv_w[l], conv_b[l], conv_ln_g[l], conv_ln_b[l])
        fk_all = jnp.concatenate([cache_fox_k[l].astype(fk.dtype), fk], axis=1)
        fv_all = jnp.concatenate([cache_fox_v[l].astype(fv.dtype), fv], axis=1)
        fc = jnp.cumsum(jnp.concatenate([cache_fox_logf[l].astype(f32), flf], axis=1), axis=1)
        fox_o = _fox_attend(fq, fc[:, past:], qpos, fk_all, fv_all, fc, kpos)
        s_new, hg_o = _hgrn_chunk(state_hgrn[l].astype(f32), hq, hk, hv, hlf)
        sk_all = jnp.concatenate([cache_sb_k[l].astype(sk.dtype), sk], axis=1)
        sv_all = jnp.concatenate([cache_sb_v[l].astype(sv.dtype), sv], axis=1)
        sb_o = _sb_attend(sq, qpos, sk_all, sv_all, kpos)
        x = finish(x, l, conv_o, fox_o, hg_o, hgate, sb_o, cache_mem_k[l], cache_mem_v[l])
        conv_sl.append(conv_h); fk_sl.append(fk); fv_sl.append(fv); flf_sl.append(flf)
        hg_sl.append(s_new); sk_sl.append(sk); sv_sl.append(sv)
    y_sample = _rmsnorm(x, norm_final)

    return (y_prompt, y_sample,
            jnp.stack(conv_pl), jnp.stack(fk_pl), jnp.stack(fv_pl), jnp.stack(flf_pl), jnp.stack(hg_pl),
            jnp.stack(sk_pl), jnp.stack(sv_pl), jnp.stack(mk_pl), jnp.stack(mv_pl),
            jnp.stack(conv_sl), jnp.stack(fk_sl), jnp.stack(fv_sl), jnp.stack(flf_sl), jnp.stack(hg_sl),
            jnp.stack(sk_sl), jnp.stack(sv_sl))
```

```python
import numpy as np
import concourse.bass as bass
import concourse.mybir as mybir
from concourse.bass_utils import run_bass_kernel_spmd
from contextlib import ExitStack

F32 = mybir.dt.float32
BF16 = mybir.dt.bfloat16
ALU = mybir.AluOpType
AF = mybir.ActivationFunctionType
AX = mybir.AxisListType


class T:
    __slots__ = ("ap", "key")

    def __init__(self, ap, key):
        self.ap = ap
        self.key = key

    def __getitem__(self, idx):
        return self.ap[idx]


class DSem:
    __slots__ = ("sem", "val")

    def __init__(self, sem):
        self.sem = sem
        self.val = 0


class Sched:
    ENGS = ("pe", "act", "dve", "pool", "sp")
    SEM_WRAP = 30000

    def __init__(self, nc, stack):
        self.nc = nc
        self.stack = stack
        self.ops = {e: [] for e in self.ENGS}
        self.last_w = {}
        self.readers = {}
        self.nk = 0

    def sem(self, name):
        return self.stack.enter_context(self.nc.semaphore(name))

    def dsem(self, name):
        return DSem(self.sem(name))

    def sb(self, name, shape, dtype):
        h = self.stack.enter_context(self.nc.sbuf_tensor("s_" + name, list(shape), dtype))
        self.nk += 1
        return T(h[:], ("sb", name, self.nk))

    def ps(self, name, shape, dtype):
        h = self.stack.enter_context(self.nc.psum_tensor("q_" + name, list(shape), dtype))
        self.nk += 1
        return T(h[:], ("ps", name, self.nk))

    def _deps(self, reads, writes, ev):
        deps = []
        def expand(lst):
            out = []
            for k in lst:
                k = k.key if isinstance(k, T) else k
                if isinstance(k, list):
                    out.extend(k)
                else:
                    out.append(k)
            return out
        reads = expand(reads)
        writes = expand(writes)
        for k in reads:
            w = self.last_w.get(k)
            if w is not None:
                deps.append(w)
            rl_ = self.readers.setdefault(k, [])
            if isinstance(k, tuple) and k and k[0] == "ps":
                for r in rl_:
                    if r[0] == "e" and r[1] != ev[1]:
                        deps.append(r)
            rl_.append(ev)
        for k in writes:
            w = self.last_w.get(k)
            if w is not None:
                deps.append(w)
            deps.extend(self.readers.get(k, ()))
            self.last_w[k] = ev
            self.readers[k] = []
        return [d for d in deps if d != ev]

    stopped = False

    countdown = None

    def mark(self, name):
        import os
        if os.environ.get("STOP_AT") == name:
            n = int(os.environ.get("STOP_N", "0"))
            if n == 0:
                self.stopped = True
            else:
                self.countdown = n

    def _tick(self):
        if self.countdown is not None:
            self.countdown -= 1
            if self.countdown < 0:
                self.stopped = True

    def op(self, eng, fn, reads=(), writes=()):
        self._tick()
        if self.stopped:
            return None
        lst = self.ops[eng]
        ev = ("e", eng, len(lst))
        deps = self._deps(reads, writes, ev)
        lst.append({"fn": fn, "deps": deps, "sig": False, "dma": None})
        return ev

    def dma(self, q, out_ap, in_ap, dsem, reads=(), writes=(), **kw):
        self._tick()
        if self.stopped:
            return None
        lst = self.ops[q]
        dsem.val += 16
        ev = ("d", dsem, dsem.val)
        deps = self._deps(reads, writes, ev)

        def fn(e, out_ap=out_ap, in_ap=in_ap, kw=kw):
            return e.dma_start(out=out_ap, in_=in_ap, **kw)

        lst.append({"fn": fn, "deps": deps, "sig": False, "dma": dsem})
        return ev

    def finalize(self, final_waits=()):
        nc = self.nc
        for e in self.ENGS:
            for o in self.ops[e]:
                for d in o["deps"]:
                    if d[0] == "e":
                        if d[1] == "pe" and e == "pe":
                            continue
                        self.ops[d[1]][d[2]]["sig"] = True
        val = {}
        for e in self.ENGS:
            c = 0
            gen = 0
            cur = None
            for i, o in enumerate(self.ops[e]):
                if o["sig"] and o["dma"] is None:
                    if cur is None or c >= self.SEM_WRAP:
                        cur = self.sem(f"es_{e}_{gen}")
                        gen += 1
                        c = 0
                    c += 1
                    val[(e, i)] = (cur, c)

        def run(e, eng):
            clock = {}
            for i, o in enumerate(self.ops[e]):
                need = {}
                for d in o["deps"]:
                    if d[0] == "e":
                        if d[1] == "pe" and e == "pe":
                            continue
                        s, v = val[(d[1], d[2])]
                    else:
                        s, v = d[1].sem, d[2]
                    sid = id(s)
                    if clock.get(sid, 0) >= v:
                        continue
                    if sid not in need or need[sid][1] < v:
                        need[sid] = (s, v)
                for sid, (s, v) in need.items():
                    eng.wait_ge(s, v)
                    clock[sid] = v
                ins = o["fn"](eng)
                if o["dma"] is not None:
                    ins.then_inc(o["dma"].sem, 16)
                elif o["sig"]:
                    s, v = val[(e, i)]
                    ins.then_inc(s, 1)
            if e == "sp":
                for ds in final_waits:
                    if ds.val > 0:
                        eng.wait_ge(ds.sem, ds.val)

        with nc.Block() as block:
            @block.tensor
            def _(eng):
                run("pe", eng)

            @block.scalar
            def _(eng):
                run("act", eng)

            @block.vector
            def _(eng):
                run("dve", eng)

            @block.gpsimd
            def _(eng):
                run("pool", eng)

            @block.sync
            def _(eng):
                run("sp", eng)


class Rot:
    def __init__(self, tiles):
        self.tiles = tiles
        self.i = 0

    def next(self):
        t = self.tiles[self.i % len(self.tiles)]
        self.i += 1
        return t


D = 1024
KC = 8
G = 256
H = 4
HD = 64
CW = 31
NMEM = 256
DFF = 4096
PIN = 3076
EPS = 1e-6
O_NMIX, O_NMEM, O_NFFN, O_CW, O_CB, O_LG, O_LB, O_BF, O_HN = 0, 8, 16, 24, 86, 88, 90, 92, 96
NPL = 96 + 256
C_ID, C_TRI, C_ONE, C_HM, C_SCM = 0, 128, 256, 384, 640
NCF = 896
C_NTF, C_NTS, C_M01, C_NTI, C_NEG1 = 896, 1152, 1408, 1664, 1792
NCONST = 1920


def make_consts(TMAX):
    c = np.zeros((128, NCONST), np.float32)
    s = np.arange(128)[:, None]
    t = np.arange(128)[None, :]
    c[:, C_ID:C_ID + 128] = (s == t)
    c[:, C_TRI:C_TRI + 128] = (s <= t)
    c[:, C_ONE:C_ONE + 128] = 1.0
    c[:, C_NTF:C_NTF + 128] = np.where(s <= t, 0.0, -30000.0)
    c[:, C_NTS:C_NTS + 128] = np.where(s < t, 0.0, -30000.0)
    c[:, C_M01:C_M01 + 128] = (s < t)
    c[:, C_M01 + 128:C_M01 + 256] = 1.0
    c[:, C_NTI:C_NTI + 128] = np.where(s >= t, -1.0, 0.0)
    c[:, C_NEG1:C_NEG1 + 128] = -1.0
    hm = (np.arange(64)[:, None] <= np.arange(64)[None, :]).astype(np.float32)
    c[:64, C_HM:C_HM + 256] = np.tile(hm, (1, 4))
    sm = np.ones((256,), np.float32)
    sm[::64] = 0.0
    c[:, C_SCM:C_SCM + 256] = sm[None, :]
    return c


FFN_WIDE = True


def build(cfg):
    SEQ, TP, DEPTH, PAST, DS, NS = cfg["SEQ"], cfg["T"], cfg["DEPTH"], cfg["PAST"], cfg["DS"], cfg["NS"]
    NKB = max(SEQ // 128, PAST // 128 + 1)
    nc = bass.Bass("TRN2", target_bir_lowering=False)

    def din(name, shape, dt=F32):
        return nc.dram_tensor(name, list(shape), dt, kind="ExternalInput").ap()

    def dout(name, shape):
        return nc.dram_tensor(name, list(shape), F32, kind="ExternalOutput").ap()

    def dint(name, shape, dt):
        return nc.dram_tensor(name, list(shape), dt, kind="Internal").ap()

    I = {}
    I["x_prompt"] = din("x_prompt", [SEQ, D])
    I["x_sample"] = din("x_sample", [NS, DS, D])
    I["mem_prompt"] = din("mem_prompt", [NMEM, D])
    I["cache_conv"] = din("cache_conv", [DEPTH, NS, CW - 1, G])
    for n in ("cache_fox_k", "cache_fox_v", "cache_sb_k", "cache_sb_v"):
        I[n] = din(n, [DEPTH, NS, PAST, G])
    I["cache_fox_logf"] = din("cache_fox_logf", [DEPTH, NS, PAST, H])
    I["state_hgrn"] = din("state_hgrn", [DEPTH, NS, H, HD, HD])
    I["cache_mem_k"] = din("cache_mem_k", [DEPTH, NS, NMEM, D])
    I["cache_mem_v"] = din("cache_mem_v", [DEPTH, NS, NMEM, D])
    WSH = {"w_in": (D, PIN), "w_out": (D, D), "w_mq": (D, D), "w_mk": (D, D), "w_mv": (D, D), "w_mo": (D, D),
           "w_up": (D, DFF), "w_down": (DFF, D)}
    Wf = {n: din(n, [DEPTH, s[0], s[1]]) for n, s in WSH.items()}
    SLABS = {"w_in": [(0, 512), (512, 512), (768, 516), (1284, 512), (1796, 512), (2308, 512), (2564, 512)],
             "w_up": [(i * 512, 512) for i in range(8)], "w_down": [(i * 128, 128) for i in range(8)]}
    for n_ in ("w_out", "w_mq", "w_mk", "w_mv", "w_mo"):
        SLABS[n_] = [(0, 512), (512, 512)]
    Wb = {n: dint(n + "_bf", [DEPTH, len(SLABS[n]), 128, 4128], BF16) for n in WSH}
    pl_d = din("pl", [DEPTH, 128, NPL])
    gl_d = din("gl", [128, KC + DEPTH * H])
    const_d = din("consts", [128, NCONST])
    O = {}
    O["y_prompt"] = dout("y_prompt", [SEQ, D])
    O["y_sample"] = dout("y_sample", [NS, DS, D])
    O["conv_p"] = dout("conv_p", [DEPTH, CW - 1, G])
    O["fox_k_p"] = dout("fox_k_p", [DEPTH, SEQ, G])
    O["fox_v_p"] = dout("fox_v_p", [DEPTH, SEQ, G])
    O["fox_logf_p"] = dout("fox_logf_p", [DEPTH, SEQ, H])
    O["hgrn_p"] = dout("hgrn_p", [DEPTH, H, HD, HD])
    O["sb_k_p"] = dout("sb_k_p", [DEPTH, SEQ, G])
    O["sb_v_p"] = dout("sb_v_p", [DEPTH, SEQ, G])
    O["mem_k_p"] = dout("mem_k_p", [DEPTH, NMEM, D])
    O["mem_v_p"] = dout("mem_v_p", [DEPTH, NMEM, D])
    O["conv_s"] = dout("conv_s", [DEPTH, NS, CW - 1, G])
    O["fox_k_s"] = dout("fox_k_s", [DEPTH, NS, DS, G])
    O["fox_v_s"] = dout("fox_v_s", [DEPTH, NS, DS, G])
    O["fox_logf_s"] = dout("fox_logf_s", [DEPTH, NS, DS, H])
    O["hgrn_s"] = dout("hgrn_s", [DEPTH, NS, H, HD, HD])
    O["sb_k_s"] = dout("sb_k_s", [DEPTH, NS, DS, G])
    O["sb_v_s"] = dout("sb_v_s", [DEPTH, NS, DS, G])
    LTOT = SEQ + NS * DS
    xs_d = dint("xs", [D, LTOT], F32)

    st = ExitStack()
    with st:
        S = Sched(nc, st)
        TM = max(TP, DS)
        cf = S.sb("cf", [128, NCF], F32)
        cb = S.sb("cb", [128, NCONST], BF16)
        d_c = S.dsem("d_cf")
        d_cb = S.dsem("d_cb")
        d_gl = S.dsem("d_gl")
        d_pls = S.dsem("d_pls")
        S.dma("sp", cf[:], const_d[:, 0:NCF], d_c, writes=[cf])
        S.dma("pool", cb[:], const_d, d_cb, writes=[cb])
        gl = S.sb("gl", [128, KC + DEPTH * H], F32)
        S.dma("sp", gl[:], gl_d, d_gl, writes=[gl])
        pls = S.sb("pls", [128, DEPTH, NPL], F32)
        for l in range(DEPTH):
            S.dma("sp", pls[:, l, :], pl_d[l], d_pls, writes=[pls])
        zer = S.sb("zer", [128, 512], BF16)
        S.op("dve", lambda e: e.memset(zer[:], 0.0), writes=[zer])
        eps_t = S.sb("eps_t", [128, 1], F32)
        S.op("dve", lambda e: e.memset(eps_t[:], EPS), writes=[eps_t])
        one_t = S.sb("one_t", [128, 1], F32)
        S.op("dve", lambda e: e.memset(one_t[:], 1.0), writes=[one_t])

        S.mark("cast")
        d_w = {}
        for l in range(DEPTH):
            for n, (r, ccols) in WSH.items():
                d_w[(l, n)] = S.dsem(f"d_w{l}_{n}")
                kcn_ = r // 128
                for idx, (c0, ncl) in enumerate(SLABS[n]):
                    S.dma("pool", Wb[n][l, idx, :, 0:kcn_ * ncl].rearrange("p (k n) -> p k n", k=kcn_),
                          Wf[n][l, :, c0:c0 + ncl].rearrange("(k p) n -> p k n", p=128), d_w[(l, n)], writes=[("wb", l, n)])

        S.mark("lb")
        lbr = gl[0:64, KC:KC + DEPTH * H]
        hl_e = S.sb("hl_e", [64, DEPTH * H], F32)
        hl_m = S.sb("hl_m", [64, H], F32)
        hl_s = S.sb("hl_s", [64, H], F32)
        lbt = S.sb("lbt", [64, DEPTH * H], F32)
        omlt = S.sb("omlt", [64, DEPTH * H], F32)
        nomlt = S.sb("nomlt", [64, DEPTH * H], F32)
        hl_all = [gl, hl_e, hl_m, hl_s, lbt, omlt, nomlt]

        def sl(l):
            return slice(l * H, (l + 1) * H)
        S.op("dve", lambda e: e.tensor_copy(out=hl_m[:], in_=gl[0:64, KC:KC + H]), reads=hl_all, writes=hl_all)
        for l in range(1, DEPTH):
            S.op("dve", lambda e, l=l: e.tensor_tensor(out=hl_m[:], in0=hl_m[:], in1=gl[0:64, KC + l * H:KC + (l + 1) * H], op=ALU.max), reads=hl_all, writes=hl_all)
        for l in range(DEPTH):
            S.op("dve", lambda e, l=l: e.tensor_tensor(out=hl_e[:, sl(l)], in0=gl[0:64, KC + l * H:KC + (l + 1) * H], in1=hl_m[:], op=ALU.subtract), reads=hl_all, writes=hl_all)
        S.op("act", lambda e: e.activation(out=hl_e[:], in_=hl_e[:], func=AF.Exp), reads=hl_all, writes=hl_all)
        S.op("dve", lambda e: e.tensor_copy(out=hl_s[:], in_=hl_e[:, sl(0)]), reads=hl_all, writes=hl_all)
        for l in range(1, DEPTH):
            S.op("dve", lambda e, l=l: e.tensor_tensor(out=hl_s[:], in0=hl_s[:], in1=hl_e[:, sl(l)], op=ALU.add), reads=hl_all, writes=hl_all)
        S.op("dve", lambda e: e.reciprocal(out=hl_s[:], in_=hl_s[:]), reads=hl_all, writes=hl_all)
        for l in range(DEPTH):
            S.op("dve", lambda e, l=l: e.tensor_tensor(out=hl_e[:, sl(l)], in0=hl_e[:, sl(l)], in1=hl_s[:], op=ALU.mult), reads=hl_all, writes=hl_all)
        S.op("dve", lambda e: e.memset(lbt[:, sl(0)], 0.0), reads=hl_all, writes=hl_all)
        for l in range(1, DEPTH):
            S.op("dve", lambda e, l=l: e.tensor_tensor(out=lbt[:, sl(l)], in0=lbt[:, sl(l - 1)], in1=hl_e[:, sl(l)], op=ALU.add), reads=hl_all, writes=hl_all)
        S.op("dve", lambda e: e.tensor_scalar(out=lbt[:], in0=lbt[:], scalar1=0.0, scalar2=1.0 - 1e-6, op0=ALU.max, op1=ALU.min), reads=hl_all, writes=hl_all)
        S.op("dve", lambda e: e.tensor_scalar(out=omlt[:], in0=lbt[:], scalar1=-1.0, scalar2=1.0, op0=ALU.mult, op1=ALU.add), reads=hl_all, writes=hl_all)
        S.op("dve", lambda e: e.tensor_scalar(out=nomlt[:], in0=omlt[:], scalar1=-1.0, scalar2=None, op0=ALU.mult), reads=hl_all, writes=hl_all)

        nK = max(2 * NKB * 128, 8192)
        nFV = max(NKB * H * 65, 8192)
        nSV = max(NKB * G, 8192)
        arena = S.sb("arena", [128, 2 * nK + nFV + nSV], BF16)
        fKT = T(arena.ap[:, 0:2 * NKB * 128].rearrange("p (c n) -> p c n", c=2), ("sb", "fKT"))
        sKT = T(arena.ap[:, nK:nK + 2 * NKB * 128].rearrange("p (c n) -> p c n", c=2), ("sb", "sKT"))
        fV = T(arena.ap[:, 2 * nK:2 * nK + NKB * H * 65].rearrange("p (n h d) -> p n h d", n=NKB, h=H), ("sb", "fV"))
        sV = T(arena.ap[:, 2 * nK + nFV:2 * nK + nFV + NKB * G].rearrange("p (n g) -> p n g", n=NKB), ("sb", "sV"))
        negC = S.sb("negC", [128, NKB, H], F32)
        Cbase = S.sb("Cbase", [128, H], F32)
        cext = S.sb("cext", [128, 2, 30 + TM], BF16)
        convD = S.sb("convD", [128, 62, 128], BF16)
        hS = S.sb("hS", [64, H, HD], F32)
        hSb = S.sb("hSb", [64, H, HD], BF16)
        mKT = S.sb("mKT", [128, 8, NMEM], BF16)
        mV = S.sb("mV", [128, 2, D], BF16)
        xT = S.sb("xT", [128, KC, TM], F32)
        xn = S.sb("xn", [128, KC, TM], BF16)
        mixT = S.sb("mixT", [128, KC, TM], BF16)
        mqT = mixT
        moT = xn
        assert TM == 256 and PAST // 128 <= 16
        hT = S.sb("hT", [128, DFF // 128, TM], BF16)
        hTflat = hT.ap.rearrange("p a t -> p (a t)")
        ktmp = T(hTflat[:, 0:4096].rearrange("p (n g) -> p n g", g=G), hT.key)
        mtmp = T(hTflat[:, 4096:6144].rearrange("p (n g) -> p n g", g=D), hT.key)
        memT = T(hTflat[:, 4096:6144].rearrange("p (k m) -> p k m", m=NMEM), hT.key)
        xtok = T(hTflat[:, 6144:8192].bitcast(F32), hT.key)
        sq = T(hTflat[:, 0:4096].bitcast(F32).rearrange("p (k t) -> p k t", k=KC), hT.key)
        rstd = S.sb("rstd", [128, TM], F32)
        fQT = S.sb("fQT", [128, 2, TM], BF16)
        sQT = S.sb("sQT", [128, 2, TM], BF16)
        sgT = S.sb("sgT", [128, 2, TM], F32)
        uF = S.sb("uF", [128, 2, 30], F32)
        cY = S.sb("cY", [128, 2, TM], F32)
        cY2 = sgT
        cmean = S.sb("cmean", [128, TM], F32)
        cvar = S.sb("cvar", [128, TM], F32)
        flf = S.sb("flf", [128, H], F32)
        flfp = S.sb("flfp", [128, max(PAST // 128, 1), H], F32)
        Cq = S.sb("Cq", [128, H], F32)
        cs_b = S.sb("cs_b", [128, H], BF16)
        cs_f = S.sb("cs_f", [128, H], F32)
        cs_r = S.sb("cs_r", [128, H], F32)
        Cparts = S.sb("Cparts", [128, H, 3], BF16)
        CTs = S.sb("CTs", [3, H, TM], BF16)
        Pt = Rot([S.sb(f"Pt{i}", [128, TM], BF16) for i in range(4)])
        SPb = Rot([S.sb(f"SPb{i}", [128, TM], BF16) for i in range(4)])
        SPaf = [S.sb(f"SPaf{i}", [128, TM], F32) for i in range(H)]
        SPab = [S.sb(f"SPab{i}", [128, TM], BF16) for i in range(H)]
        rec = S.sb("rec", [128, 4], F32)
        mixtok = S.sb("mixtok", [128, 4, G], BF16)
        rden = cmean
        Pm = S.sb("Pm", [128, 2, TM], BF16)
        rl = Rot([S.sb(f"rl{i}", [128, TM], F32) for i in range(1)])
        Et = rl
        stg = Rot([S.sb(f"stg{i}", [128, 512], F32) for i in range(2)])
        d_stg = [S.dsem(f"d_stg{i}") for i in range(2)]
        stg_i = [0]
        d_xtok = S.dsem("d_xtok")
        d_x = S.dsem("d_x")
        d_xs = S.dsem("d_xs")
        d_misc = S.dsem("d_misc")
        d_oflf = S.dsem("d_oflf")
        d_ouF = S.dsem("d_ouF")
        d_ohS = S.dsem("d_ohS")
        d_cext = S.dsem("d_cext"); d_hS = S.dsem("d_hS"); d_fV = S.dsem("d_fV"); d_sV = S.dsem("d_sV")
        d_ktmp = S.dsem("d_ktmp"); d_flfp = S.dsem("d_flfp"); d_mV = S.dsem("d_mV"); d_mtmp = S.dsem("d_mtmp")
        slab = [S.sb(f"slab{i}", [128, 4128], BF16) for i in range(3)]
        d_slab = [S.dsem(f"d_slab{i}") for i in range(3)]
        slab_i = [0]
        hTf32 = hTflat.bitcast(F32)
        hA = T(hTf32[0:64, 0:1024].rearrange("p (h t) -> p h t", h=H), hT.key)
        hB = T(hTf32[0:64, 1024:2048].rearrange("p (h t) -> p h t", h=H), hT.key)
        hBc = T(hTf32[0:64, 2048:3072].rearrange("p (h t) -> p h t", h=H), hT.key)
        hE = T(hTf32[0:64, 3072:4096].rearrange("p (h t) -> p h t", h=H), hT.key)
        hqt = S.sb("hqt", [64, H, TM], BF16)
        hkt = S.sb("hkt", [64, H, TM], BF16)
        hql = S.sb("hql", [64, H, TM], BF16)
        hkl = S.sb("hkl", [64, H, TM], BF16)
        NCH = max(TM // 64, 1)
        hV = S.sb("hV", [64, NCH, G], BF16)
        hG = S.sb("hG", [64, NCH, G], BF16)
        khT = S.sb("khT", [64, H, 64], BF16)
        kh = S.sb("kh", [64, H, 64], BF16)
        scT = S.sb("scT", [64, H, 64], BF16)
        ofp = S.sb("ofp", [64, H, 64], F32)
        osq = S.sb("osq", [64, H, 64], F32)
        ssq = S.sb("ssq", [64, H], F32)
        hgts = [S.sb(f"hgt{i}", [64, G], BF16) for i in range(2)]
        pA = Rot([S.ps(f"pA{i}", [128, 512], F32) for i in range(2)])
        pS = Rot([S.ps(f"pS{i}", [128, 512], F32) for i in range(2)])
        pO = Rot([S.ps(f"pO{i}", [128, 512], F32) for i in range(2)])
        pM = Rot([S.ps(f"pM{i}", [128, 512], F32) for i in range(2)])
        bank6 = Rot(pS.tiles + pA.tiles + pM.tiles)

        ident_f = cf[:, C_ID:C_ID + 128]
        ident_b = cb[:, C_ID:C_ID + 128]
        ones_f = cf[:, C_ONE:C_ONE + 128]
        ones_b = cb[:, C_ONE:C_ONE + 128]

        def mm(out, lhsT, rhs, start, stop, reads, writes):
            S.op("pe", lambda e: e.matmul(out, lhsT=lhsT, rhs=rhs, start=start, stop=stop), reads=reads, writes=writes)

        def tr(out, in_, ident, reads, writes):
            S.op("pe", lambda e: e.transpose(out, in_, ident), reads=reads, writes=writes)

        def act(out, in_, func, reads, writes, **kw):
            S.op("act", lambda e: e.activation(out=out, in_=in_, func=func, **kw), reads=reads, writes=writes)

        def vcopy(out, in_, reads, writes, eng="dve"):
            S.op(eng, lambda e: e.tensor_copy(out=out, in_=in_), reads=reads, writes=writes)

        def vtt(out, in0, in1, op, reads, writes, eng="dve"):
            S.op(eng, lambda e: e.tensor_tensor(out=out, in0=in0, in1=in1, op=op), reads=reads, writes=writes)

        def vts(out, in0, s1, s2, op0, op1, reads, writes, eng="dve"):
            if op1 is None:
                S.op(eng, lambda e: e.tensor_scalar(out=out, in0=in0, scalar1=s1, scalar2=None, op0=op0), reads=reads, writes=writes)
            else:
                S.op(eng, lambda e: e.tensor_scalar(out=out, in0=in0, scalar1=s1, scalar2=s2, op0=op0, op1=op1), reads=reads, writes=writes)

        def vstt(out, in0, scalar, in1, op0, op1, reads, writes, eng="dve"):
            S.op(eng, lambda e: e.scalar_tensor_tensor(out=out, in0=in0, scalar=scalar, in1=in1, op0=op0, op1=op1), reads=reads, writes=writes)

        sq_default = sq

        def load_slab(wname, l, c0, ncols, kcn=KC):
            i = slab_i[0] % 3
            slab_i[0] += 1
            t = slab[i]
            v = t[:, 0:kcn * ncols].rearrange("p (k n) -> p k n", k=kcn)
            idx = SLABS[wname].index((c0, ncols))
            src = Wb[wname][l, idx, :, 0:kcn * ncols]
            S.dma("sp", t[:, 0:kcn * ncols], src, d_slab[i], reads=[("wb", l, wname)], writes=[t])
            return t, v

        def stage_out(n_rows, width, fill, dsts):
            i = stg_i[0] % 2
            stg_i[0] += 1
            t = stg.tiles[i]
            fill(t)
            for (dap, c0, ncl) in dsts:
                S.dma("sp", dap, t[0:n_rows, c0:c0 + ncl], d_stg[i], reads=[t], writes=[])
            return t

        def rmsnorm_fm(Tn, gcol, out_t, out_dt_is_f32=False, xT=xT, sq=None, rstd=rstd):
            sq = sq_default if sq is None else sq
            S.op("dve", lambda e: e.tensor_tensor(out=sq[:, :, 0:Tn], in0=xT[:, :, 0:Tn], in1=xT[:, :, 0:Tn], op=ALU.mult), reads=[xT], writes=[sq])
            p = pM.next()
            for kc in range(KC):
                mm(p[:, 0:Tn], ones_f, sq[:, kc, 0:Tn], kc == 0, kc == KC - 1, [sq, cf], [p])
            act(rstd[:, 0:Tn], p[:, 0:Tn], AF.Ln, [p, eps_t], [rstd], bias=eps_t[:, 0:1], scale=1.0 / D)
            act(rstd[:, 0:Tn], rstd[:, 0:Tn], AF.Exp, [rstd], [rstd], scale=-0.5)
            for kc in range(KC):
                vstt(out_t[:, kc, 0:Tn], xT[:, kc, 0:Tn], gcol[:, kc:kc + 1], rstd[:, 0:Tn], ALU.mult, ALU.mult, [xT, rstd, pls, gl], [out_t])

        def linear_fm(wname, l, src_t, Tn, nout, evac, kcn=KC, c_base=0):
            if kcn == KC:
                for g0 in range(0, nout, 4):
                    ng = min(4, nout - g0)
                    t, v = load_slab(wname, l, c_base + g0 * 128, ng * 128)
                    for j in range(ng):
                        p = pA.next()
                        for kc in range(KC):
                            mm(p[:, 0:Tn], v[:, kc, j * 128:(j + 1) * 128], src_t[:, kc, 0:Tn], kc == 0, kc == KC - 1, [t, src_t], [p])
                        evac(g0 + j, p)
            else:
                for oc in range(nout):
                    t, v = load_slab(wname, l, c_base + oc * 128, 128, kcn=kcn)
                    p = pA.next()
                    for kc in range(kcn):
                        mm(p[:, 0:Tn], v[:, kc, :], src_t[:, kc, 0:Tn], kc == 0, kc == kcn - 1, [t, src_t], [p])
                    evac(oc, p)

        def add_resid(Tn):
            def ev(oc, p):
                vtt(xT[:, oc, 0:Tn], xT[:, oc, 0:Tn], p[:, 0:Tn], ALU.add, [p, xT], [xT])
            return ev

        def fox_c_block(flf_ap, flf_key, n, blk, want_q, qcol0):
            p = pM.next()
            mm(p[0:n, 0:H], cf[0:n, C_TRI:C_TRI + n], flf_ap, True, True, [cf, flf_key], [p])
            vtt(Cq[0:n, :], p[0:n, 0:H], Cbase[0:n, :], ALU.add, [p, Cbase], [Cq])
            vts(negC[0:n, blk, :], Cq[0:n, :], -1.0, None, ALU.mult, None, [Cq], [negC])
            p2 = pM.next()
            mm(p2[:, 0:H], cf[0:n, C_ONE:C_ONE + 128], flf_ap, True, True, [cf, flf_key], [p2])
            vtt(Cbase[:], Cbase[:], p2[:, 0:H], ALU.add, [p2, Cq], [Cbase])
            if want_q:
                cst = [Cq, cs_b, cs_f, cs_r, Cparts]
                vcopy(cs_b[0:n, :], Cq[0:n, :], cst, cst)
                vcopy(Cparts[0:n, :, 0], cs_b[0:n, :], cst, cst)
                vcopy(cs_f[0:n, :], cs_b[0:n, :], cst, cst)
                vtt(cs_r[0:n, :], Cq[0:n, :], cs_f[0:n, :], ALU.subtract, cst, cst)
                vcopy(cs_b[0:n, :], cs_r[0:n, :], cst, cst)
                vcopy(Cparts[0:n, :, 1], cs_b[0:n, :], cst, cst)
                vcopy(cs_f[0:n, :], cs_b[0:n, :], cst, cst)
                vtt(cs_r[0:n, :], cs_r[0:n, :], cs_f[0:n, :], ALU.subtract, cst, cst)
                vcopy(Cparts[0:n, :, 2], cs_r[0:n, :], cst, cst)
                p3 = pM.next()
                for h in range(H):
                    mm(p3[0:3, h * 128:h * 128 + n], Cparts[0:n, h, :], ident_b[0:n, 0:n], True, True, [Cparts, cb], [p3])
                for h in range(H):
                    vcopy(CTs[0:3, h, qcol0:qcol0 + n], p3[0:3, h * 128:h * 128 + n], [p3], [CTs])

        def layer_tile(sq_, l, ti):
            Tn = sq_["T"]
            SBk = min(128, Tn)
            nsb = Tn // SBk
            nch = Tn // 64
            t0 = ti * Tn
            past = sq_["past"]
            kb0 = (past + t0) // 128
            L = sq_["L"]
            ntiles = L // Tn
            last_tile = ti == ntiles - 1
            xoff = sq_["xoff"] + t0
            PL = pls[:, l, :]
            b = sq_["b"]

            S.mark(f"A{l}_{sq_['name']}_{ti}")
            if l == 0:
                for sbi in range(nsb):
                    S.dma("sp", xtok[0:SBk, :], sq_["x_src"][t0 + sbi * SBk:t0 + (sbi + 1) * SBk, :], d_xtok, writes=[xtok])
                    for g0 in range(0, KC, 4):
                        p = pM.next()
                        for j in range(4):
                            tr(p[:, j * 128:j * 128 + SBk], xtok[0:SBk, (g0 + j) * 128:(g0 + j + 1) * 128], ident_f[0:SBk, 0:SBk], [xtok, cf], [p])
                        for j in range(4):
                            vcopy(xT[:, g0 + j, sbi * SBk:(sbi + 1) * SBk], p[:, j * 128:j * 128 + SBk], [p], [xT])
            else:
                S.dma("sp", xT[:, :, 0:Tn], xs_d[:, xoff:xoff + Tn].rearrange("(k p) t -> p k t", p=128), d_x,
                      reads=[("xs", sq_["name"], ti)], writes=[xT])

            S.mark(f"B{l}_{sq_['name']}_{ti}")
            rmsnorm_fm(Tn, PL[:, O_NMIX:O_NMIX + KC], xn)

            S.mark(f"C{l}_{sq_['name']}_{ti}")
            t, v = load_slab("w_in", l, 0, 512)
            for oc in (2, 3, 0, 1):
                p = pA.next()
                for kc in range(KC):
                    mm(p[:, 0:Tn], v[:, kc, oc * 128:(oc + 1) * 128], xn[:, kc, 0:Tn], kc == 0, kc == KC - 1, [t, xn], [p])
                if oc >= 2:
                    act(sgT[:, oc - 2, 0:Tn], p[:, 0:Tn], AF.Sigmoid, [p], [sgT])
                else:
                    vtt(cext[:, oc, 30:30 + Tn], p[:, 0:Tn], sgT[:, oc, 0:Tn], ALU.mult, [p, sgT], [cext])
                    if last_tile:
                        vtt(uF[:, oc, :], p[:, Tn - 30:Tn], sgT[:, oc, Tn - 30:Tn], ALU.mult, [p, sgT], [uF])
            S.mark(f"C2_{l}_{sq_['name']}_{ti}")
            t, v = load_slab("w_in", l, 512, 512)
            for oc in range(4):
                p = pA.next()
                for kc in range(KC):
                    mm(p[:, 0:Tn], v[:, kc, oc * 128:(oc + 1) * 128], xn[:, kc, 0:Tn], kc == 0, kc == KC - 1, [t, xn], [p])
                if oc < 2:
                    act(fQT[:, oc, 0:Tn], p[:, 0:Tn], AF.Copy, [p], [fQT], scale=0.125)
                else:
                    vcopy(fKT[:, oc - 2, (past + t0):(past + t0) + Tn], p[:, 0:Tn], [p], [fKT])
            S.mark(f"C3_{l}_{sq_['name']}_{ti}")
            t, v = load_slab("w_in", l, 768, 516)
            for sbi in range(nsb):
                blk = kb0 + sbi
                r0 = t0 + sbi * SBk
                p = pA.next()
                p2 = pM.next()
                for kc in range(KC):
                    mm(p[0:SBk, 0:512], xn[:, kc, sbi * SBk:(sbi + 1) * SBk], v[:, kc, 0:512], kc == 0, kc == KC - 1, [t, xn], [p])
                for kc in range(KC):
                    mm(p2[0:SBk, 0:4], xn[:, kc, sbi * SBk:(sbi + 1) * SBk], v[:, kc, 512:516], kc == 0, kc == KC - 1, [t, xn], [p2])

                def fill(tl, p=p):
                    act(tl[0:SBk, 0:512], p[0:SBk, 0:512], AF.Copy, [p], [tl])
                tl = stage_out(SBk, 512, fill, [(sq_["fox_k"][l][r0:r0 + SBk, :], 0, 256), (sq_["fox_v"][l][r0:r0 + SBk, :], 256, 256)])
                vcopy(fV[0:SBk, blk, :, 0:64], tl[0:SBk, 256:512].rearrange("p (h d) -> p h d", h=H), [tl], [fV], eng="pool")
                vtt(flf[0:SBk, :], p2[0:SBk, 0:4], PL[0:SBk, O_BF:O_BF + 4], ALU.add, [p2, pls], [flf])
                act(flf[0:SBk, :], flf[0:SBk, :], AF.Sigmoid, [flf], [flf])
                act(flf[0:SBk, :], flf[0:SBk, :], AF.Ln, [flf], [flf])
                S.dma("sp", sq_["fox_logf"][l][r0:r0 + SBk, :], flf[0:SBk, :], d_oflf, reads=[flf], writes=[])
                fox_c_block(flf[0:SBk, :], flf, SBk, blk, True, sbi * SBk)
            S.mark(f"C4_{l}_{sq_['name']}_{ti}")
            t, v = load_slab("w_in", l, 1284, 512)
            for which in (1,):
                for h in range(H):
                    p = pA.next()
                    for kc in range(KC):
                        mm(p[0:64, 0:Tn], v[:, kc, which * 256 + h * 64:which * 256 + (h + 1) * 64], xn[:, kc, 0:Tn], kc == 0, kc == KC - 1, [t, xn], [p])
                    if which == 0:
                        vtt(hB[:, h, 0:Tn], p[0:64, 0:Tn], hE[:, h, 0:Tn], ALU.mult, [p, hE], [hB])
                    else:
                        act(hA[:, h, 0:Tn], p[0:64, 0:Tn], AF.Sigmoid, [p], [hA])
                if which == 1:
                    for h in range(H):
                        vts(hB[:, h, 0:Tn], hA[:, h, 0:Tn], omlt[:, l * H + h:l * H + h + 1], lbt[:, l * H + h:l * H + h + 1], ALU.mult, ALU.add, [hA, omlt, lbt], [hB])
                    act(hB[:, :, 0:Tn], hB[:, :, 0:Tn], AF.Ln, [hB], [hB])
                    for h in range(H):
                        S.op("dve", lambda e, h=h: e.tensor_tensor_scan(out=hBc[:, h, 0:Tn], data0=cf[0:64, C_SCM:C_SCM + Tn], data1=hB[:, h, 0:Tn], initial=0.0, op0=ALU.mult, op1=ALU.add),
                             reads=[hB, cf], writes=[hBc])
                    vts(hBc[:, :, 0:Tn], hBc[:, :, 0:Tn], -80.0, None, ALU.max, None, [hBc], [hBc])
                    act(hE[:, :, 0:Tn], hBc[:, :, 0:Tn], AF.Exp, [hBc], [hE])
                    act(hB[:, :, 0:Tn], hBc[:, :, 0:Tn], AF.Exp, [hBc], [hB], scale=-1.0)
                    for h in range(H):
                        vts(hA[:, h, 0:Tn], hA[:, h, 0:Tn], nomlt[:, l * H + h:l * H + h + 1], omlt[:, l * H + h:l * H + h + 1], ALU.mult, ALU.add, [hA, omlt, nomlt], [hA])
                    vtt(hA[:, :, 0:Tn], hA[:, :, 0:Tn], hB[:, :, 0:Tn], ALU.mult, [hA, hB], [hA])
                    vcopy(hkt[:, :, 0:Tn], hA[:, :, 0:Tn], [hA], [hkt])
                    vtt(hB[:, :, 0:Tn], hA[:, :, 0:Tn], hkt[:, :, 0:Tn], ALU.subtract, [hA, hkt], [hB])
                    vcopy(hkl[:, :, 0:Tn], hB[:, :, 0:Tn], [hB], [hkl], eng="pool")
                else:
                    vcopy(hqt[:, :, 0:Tn], hB[:, :, 0:Tn], [hB], [hqt])
                    vtt(hA[:, :, 0:Tn], hB[:, :, 0:Tn], hqt[:, :, 0:Tn], ALU.subtract, [hB, hqt], [hA])
                    vcopy(hql[:, :, 0:Tn], hA[:, :, 0:Tn], [hA], [hql], eng="pool")
            S.mark(f"C5_{l}_{sq_['name']}_{ti}")
            t, v = load_slab("w_in", l, 1796, 512)
            for ch in range(nch):
                p = pA.next()
                for kc in range(KC):
                    mm(p[0:64, 0:512], xn[:, kc, ch * 64:(ch + 1) * 64], v[:, kc, 0:512], kc == 0, kc == KC - 1, [t, xn], [p])
                vcopy(hV[:, ch, :], p[0:64, 0:256], [p], [hV])
                act(hG[:, ch, :], p[0:64, 256:512], AF.Silu, [p], [hG])
                vtt(hG[:, ch, :], hG[:, ch, :], PL[0:64, O_HN:O_HN + 256], ALU.mult, [hG, pls], [hG])
            S.mark(f"C6_{l}_{sq_['name']}_{ti}")
            t, v = load_slab("w_in", l, 2308, 512)
            for oc in range(4):
                p = pA.next()
                for kc in range(KC):
                    mm(p[:, 0:Tn], v[:, kc, oc * 128:(oc + 1) * 128], xn[:, kc, 0:Tn], kc == 0, kc == KC - 1, [t, xn], [p])
                if oc < 2:
                    act(sQT[:, oc, 0:Tn], p[:, 0:Tn], AF.Copy, [p], [sQT], scale=0.125)
                else:
                    vcopy(sKT[:, oc - 2, (past + t0):(past + t0) + Tn], p[:, 0:Tn], [p], [sKT])
            S.mark(f"C7_{l}_{sq_['name']}_{ti}")
            t, v = load_slab("w_in", l, 2564, 512)
            for sbi in range(nsb):
                blk = kb0 + sbi
                r0 = t0 + sbi * SBk
                p = pA.next()
                for kc in range(KC):
                    mm(p[0:SBk, 0:512], xn[:, kc, sbi * SBk:(sbi + 1) * SBk], v[:, kc, 0:512], kc == 0, kc == KC - 1, [t, xn], [p])

                def fill(tl, p=p):
                    act(tl[0:SBk, 0:512], p[0:SBk, 0:512], AF.Copy, [p], [tl])
                tl = stage_out(SBk, 512, fill, [(sq_["sb_k"][l][r0:r0 + SBk, :], 0, 256), (sq_["sb_v"][l][r0:r0 + SBk, :], 256, 256)])
                vcopy(sV[0:SBk, blk, :], tl[0:SBk, 256:512], [tl], [sV], eng="pool")

            t, v = load_slab("w_in", l, 1284, 512)
            for which in (0,):
                for h in range(H):
                    p = pA.next()
                    for kc in range(KC):
                        mm(p[0:64, 0:Tn], v[:, kc, which * 256 + h * 64:which * 256 + (h + 1) * 64], xn[:, kc, 0:Tn], kc == 0, kc == KC - 1, [t, xn], [p])
                    if which == 0:
                        vtt(hB[:, h, 0:Tn], p[0:64, 0:Tn], hE[:, h, 0:Tn], ALU.mult, [p, hE], [hB])
                    else:
                        act(hA[:, h, 0:Tn], p[0:64, 0:Tn], AF.Sigmoid, [p], [hA])
                if which == 1:
                    for h in range(H):
                        vts(hB[:, h, 0:Tn], hA[:, h, 0:Tn], omlt[:, l * H + h:l * H + h + 1], lbt[:, l * H + h:l * H + h + 1], ALU.mult, ALU.add, [hA, omlt, lbt], [hB])
                    act(hB[:, :, 0:Tn], hB[:, :, 0:Tn], AF.Ln, [hB], [hB])
                    for h in range(H):
                        S.op("dve", lambda e, h=h: e.tensor_tensor_scan(out=hBc[:, h, 0:Tn], data0=cf[0:64, C_SCM:C_SCM + Tn], data1=hB[:, h, 0:Tn], initial=0.0, op0=ALU.mult, op1=ALU.add),
                             reads=[hB, cf], writes=[hBc])
                    vts(hBc[:, :, 0:Tn], hBc[:, :, 0:Tn], -80.0, None, ALU.max, None, [hBc], [hBc])
                    act(hE[:, :, 0:Tn], hBc[:, :, 0:Tn], AF.Exp, [hBc], [hE])
                    act(hB[:, :, 0:Tn], hBc[:, :, 0:Tn], AF.Exp, [hBc], [hB], scale=-1.0)
                    for h in range(H):
                        vts(hA[:, h, 0:Tn], hA[:, h, 0:Tn], nomlt[:, l * H + h:l * H + h + 1], omlt[:, l * H + h:l * H + h + 1], ALU.mult, ALU.add, [hA, omlt, nomlt], [hA])
                    vtt(hA[:, :, 0:Tn], hA[:, :, 0:Tn], hB[:, :, 0:Tn], ALU.mult, [hA, hB], [hA])
                    vcopy(hkt[:, :, 0:Tn], hA[:, :, 0:Tn], [hA], [hkt])
                    vtt(hB[:, :, 0:Tn], hA[:, :, 0:Tn], hkt[:, :, 0:Tn], ALU.subtract, [hA, hkt], [hB])
                    vcopy(hkl[:, :, 0:Tn], hB[:, :, 0:Tn], [hB], [hkl], eng="pool")
                else:
                    vcopy(hqt[:, :, 0:Tn], hB[:, :, 0:Tn], [hB], [hqt])
                    vtt(hA[:, :, 0:Tn], hB[:, :, 0:Tn], hqt[:, :, 0:Tn], ALU.subtract, [hB, hqt], [hA])
                    vcopy(hql[:, :, 0:Tn], hA[:, :, 0:Tn], [hA], [hql], eng="pool")
            S.mark(f"D{l}_{sq_['name']}_{ti}")
            for chn in range(2):
                p = pA.next()
                for k in range(CW):
                    mm(p[:, 0:Tn], convD[:, chn * CW + k, :], cext[:, chn, k:k + Tn], k == 0, k == CW - 1, [convD, cext], [p])
                vts(cY[:, chn, 0:Tn], p[:, 0:Tn], PL[:, O_CB + chn:O_CB + chn + 1], None, ALU.add, None, [p, pls], [cY])
            vtt(cY2[:, :, 0:Tn], cY[:, :, 0:Tn], cY[:, :, 0:Tn], ALU.mult, [cY], [cY2])
            p = pM.next()
            for chn in range(2):
                mm(p[:, 0:Tn], ones_f, cY[:, chn, 0:Tn], chn == 0, chn == 1, [cY, cf], [p])
            p2 = pM.next()
            for chn in range(2):
                mm(p2[:, 0:Tn], ones_f, cY2[:, chn, 0:Tn], chn == 0, chn == 1, [cY2, cf], [p2])
            vts(cmean[:, 0:Tn], p[:, 0:Tn], 1.0 / G, None, ALU.mult, None, [p], [cmean])
            vtt(cvar[:, 0:Tn], cmean[:, 0:Tn], cmean[:, 0:Tn], ALU.mult, [cmean], [cvar])
            vstt(cvar[:, 0:Tn], p2[:, 0:Tn], 1.0 / G, cvar[:, 0:Tn], ALU.mult, ALU.subtract, [p2, cvar], [cvar])
            act(cvar[:, 0:Tn], cvar[:, 0:Tn], AF.Ln, [cvar, eps_t], [cvar], bias=eps_t[:, 0:1], scale=1.0)
            act(cvar[:, 0:Tn], cvar[:, 0:Tn], AF.Exp, [cvar], [cvar], scale=-0.5)
            for chn in range(2):
                vtt(cY[:, chn, 0:Tn], cY[:, chn, 0:Tn], cmean[:, 0:Tn], ALU.subtract, [cY, cmean], [cY])
                vtt(cY[:, chn, 0:Tn], cY[:, chn, 0:Tn], cvar[:, 0:Tn], ALU.mult, [cY, cvar], [cY])
                vts(cY[:, chn, 0:Tn], cY[:, chn, 0:Tn], PL[:, O_LG + chn:O_LG + chn + 1], PL[:, O_LB + chn:O_LB + chn + 1], ALU.mult, ALU.add, [cY, pls], [cY])
                act(mixT[:, chn, 0:Tn], cY[:, chn, 0:Tn], AF.Silu, [cY], [mixT])
            if last_tile:
                for cc_ in range(2):
                    S.dma("sp", sq_["conv"][l][:, cc_ * 128:(cc_ + 1) * 128].rearrange("r p -> p r"), uF[:, cc_, :], d_ouF, reads=[uF], writes=[],
                          allow_slow_non_contiguous=True)
            else:
                vcopy(cext[:, :, 0:30], cext[:, :, Tn:Tn + 30], [cext], [cext], eng="pool")

            S.mark(f"F{l}_{sq_['name']}_{ti}")
            past_blocks = list(range(kb0))
            diag = [(kb0 + j, SBk, j * SBk) for j in range(nsb)]
            blocks = [(bk, 128, 0, False) for bk in past_blocks] + [(bk, nk, c0, True) for (bk, nk, c0) in diag]
            Ob = pO.tiles
            for ob in Ob:
                mm(ob[0:SBk, 0:2 * nsb * 65], zer[0:128, 0:SBk], zer[0:128, 0:2 * nsb * 65], True, False, [zer], [ob])
            nb = len(blocks)

            def fox_scores(bi):
                bk, nk, c0, isd = blocks[bi]
                res = []
                for h in range(H):
                    c = h // 2
                    pr = slice(64 * (h % 2), 64 * (h % 2) + 64)
                    Sx = bank6.next()
                    mm(Sx[0:nk, c0:Tn], fKT[pr, c, bk * 128:bk * 128 + nk], fQT[pr, c, c0:Tn], True, False, [fKT, fQT], [Sx])
                    mm(Sx[0:nk, c0:Tn], ones_b[0:3, 0:nk], CTs[0:3, h, c0:Tn], False, not isd, [cb, CTs], [Sx])
                    if isd:
                        mm(Sx[0:nk, c0:Tn], ident_b[0:nk, 0:nk], cb[0:nk, C_NTF:C_NTF + Tn - c0], False, True, [cb], [Sx])
                    P_ = Pt.next()
                    act(P_[0:nk, c0:Tn], Sx[0:nk, c0:Tn], AF.Exp, [Sx, negC], [P_], bias=negC[0:nk, bk, h:h + 1], scale=1.0)
                    res.append(P_)
                return res

            def fox_pv(bi, Ps):
                bk, nk, c0, isd = blocks[bi]
                for h in range(H):
                    O_ = Ob[h // 2]
                    P_ = Ps[h]
                    for jq in range(nsb):
                        if jq * SBk < c0:
                            continue
                        o0 = ((h % 2) * nsb + jq) * 65
                        lastq = (bi == nb - 1) and (jq == nsb - 1) and (h % 2 == 1)
                        mm(O_[0:SBk, o0:o0 + 65], P_[0:nk, jq * SBk:(jq + 1) * SBk], fV[0:nk, bk, h, :], False, lastq, [P_, fV], [O_])
            for bi in range(nb):
                fox_pv(bi, fox_scores(bi))
            for h in range(H):
                O_ = Ob[h // 2]
                for jq in range(nsb):
                    o0 = ((h % 2) * nsb + jq) * 65
                    S.op("dve", lambda e, jq=jq, O_=O_, o0=o0: e.reciprocal(out=rec[0:SBk, jq:jq + 1], in_=O_[0:SBk, o0 + 64:o0 + 65]), reads=[O_], writes=[rec])
                    vts(mixtok[0:SBk, jq, h * 64:(h + 1) * 64], O_[0:SBk, o0:o0 + 64], rec[0:SBk, jq:jq + 1], None, ALU.mult, None, [O_, rec], [mixtok])
            for jq in range(nsb):
                pt = pM.next()
                ptb = pt[:].bitcast(BF16)
                for cc in range(2):
                    tr(ptb[:, cc * 128:cc * 128 + SBk], mixtok[0:SBk, jq, cc * 128:(cc + 1) * 128], ident_b[0:SBk, 0:SBk], [mixtok, cb], [pt])
                for cc in range(2):
                    vcopy(mixT[:, 2 + cc, jq * SBk:(jq + 1) * SBk], ptb[:, cc * 128:cc * 128 + SBk], [pt], [mixT])

            S.mark(f"G{l}_{sq_['name']}_{ti}")
            blocks = [(bk, nk, c0, True) for (bk, nk, c0) in reversed(diag)] + [(bk, 128, 0, False) for bk in reversed(past_blocks)]
            O_ = Ob[0]
            mm(O_[0:SBk, 0:H * nsb * 64], zer[0:128, 0:SBk], zer[0:128, 0:H * nsb * 64], True, False, [zer], [O_])
            for h in range(H):
                S.op("pool", lambda e, h=h: e.memset(SPaf[h][:], 0.0), writes=[SPaf[h]])
                S.op("pool", lambda e, h=h: e.memset(SPab[h][:], 0.0), writes=[SPab[h]])
            nb = len(blocks)

            def sb_stage1(bi):
                bk, nk, c0, isd = blocks[bi]
                res = []
                for h in range(H):
                    c = h // 2
                    pr = slice(64 * (h % 2), 64 * (h % 2) + 64)
                    Z = bank6.next()
                    mm(Z[0:nk, c0:Tn], sKT[pr, c, bk * 128:bk * 128 + nk], sQT[pr, c, c0:Tn], True, True, [sKT, sQT], [Z])
                    E_ = Et.next()
                    act(E_[0:nk, c0:Tn], Z[0:nk, c0:Tn], AF.Exp, [Z], [E_])
                    sp_ = SPb.next()
                    act(sp_[0:nk, c0:Tn], E_[0:nk, c0:Tn], AF.Ln, [E_, one_t], [sp_], bias=one_t[0:nk, 0:1], scale=1.0)
                    if isd:
                        vtt(sp_[0:nk, c0:Tn], sp_[0:nk, c0:Tn], cb[0:nk, C_M01:C_M01 + Tn - c0], ALU.mult, [sp_, cb], [sp_])
                    res.append(sp_)
                return res

            def sb_stage23(bi, sps):
                bk, nk, c0, isd = blocks[bi]
                have_acc = bi > 0
                As = []
                for h in range(H):
                    c = h // 2
                    pr = slice(64 * (h % 2), 64 * (h % 2) + 64)
                    sp_ = sps[h]
                    Lg = bank6.next()
                    mm(Lg[0:nk, c0:Tn], sKT[pr, c, bk * 128:bk * 128 + nk], sQT[pr, c, c0:Tn], True, False, [sKT, sQT], [Lg])
                    mm(Lg[0:nk, c0:Tn], cb[0:nk, C_NTI:C_NTI + nk], sp_[0:nk, c0:Tn], False, not (have_acc or isd), [cb, sp_], [Lg])
                    if have_acc:
                        mm(Lg[0:nk, c0:Tn], cb[0:128, C_NEG1:C_NEG1 + nk], SPab[h][0:128, c0:Tn], False, not isd, [cb, SPab[h]], [Lg])
                    if isd:
                        mm(Lg[0:nk, c0:Tn], ident_b[0:nk, 0:nk], cb[0:nk, C_NTS:C_NTS + Tn - c0], False, True, [cb], [Lg])
                    A_ = Pt.next()
                    act(A_[0:nk, c0:Tn], Lg[0:nk, c0:Tn], AF.Exp, [Lg], [A_])
                    As.append(A_)
                    if bi != nb - 1:
                        vtt(SPaf[h][0:nk, c0:Tn], SPaf[h][0:nk, c0:Tn], sp_[0:nk, c0:Tn], ALU.add, [SPaf[h], sp_], [SPaf[h]])
                        vcopy(SPab[h][0:nk, c0:Tn], SPaf[h][0:nk, c0:Tn], [SPaf[h]], [SPab[h]])
                return As

            def sb_stage3(bi, As):
                bk, nk, c0, isd = blocks[bi]
                for h in range(H):
                    A_ = As[h]
                    for jq in range(nsb):
                        if jq * SBk < c0:
                            continue
                        o0 = (h * nsb + jq) * 64
                        mm(O_[0:SBk, o0:o0 + 64], A_[0:nk, jq * SBk:(jq + 1) * SBk], sV[0:nk, bk, h * 64:(h + 1) * 64], False,
                           (bi == nb - 1) and (jq == nsb - 1) and (h == H - 1), [A_, sV], [O_])
            cur = sb_stage1(0)
            for bi in range(nb):
                As = sb_stage23(bi, cur)
                cur = sb_stage1(bi + 1) if bi + 1 < nb else None
                sb_stage3(bi, As)
            for jq in range(nsb):
                vcopy(mixtok[0:SBk, jq, :].rearrange("p (h d) -> p h d", h=H),
                      O_[0:SBk, 0:H * nsb * 64].rearrange("p (h j d) -> p h j d", h=H, j=nsb)[:, :, jq, :], [O_], [mixtok])
            for jq in range(nsb):
                pt = pM.next()
                ptb = pt[:].bitcast(BF16)
                for cc in range(2):
                    tr(ptb[:, cc * 128:cc * 128 + SBk], mixtok[0:SBk, jq, cc * 128:(cc + 1) * 128], ident_b[0:SBk, 0:SBk], [mixtok, cb], [pt])
                for cc in range(2):
                    vcopy(mixT[:, 6 + cc, jq * SBk:(jq + 1) * SBk], ptb[:, cc * 128:cc * 128 + SBk], [pt], [mixT])

            S.mark(f"H{l}_{sq_['name']}_{ti}")
            pend_tr = []

            def emit_hg_tr(hgt_, cs_):
                pt = pM.next()
                ptb = pt[:].bitcast(BF16)
                for cc in range(2):
                    tr(ptb[:, cc * 64:(cc + 1) * 64], hgt_[:, cc * 128:(cc + 1) * 128], ident_b[0:64, 0:64], [hgt_, cb], [pt])
                for cc in range(2):
                    vcopy(mixT[:, 4 + cc, cs_], ptb[:, cc * 64:(cc + 1) * 64], [pt], [mixT])
            for ch in range(nch):
                cs = slice(ch * 64, (ch + 1) * 64)
                cl = ch * 64 + 63
                hgt = hgts[ch % 2]
                for h in range(H):
                    vts(khT[:, h, :], hkt[:, h, cs], hE[:, h, cl:cl + 1], None, ALU.mult, None, [hkt, hE], [khT])
                pt = pM.next()
                ptb = pt[:].bitcast(BF16)
                for h in range(H):
                    tr(ptb[0:64, h * 64:(h + 1) * 64], khT[:, h, :], ident_b[0:64, 0:64], [khT, cb], [pt])
                vcopy(kh[:].rearrange("p h k -> p (h k)"), ptb[0:64, 0:256], [pt], [kh])
                psc = pM.next()
                for h in range(H):
                    mm(psc[0:64, h * 64:(h + 1) * 64], hkt[:, h, cs], hqt[:, h, cs], True, False, [hkt, hqt], [psc])
                    mm(psc[0:64, h * 64:(h + 1) * 64], hkl[:, h, cs], hqt[:, h, cs], False, False, [hkl, hqt], [psc])
                    mm(psc[0:64, h * 64:(h + 1) * 64], hkt[:, h, cs], hql[:, h, cs], False, True, [hkt, hql], [psc])
                vtt(scT[:].rearrange("p h k -> p (h k)"), psc[0:64, 0:256], cf[0:64, C_HM:C_HM + 256], ALU.mult, [psc, cf], [scT])
                po = pO.next()
                for h in range(H):
                    mm(po[0:64, h * 64:(h + 1) * 64], scT[:, h, :], hV[:, ch, h * 64:(h + 1) * 64], True, False, [scT, hV], [po])
                    mm(po[0:64, h * 64:(h + 1) * 64], hqt[:, h, cs], hSb[:, h, :], False, True, [hqt, hSb], [po])
                psu = pM.next()
                for h in range(H):
                    mm(psu[0:64, h * 64:(h + 1) * 64], kh[:, h, :], hV[:, ch, h * 64:(h + 1) * 64], True, True, [kh, hV], [psu])
                for h in range(H):
                    vstt(hS[:, h, :], hS[:, h, :], hE[:, h, cl:cl + 1], psu[0:64, h * 64:(h + 1) * 64], ALU.mult, ALU.add, [hS, hE, psu], [hS])
                vcopy(hSb[:], hS[:], [hS], [hSb])
                act(ofp[:].rearrange("p h k -> p (h k)"), po[0:64, 0:256], AF.Copy, [po], [ofp])
                vtt(osq[:], ofp[:], ofp[:], ALU.mult, [ofp], [osq])
                S.op("dve", lambda e: e.tensor_reduce(out=ssq[:], in_=osq[:], axis=AX.X, op=ALU.add), reads=[osq], writes=[ssq])
                act(ssq[:], ssq[:], AF.Ln, [ssq, eps_t], [ssq], bias=eps_t[0:64, 0:1], scale=1.0 / HD)
                act(ssq[:], ssq[:], AF.Exp, [ssq], [ssq], scale=-0.5)
                for h in range(H):
                    vstt(hgt[:, h * 64:(h + 1) * 64], ofp[:, h, :], ssq[:, h:h + 1], hG[:, ch, h * 64:(h + 1) * 64], ALU.mult, ALU.mult, [ofp, ssq, hG], [hgt])
                pend_tr.append((hgt, cs))
                if len(pend_tr) > 1:
                    emit_hg_tr(*pend_tr.pop(0))
            while pend_tr:
                emit_hg_tr(*pend_tr.pop(0))
            if last_tile:
                S.dma("sp", sq_["hgrn"][l].rearrange("h k v -> k h v"), hS[:], d_ohS, reads=[hS], writes=[])

            S.mark(f"I{l}_{sq_['name']}_{ti}")
            linear_fm("w_out", l, mixT, Tn, 8, add_resid(Tn))
            S.mark(f"J{l}_{sq_['name']}_{ti}")
            rmsnorm_fm(Tn, PL[:, O_NMEM:O_NMEM + KC], xn)

            def ev_q(oc, p):
                act(mqT[:, oc, 0:Tn], p[:, 0:Tn], AF.Copy, [p], [mqT], scale=1.0 / 16.0)
            linear_fm("w_mq", l, xn, Tn, 8, ev_q)
            for h in range(H):
                for mb in range(2):
                    Sx = pS.next()
                    for dc in range(2):
                        mm(Sx[:, 0:Tn], mKT[:, h * 2 + dc, mb * 128:(mb + 1) * 128], mqT[:, h * 2 + dc, 0:Tn], dc == 0, dc == 1, [mKT, mqT], [Sx])
                    act(Pm[:, mb, 0:Tn], Sx[:, 0:Tn], AF.Exp, [Sx], [Pm])
                pd = pM.next()
                for mb in range(2):
                    mm(pd[:, 0:Tn], ones_b, Pm[:, mb, 0:Tn], mb == 0, mb == 1, [cb, Pm], [pd])
                S.op("dve", lambda e, pd=pd: e.reciprocal(out=rden[:, 0:Tn], in_=pd[:, 0:Tn]), reads=[pd], writes=[rden])
                for dc in range(2):
                    po = pO.next()
                    for mb in range(2):
                        mm(po[:, 0:Tn], mV[:, mb, h * 256 + dc * 128:h * 256 + (dc + 1) * 128], Pm[:, mb, 0:Tn], mb == 0, mb == 1, [mV, Pm], [po])
                    vtt(moT[:, h * 2 + dc, 0:Tn], po[:, 0:Tn], rden[:, 0:Tn], ALU.mult, [po, rden], [moT])
            linear_fm("w_mo", l, moT, Tn, 8, add_resid(Tn))
            S.mark(f"K{l}_{sq_['name']}_{ti}")
            S.dma("sp", xs_d[:, xoff:xoff + Tn].rearrange("(k p) t -> p k t", p=128), xT[:, :, 0:Tn], d_xs,
                  reads=[xT], writes=[("xs", sq_["name"], ti)])

        TG = 512
        if FFN_WIDE:
            hT2 = T(arena.ap[:, 0:2 * nK][:, 0:32 * TG].rearrange("p (a t) -> p a t", a=DFF // 128), [fKT.key, sKT.key])
            xT2 = T(arena.ap[:, 2 * nK:2 * nK + KC * TG * 2].bitcast(F32).rearrange("p (k t) -> p k t", k=KC), [fV.key])
            sq2 = T(arena.ap[:, 2 * nK + nFV:2 * nK + nFV + KC * TG * 2].bitcast(F32).rearrange("p (k t) -> p k t", k=KC), [sV.key])
            xn2 = T(hTflat[:, 0:KC * TG].rearrange("p (k t) -> p k t", k=KC), hT.key)
            rstd2 = T(hTflat[:, KC * TG:KC * TG + 2 * TG].bitcast(F32), hT.key)
            r2 = T(mixT.ap.rearrange("p k t -> p (k t)")[:, 0:2 * TG].bitcast(F32), mixT.key)
        d_x2 = S.dsem("d_x2")

        def ffn_pass(l, goff, Tg, xs_keys, outs):
            S.dma("sp", xT2[:, :, 0:Tg], xs_d[:, goff:goff + Tg].rearrange("(k p) t -> p k t", p=128), d_x2, reads=xs_keys, writes=[xT2])
            PL = pls[:, l, :]
            rmsnorm_fm(Tg, PL[:, O_NFFN:O_NFFN + KC], xn2, xT=xT2, sq=sq2, rstd=rstd2)

            def ev_up(oc, p):
                act(r2[:, 0:Tg], p[:, 0:Tg], AF.Relu, [p], [r2])
                vtt(hT2[:, oc, 0:Tg], r2[:, 0:Tg], r2[:, 0:Tg], ALU.mult, [r2], [hT2])
            linear_fm("w_up", l, xn2, Tg, DFF // 128, ev_up)

            def ev_dn(oc, p):
                vtt(xT2[:, oc, 0:Tg], xT2[:, oc, 0:Tg], p[:, 0:Tg], ALU.add, [p, xT2], [xT2])
            linear_fm("w_down", l, hT2, Tg, 8, ev_dn, kcn=DFF // 128)
            if l < DEPTH - 1:
                S.dma("sp", xs_d[:, goff:goff + Tg].rearrange("(k p) t -> p k t", p=128), xT2[:, :, 0:Tg], d_xs, reads=[xT2], writes=xs_keys)
            else:
                rmsnorm_fm(Tg, gl[:, 0:KC], sq2, xT=xT2, sq=sq2, rstd=rstd2)
                for (ydst, c0, nr) in outs:
                    for half in range(2):
                        p = pM.next()
                        for j in range(4):
                            tr(p[0:nr, j * 128:(j + 1) * 128], sq2[:, half * 4 + j, c0:c0 + nr], ident_f, [sq2, cf], [p])

                        def fill(tl, p=p, nr=nr):
                            vcopy(tl[0:nr, 0:512], p[0:nr, 0:512], [p], [tl])
                        stage_out(nr, 512, fill, [(ydst[:, half * 512:(half + 1) * 512], 0, 512)])

        convD_layer = [None]

        def seq_layer_setup(sq_, l):
            S.mark(f"setup{l}_{sq_['name']}")
            past = sq_["past"]
            b = sq_["b"]
            PL = pls[:, l, :]
            if convD_layer[0] != l:
                convD_layer[0] = l
                for j in range(62):
                    vts(convD[:, j, :], ident_f, PL[:, O_CW + j:O_CW + j + 1], None, ALU.mult, None, [cf, pls], [convD])
            S.op("dve", lambda e: e.memset(Cbase[:], 0.0), writes=[Cbase])
            S.op("dve", lambda e: e.memset(fV[:, :, :, 64:65], 1.0), writes=[fV])
            if past == 0:
                S.op("pool", lambda e: e.memset(cext[:, :, 0:30], 0.0), writes=[cext])
                S.op("dve", lambda e: e.memset(hS[:], 0.0), writes=[hS])
                S.op("dve", lambda e: e.memset(hSb[:], 0.0), writes=[hSb])
                for mb in range(2):
                    S.dma("sp", xtok[:, :], I["mem_prompt"][mb * 128:(mb + 1) * 128, :], d_xtok, writes=[xtok])
                    for g0 in range(0, KC, 4):
                        p = pM.next()
                        for j in range(4):
                            tr(p[:, j * 128:(j + 1) * 128], xtok[:, (g0 + j) * 128:(g0 + j + 1) * 128], ident_f, [xtok, cf], [p])
                        for j in range(4):
                            vcopy(memT[:, g0 + j, mb * 128:(mb + 1) * 128], p[:, j * 128:(j + 1) * 128], [p], [memT])


                for wi, wname in enumerate(("w_mk", "w_mv")):
                    for g0 in range(2):
                        t, v = load_slab(wname, l, g0 * 512, 512)
                        if wi == 0:
                            for j in range(4):
                                p = pA.next()
                                for kc in range(KC):
                                    mm(p[:, 0:NMEM], v[:, kc, j * 128:(j + 1) * 128], memT[:, kc, :], kc == 0, kc == KC - 1, [t, memT], [p])
                                vcopy(mKT[:, g0 * 4 + j, :], p[:, 0:NMEM], [p], [mKT])
                        for mb in range(2):
                            p = pA.next()
                            for kc in range(KC):
                                mm(p[:, 0:512], memT[:, kc, mb * 128:(mb + 1) * 128], v[:, kc, :], kc == 0, kc == KC - 1, [t, memT], [p])

                            def fill(tl, p=p):
                                act(tl[:, 0:512], p[:, 0:512], AF.Copy, [p], [tl])
                            dst = sq_["mem_k" if wi == 0 else "mem_v"][l][mb * 128:(mb + 1) * 128, g0 * 512:(g0 + 1) * 512]
                            tl = stage_out(128, 512, fill, [(dst, 0, 512)])
                            if wi == 1:
                                vcopy(mV[:, mb, g0 * 512:(g0 + 1) * 512], tl[:, 0:512], [tl], [mV], eng="pool")
            else:
                npb = past // 128
                for cc_ in range(2):
                    S.dma("pool", cext[:, cc_, 0:30], I["cache_conv"][l, b][:, cc_ * 128:(cc_ + 1) * 128].rearrange("r p -> p r"), d_cext, writes=[cext],
                          allow_slow_non_contiguous=True)
                S.dma("sp", hS[:], I["state_hgrn"][l, b].rearrange("h k v -> k h v"), d_hS, writes=[hS])
                vcopy(hSb[:], hS[:], [hS], [hSb])
                for h_ in range(H):
                    S.dma("pool", fV[:, 0:npb, h_, 0:64], I["cache_fox_v"][l, b][:, h_ * 64:(h_ + 1) * 64].rearrange("(n p) d -> p n d", p=128), d_fV, writes=[fV])
                S.dma("pool", sV[:, 0:npb, :], I["cache_sb_v"][l, b].rearrange("(n p) g -> p n g", p=128), d_sV, writes=[sV])
                for (cn, dstT) in (("cache_fox_k", fKT), ("cache_sb_k", sKT)):
                    S.dma("pool", ktmp[:, 0:npb, :], I[cn][l, b].rearrange("(n p) g -> p n g", p=128), d_ktmp, writes=[ktmp])
                    for c in range(2):
                        for g0 in range(0, npb, 4):
                            pt = pM.next()
                            ptb = pt[:].bitcast(BF16)
                            ng = min(4, npb - g0)
                            for j in range(ng):
                                tr(ptb[:, j * 128:(j + 1) * 128], ktmp[:, g0 + j, c * 128:(c + 1) * 128], ident_b, [ktmp, cb], [pt])
                            vcopy(dstT[:, c, g0 * 128:(g0 + ng) * 128], ptb[:, 0:ng * 128], [pt], [dstT])
                S.dma("sp", flfp[:, 0:npb, :], I["cache_fox_logf"][l, b].rearrange("(n p) h -> p n h", p=128), d_flfp, writes=[flfp])
                for bk in range(npb):
                    fox_c_block(flfp[:, bk, :], flfp, 128, bk, False, 0)
                S.dma("pool", mV[:], I["cache_mem_v"][l, b].rearrange("(n p) g -> p n g", p=128), d_mV, writes=[mV])
                S.dma("pool", mtmp[:], I["cache_mem_k"][l, b].rearrange("(n p) g -> p n g", p=128), d_mtmp, writes=[mtmp])
                for mb in range(2):
                    for g0 in range(0, 8, 4):
                        pt = pM.next()
                        ptb = pt[:].bitcast(BF16)
                        for j in range(4):
                            tr(ptb[:, j * 128:(j + 1) * 128], mtmp[:, mb, (g0 + j) * 128:(g0 + j + 1) * 128], ident_b, [mtmp, cb], [pt])
                        for j in range(4):
                            vcopy(mKT[:, g0 + j, mb * 128:(mb + 1) * 128], ptb[:, j * 128:(j + 1) * 128], [pt], [mKT])

        seqs = [dict(name="p", L=SEQ, T=TP, past=0, b=0, xoff=0, x_src=I["x_prompt"], y=O["y_prompt"], first_seq=True,
                     conv=[O["conv_p"][l] for l in range(DEPTH)], fox_k=[O["fox_k_p"][l] for l in range(DEPTH)],
                     fox_v=[O["fox_v_p"][l] for l in range(DEPTH)], fox_logf=[O["fox_logf_p"][l] for l in range(DEPTH)],
                     hgrn=[O["hgrn_p"][l] for l in range(DEPTH)], sb_k=[O["sb_k_p"][l] for l in range(DEPTH)],
                     sb_v=[O["sb_v_p"][l] for l in range(DEPTH)], mem_k=[O["mem_k_p"][l] for l in range(DEPTH)],
                     mem_v=[O["mem_v_p"][l] for l in range(DEPTH)])]
        for s_ in range(NS):
            seqs.append(dict(name=f"s{s_}", L=DS, T=DS, past=PAST, b=s_, xoff=SEQ + s_ * DS, x_src=I["x_sample"][s_], y=O["y_sample"][s_], first_seq=False,
                             conv=[O["conv_s"][l, s_] for l in range(DEPTH)], fox_k=[O["fox_k_s"][l, s_] for l in range(DEPTH)],
                             fox_v=[O["fox_v_s"][l, s_] for l in range(DEPTH)], fox_logf=[O["fox_logf_s"][l, s_] for l in range(DEPTH)],
                             hgrn=[O["hgrn_s"][l, s_] for l in range(DEPTH)], sb_k=[O["sb_k_s"][l, s_] for l in range(DEPTH)],
                             sb_v=[O["sb_v_s"][l, s_] for l in range(DEPTH)]))
        for l in range(DEPTH):
            sp_ = seqs[0]
            seq_layer_setup(sp_, l)
            ntl = sp_["L"] // sp_["T"]
            for ti in range(ntl):
                layer_tile(sp_, l, ti)
            tpg = TG // sp_["T"]
            for g in range(sp_["L"] // TG):
                outs = [(sp_["y"][g * TG + j * 128:g * TG + (j + 1) * 128, :], j * 128, 128) for j in range(TG // 128)]
                ffn_pass(l, g * TG, TG, [("xs", "p", g * tpg + j) for j in range(tpg)], outs)
        for l in range(DEPTH):
            for sq_ in seqs[1:]:
                seq_layer_setup(sq_, l)
                layer_tile(sq_, l, 0)
            outs = [(sq_["y"], i * DS, DS) for i, sq_ in enumerate(seqs[1:])]
            ffn_pass(l, SEQ, NS * DS, [("xs", sq_["name"], 0) for sq_ in seqs[1:]], outs)
        S.finalize(final_waits=[d_oflf, d_ouF, d_ohS] + d_stg + [d_xs])
        n_ops = {e: len(S.ops[e]) for e in S.ENGS}
    return nc, n_ops


FULL_CFG = dict(SEQ=4096, T=256, DEPTH=4, PAST=2048, DS=64, NS=2)


def host_pack(inputs, cfg):
    DEPTH = cfg["DEPTH"]
    pl = np.zeros((DEPTH, 128, NPL), np.float32)
    for l in range(DEPTH):
        pl[l, :, O_NMIX:O_NMIX + 8] = inputs["norm_mix"][l].reshape(8, 128).T
        pl[l, :, O_NMEM:O_NMEM + 8] = inputs["norm_mem"][l].reshape(8, 128).T
        pl[l, :, O_NFFN:O_NFFN + 8] = inputs["norm_ffn"][l].reshape(8, 128).T
        cw = inputs["conv_w"][l]
        for ch in range(2):
            pl[l, :, O_CW + ch * 31:O_CW + (ch + 1) * 31] = cw[:, ch * 128:(ch + 1) * 128].T
        pl[l, :, O_CB:O_CB + 2] = inputs["conv_b"][l].reshape(2, 128).T
        pl[l, :, O_LG:O_LG + 2] = inputs["conv_ln_g"][l].reshape(2, 128).T
        pl[l, :, O_LB:O_LB + 2] = inputs["conv_ln_b"][l].reshape(2, 128).T
        pl[l, :, O_BF:O_BF + 4] = inputs["b_fox_f"][l][None, :]
        pl[l, :, O_HN:O_HN + 256] = np.tile(inputs["hgrn_norm"][l], 4)[None, :]
    gl = np.zeros((128, 8 + DEPTH * 4), np.float32)
    gl[:, 0:8] = inputs["norm_final"].reshape(8, 128).T
    gl[0:64, 8:] = inputs["hgrn_lb"].reshape(DEPTH, 4, 64).transpose(2, 0, 1).reshape(64, DEPTH * 4)
    return pl, gl


_NC_CACHE = {}


def make_in_maps(inputs, cfg, n_cores, nb_prompt):
    pl, gl = host_pack(inputs, cfg)
    consts = make_consts(cfg["T"])
    NS = cfg["NS"]
    DEPTH = cfg["DEPTH"]
    maps = []
    f = lambda a: np.ascontiguousarray(a, dtype=np.float32)
    for c in range(n_cores):
        bp = c % nb_prompt
        ss = slice(c * NS, (c + 1) * NS)
        m = {
            "x_prompt": f(inputs["x_prompt"][bp]), "x_sample": f(inputs["x_sample"][ss]), "mem_prompt": f(inputs["mem_prompt"][bp]),
            "cache_conv": f(inputs["cache_conv"][:, ss]),
            "cache_fox_k": f(inputs["cache_fox_k"][:, ss].reshape(DEPTH, NS, -1, G)), "cache_fox_v": f(inputs["cache_fox_v"][:, ss].reshape(DEPTH, NS, -1, G)),
            "cache_sb_k": f(inputs["cache_sb_k"][:, ss].reshape(DEPTH, NS, -1, G)), "cache_sb_v": f(inputs["cache_sb_v"][:, ss].reshape(DEPTH, NS, -1, G)),
            "cache_fox_logf": f(inputs["cache_fox_logf"][:, ss]), "state_hgrn": f(inputs["state_hgrn"][:, ss]),
            "cache_mem_k": f(inputs["cache_mem_k"][:, ss].reshape(DEPTH, NS, NMEM, D)), "cache_mem_v": f(inputs["cache_mem_v"][:, ss].reshape(DEPTH, NS, NMEM, D)),
            "pl": pl, "gl": gl, "consts": consts,
        }
        for n in ("w_in", "w_out", "w_mq", "w_mk", "w_mv", "w_mo", "w_up", "w_down"):
            m[n] = f(inputs[n])
        maps.append(m)
    return maps


def assemble(results, cfg, n_cores, nb_prompt, nb_sample):
    DEPTH, SEQ, DS, NS = cfg["DEPTH"], cfg["SEQ"], cfg["DS"], cfg["NS"]
    R = results
    P = lambda k, shp: np.stack([np.asarray(R[b][k], np.float32) for b in range(nb_prompt)], axis=0 if k.startswith("y_") else 1).reshape(shp)
    y_p = np.stack([R[b]["y_prompt"] for b in range(nb_prompt)], 0)
    y_s = np.concatenate([R[c]["y_sample"] for c in range(n_cores)], 0)

    def pp(k, tail):
        return np.stack([np.asarray(R[b][k]) for b in range(nb_prompt)], 1).reshape((DEPTH, nb_prompt) + tail)

    def ps_(k, tail):
        return np.concatenate([np.asarray(R[c][k]) for c in range(n_cores)], 1).reshape((DEPTH, nb_sample) + tail)
    outs = (y_p, y_s,
            pp("conv_p", (CW - 1, G)), pp("fox_k_p", (SEQ, H, HD)), pp("fox_v_p", (SEQ, H, HD)), pp("fox_logf_p", (SEQ, H)),
            pp("hgrn_p", (H, HD, HD)), pp("sb_k_p", (SEQ, H, HD)), pp("sb_v_p", (SEQ, H, HD)),
            pp("mem_k_p", (NMEM, H, 256)), pp("mem_v_p", (NMEM, H, 256)),
            ps_("conv_s", (CW - 1, G)), ps_("fox_k_s", (DS, H, HD)), ps_("fox_v_s", (DS, H, HD)), ps_("fox_logf_s", (DS, H)),
            ps_("hgrn_s", (H, HD, HD)), ps_("sb_k_s", (DS, H, HD)), ps_("sb_v_s", (DS, H, HD)))
    return tuple(np.ascontiguousarray(o, dtype=np.float32) for o in outs)


def kernel(**inputs):
    cfg = FULL_CFG
    inputs = {k: np.asarray(v) for k, v in inputs.items()}
    if "nc" not in _NC_CACHE:
        _NC_CACHE["nc"] = build(cfg)[0]
    nc = _NC_CACHE["nc"]
    maps = make_in_maps(inputs, cfg, 8, 4)
    res = run_bass_kernel_spmd(nc, maps, core_ids=list(range(8)))
    return assemble(res.results, cfg, 8, 4, 16)
```

```python
import numpy as np
import concourse.bass as bass
import concourse.mybir as mybir
from concourse.bass_utils import run_bass_kernel_spmd
from contextlib import ExitStack

F32 = mybir.dt.float32
BF16 = mybir.dt.bfloat16
ALU = mybir.AluOpType
AF = mybir.ActivationFunctionType
AX = mybir.AxisListType


class T:
    __slots__ = ("ap", "key")

    def __init__(self, ap, key):
        self.ap = ap
        self.key = key

    def __getitem__(self, idx):
        return self.ap[idx]


class DSem:
    __slots__ = ("sem", "val")

    def __init__(self, sem):
        self.sem = sem
        self.val = 0


class Sched:
    ENGS = ("pe", "act", "dve", "pool", "sp")
    SEM_WRAP = 30000

    def __init__(self, nc, stack):
        self.nc = nc
        self.stack = stack
        self.ops = {e: [] for e in self.ENGS}
        self.last_w = {}
        self.readers = {}
        self.nk = 0

    def sem(self, name):
        return self.stack.enter_context(self.nc.semaphore(name))

    def dsem(self, name):
        return DSem(self.sem(name))

    def sb(self, name, shape, dtype):
        h = self.stack.enter_context(self.nc.sbuf_tensor("s_" + name, list(shape), dtype))
        self.nk += 1
        return T(h[:], ("sb", name, self.nk))

    def ps(self, name, shape, dtype):
        h = self.stack.enter_context(self.nc.psum_tensor("q_" + name, list(shape), dtype))
        self.nk += 1
        return T(h[:], ("ps", name, self.nk))

    def _deps(self, reads, writes, ev):
        deps = []
        def expand(lst):
            out = []
            for k in lst:
                k = k.key if isinstance(k, T) else k
                if isinstance(k, list):
                    out.extend(k)
                else:
                    out.append(k)
            return out
        reads = expand(reads)
        writes = expand(writes)
        for k in reads:
            w = self.last_w.get(k)
            if w is not None:
                deps.append(w)
            rl_ = self.readers.setdefault(k, [])
            if isinstance(k, tuple) and k and k[0] == "ps":
                for r in rl_:
                    if r[0] == "e" and r[1] != ev[1]:
                        deps.append(r)
            rl_.append(ev)
        for k in writes:
            w = self.last_w.get(k)
            if w is not None:
                deps.append(w)
            deps.extend(self.readers.get(k, ()))
            self.last_w[k] = ev
            self.readers[k] = []
        return [d for d in deps if d != ev]

    stopped = False

    countdown = None

    def mark(self, name):
        import os
        if os.environ.get("STOP_AT") == name:
            n = int(os.environ.get("STOP_N", "0"))
            if n == 0:
                self.stopped = True
            else:
                self.countdown = n

    def _tick(self):
        if self.countdown is not None:
            self.countdown -= 1
            if self.countdown < 0:
                self.stopped = True

    def op(self, eng, fn, reads=(), writes=()):
        self._tick()
        if self.stopped:
            return None
        lst = self.ops[eng]
        ev = ("e", eng, len(lst))
        deps = self._deps(reads, writes, ev)
        lst.append({"fn": fn, "deps": deps, "sig": False, "dma": None})
        return ev

    def dma(self, q, out_ap, in_ap, dsem, reads=(), writes=(), **kw):
        self._tick()
        if self.stopped:
            return None
        lst = self.ops[q]
        dsem.val += 16
        ev = ("d", dsem, dsem.val)
        deps = self._deps(reads, writes, ev)

        def fn(e, out_ap=out_ap, in_ap=in_ap, kw=kw):
            return e.dma_start(out=out_ap, in_=in_ap, **kw)

        lst.append({"fn": fn, "deps": deps, "sig": False, "dma": dsem})
        return ev

    def finalize(self, final_waits=()):
        nc = self.nc
        for e in self.ENGS:
            for o in self.ops[e]:
                for d in o["deps"]:
                    if d[0] == "e":
                        if d[1] == "pe" and e == "pe":
                            continue
                        self.ops[d[1]][d[2]]["sig"] = True
        val = {}
        for e in self.ENGS:
            c = 0
            gen = 0
            cur = None
            for i, o in enumerate(self.ops[e]):
                if o["sig"] and o["dma"] is None:
                    if cur is None or c >= self.SEM_WRAP:
                        cur = self.sem(f"es_{e}_{gen}")
                        gen += 1
                        c = 0
                    c += 1
                    val[(e, i)] = (cur, c)

        def run(e, eng):
            clock = {}
            for i, o in enumerate(self.ops[e]):
                need = {}
                for d in o["deps"]:
                    if d[0] == "e":
                        if d[1] == "pe" and e == "pe":
                            continue
                        s, v = val[(d[1], d[2])]
                    else:
                        s, v = d[1].sem, d[2]
                    sid = id(s)
                    if clock.get(sid, 0) >= v:
                        continue
                    if sid not in need or need[sid][1] < v:
                        need[sid] = (s, v)
                for sid, (s, v) in need.items():
                    eng.wait_ge(s, v)
                    clock[sid] = v
                ins = o["fn"](eng)
                if o["dma"] is not None:
                    ins.then_inc(o["dma"].sem, 16)
                elif o["sig"]:
                    s, v = val[(e, i)]
                    ins.then_inc(s, 1)
            if e == "sp":
                for ds in final_waits:
                    if ds.val > 0:
                        eng.wait_ge(ds.sem, ds.val)

        with nc.Block() as block:
            @block.tensor
            def _(eng):
                run("pe", eng)

            @block.scalar
            def _(eng):
                run("act", eng)

            @block.vector
            def _(eng):
                run("dve", eng)

            @block.gpsimd
            def _(eng):
                run("pool", eng)

            @block.sync
            def _(eng):
                run("sp", eng)


class Rot:
    def __init__(self, tiles):
        self.tiles = tiles
        self.i = 0

    def next(self):
        t = self.tiles[self.i % len(self.tiles)]
        self.i += 1
        return t


D = 1024
KC = 8
G = 256
H = 4
HD = 64
CW = 31
NMEM = 256
DFF = 4096
PIN = 3076
EPS = 1e-6
O_NMIX, O_NMEM, O_NFFN, O_CW, O_CB, O_LG, O_LB, O_BF, O_HN = 0, 8, 16, 24, 86, 88, 90, 92, 96
NPL = 96 + 256
C_ID, C_TRI, C_ONE, C_HM, C_SCM = 0, 128, 256, 384, 640
NCF = 896
C_NTF, C_NTS, C_M01, C_NTI, C_NEG1 = 896, 1152, 1408, 1664, 1792
NCONST = 1920


def make_consts(TMAX):
    c = np.zeros((128, NCONST), np.float32)
    s = np.arange(128)[:, None]
    t = np.arange(128)[None, :]
    c[:, C_ID:C_ID + 128] = (s == t)
    c[:, C_TRI:C_TRI + 128] = (s <= t)
    c[:, C_ONE:C_ONE + 128] = 1.0
    c[:, C_NTF:C_NTF + 128] = np.where(s <= t, 0.0, -30000.0)
    c[:, C_NTS:C_NTS + 128] = np.where(s < t, 0.0, -30000.0)
    c[:, C_M01:C_M01 + 128] = (s < t)
    c[:, C_M01 + 128:C_M01 + 256] = 1.0
    c[:, C_NTI:C_NTI + 128] = np.where(s >= t, -1.0, 0.0)
    c[:, C_NEG1:C_NEG1 + 128] = -1.0
    hm = (np.arange(64)[:, None] <= np.arange(64)[None, :]).astype(np.float32)
    c[:64, C_HM:C_HM + 256] = np.tile(hm, (1, 4))
    sm = np.ones((256,), np.float32)
    sm[::64] = 0.0
    c[:, C_SCM:C_SCM + 256] = sm[None, :]
    return c


FFN_WIDE = True


def build(cfg):
    SEQ, TP, DEPTH, PAST, DS, NS = cfg["SEQ"], cfg["T"], cfg["DEPTH"], cfg["PAST"], cfg["DS"], cfg["NS"]
    NKB = max(SEQ // 128, PAST // 128 + 1)
    nc = bass.Bass("TRN2", target_bir_lowering=False)

    def din(name, shape, dt=F32):
        return nc.dram_tensor(name, list(shape), dt, kind="ExternalInput").ap()

    def dout(name, shape):
        return nc.dram_tensor(name, list(shape), F32, kind="ExternalOutput").ap()

    def dint(name, shape, dt):
        return nc.dram_tensor(name, list(shape), dt, kind="Internal").ap()

    I = {}
    I["x_prompt"] = din("x_prompt", [SEQ, D])
    I["x_sample"] = din("x_sample", [NS, DS, D])
    I["mem_prompt"] = din("mem_prompt", [NMEM, D])
    I["cache_conv"] = din("cache_conv", [DEPTH, NS, CW - 1, G])
    for n in ("cache_fox_k", "cache_fox_v", "cache_sb_k", "cache_sb_v"):
        I[n] = din(n, [DEPTH, NS, PAST, G])
    I["cache_fox_logf"] = din("cache_fox_logf", [DEPTH, NS, PAST, H])
    I["state_hgrn"] = din("state_hgrn", [DEPTH, NS, H, HD, HD])
    I["cache_mem_k"] = din("cache_mem_k", [DEPTH, NS, NMEM, D])
    I["cache_mem_v"] = din("cache_mem_v", [DEPTH, NS, NMEM, D])
    WSH = {"w_in": (D, PIN), "w_out": (D, D), "w_mq": (D, D), "w_mk": (D, D), "w_mv": (D, D), "w_mo": (D, D),
           "w_up": (D, DFF), "w_down": (DFF, D)}
    Wf = {n: din(n, [DEPTH, s[0], s[1]]) for n, s in WSH.items()}
    SLABS = {"w_in": [(0, 512), (512, 512), (768, 516), (1284, 512), (1796, 512), (2308, 512), (2564, 512)],
             "w_up": [(i * 512, 512) for i in range(8)], "w_down": [(i * 128, 128) for i in range(8)]}
    for n_ in ("w_out", "w_mq", "w_mk", "w_mv", "w_mo"):
        SLABS[n_] = [(0, 512), (512, 512)]
    Wb = {n: dint(n + "_bf", [DEPTH, len(SLABS[n]), 128, 4128], BF16) for n in WSH}
    pl_d = din("pl", [DEPTH, 128, NPL])
    gl_d = din("gl", [128, KC + DEPTH * H])
    const_d = din("consts", [128, NCONST])
    O = {}
    O["y_prompt"] = dout("y_prompt", [SEQ, D])
    O["y_sample"] = dout("y_sample", [NS, DS, D])
    O["conv_p"] = dout("conv_p", [DEPTH, CW - 1, G])
    O["fox_k_p"] = dout("fox_k_p", [DEPTH, SEQ, G])
    O["fox_v_p"] = dout("fox_v_p", [DEPTH, SEQ, G])
    O["fox_logf_p"] = dout("fox_logf_p", [DEPTH, SEQ, H])
    O["hgrn_p"] = dout("hgrn_p", [DEPTH, H, HD, HD])
    O["sb_k_p"] = dout("sb_k_p", [DEPTH, SEQ, G])
    O["sb_v_p"] = dout("sb_v_p", [DEPTH, SEQ, G])
    O["mem_k_p"] = dout("mem_k_p", [DEPTH, NMEM, D])
    O["mem_v_p"] = dout("mem_v_p", [DEPTH, NMEM, D])
    O["conv_s"] = dout("conv_s", [DEPTH, NS, CW - 1, G])
    O["fox_k_s"] = dout("fox_k_s", [DEPTH, NS, DS, G])
    O["fox_v_s"] = dout("fox_v_s", [DEPTH, NS, DS, G])
    O["fox_logf_s"] = dout("fox_logf_s", [DEPTH, NS, DS, H])
    O["hgrn_s"] = dout("hgrn_s", [DEPTH, NS, H, HD, HD])
    O["sb_k_s"] = dout("sb_k_s", [DEPTH, NS, DS, G])
    O["sb_v_s"] = dout("sb_v_s", [DEPTH, NS, DS, G])
    CC = {n: dint(n + "_bf", [DEPTH, NS, PAST, G], BF16) for n in ("cache_fox_k", "cache_fox_v", "cache_sb_k", "cache_sb_v")}
    CC["cache_mem_k"] = dint("cache_mem_k_bf", [DEPTH, NS, NMEM, D], BF16)
    CC["cache_mem_v"] = dint("cache_mem_v_bf", [DEPTH, NS, NMEM, D], BF16)
    LTOT = SEQ + NS * DS
    xs_d = dint("xs", [D, LTOT], F32)

    st = ExitStack()
    with st:
        S = Sched(nc, st)
        TM = max(TP, DS)
        cf = S.sb("cf", [128, NCF], F32)
        cb = S.sb("cb", [128, NCONST], BF16)
        d_c = S.dsem("d_cf")
        d_cb = S.dsem("d_cb")
        d_gl = S.dsem("d_gl")
        d_pls = S.dsem("d_pls")
        S.dma("sp", cf[:], const_d[:, 0:NCF], d_c, writes=[cf])
        S.dma("pool", cb[:], const_d, d_cb, writes=[cb])
        gl = S.sb("gl", [128, KC + DEPTH * H], F32)
        S.dma("sp", gl[:], gl_d, d_gl, writes=[gl])
        pls = S.sb("pls", [128, DEPTH, NPL], F32)
        for l in range(DEPTH):
            S.dma("sp", pls[:, l, :], pl_d[l], d_pls, writes=[pls])
        zer = S.sb("zer", [128, 512], BF16)
        S.op("dve", lambda e: e.memset(zer[:], 0.0), writes=[zer])
        eps_t = S.sb("eps_t", [128, 1], F32)
        S.op("dve", lambda e: e.memset(eps_t[:], EPS), writes=[eps_t])
        one_t = S.sb("one_t", [128, 1], F32)
        S.op("dve", lambda e: e.memset(one_t[:], 1.0), writes=[one_t])

        S.mark("cast")
        d_w = {}
        for l in range(DEPTH):
            for n, (r, ccols) in WSH.items():
                d_w[(l, n)] = S.dsem(f"d_w{l}_{n}")
                kcn_ = r // 128
                for idx, (c0, ncl) in enumerate(SLABS[n]):
                    S.dma("pool", Wb[n][l, idx, :, 0:kcn_ * ncl].rearrange("p (k n) -> p k n", k=kcn_),
                          Wf[n][l, :, c0:c0 + ncl].rearrange("(k p) n -> p k n", p=128), d_w[(l, n)], writes=[("wb", l, n)])

        d_cc = S.dsem("d_cc")
        for l in range(DEPTH):
            for b_ in range(NS):
                for n in ("cache_fox_k", "cache_fox_v", "cache_sb_k", "cache_sb_v"):
                    for r0 in range(0, PAST, 1024):
                        r1 = min(PAST, r0 + 1024)
                        S.dma("pool", CC[n][l, b_, r0:r1, :], I[n][l, b_, r0:r1, :], d_cc, writes=[("cc",)])
                for n in ("cache_mem_k", "cache_mem_v"):
                    S.dma("pool", CC[n][l, b_], I[n][l, b_], d_cc, writes=[("cc",)])
        S.mark("lb")
        lbr = gl[0:64, KC:KC + DEPTH * H]
        hl_e = S.sb("hl_e", [64, DEPTH * H], F32)
        hl_m = S.sb("hl_m", [64, H], F32)
        hl_s = S.sb("hl_s", [64, H], F32)
        lbt = S.sb("lbt", [64, DEPTH * H], F32)
        omlt = S.sb("omlt", [64, DEPTH * H], F32)
        nomlt = S.sb("nomlt", [64, DEPTH * H], F32)
        hl_all = [gl, hl_e, hl_m, hl_s, lbt, omlt, nomlt]

        def sl(l):
            return slice(l * H, (l + 1) * H)
        S.op("dve", lambda e: e.tensor_copy(out=hl_m[:], in_=gl[0:64, KC:KC + H]), reads=hl_all, writes=hl_all)
        for l in range(1, DEPTH):
            S.op("dve", lambda e, l=l: e.tensor_tensor(out=hl_m[:], in0=hl_m[:], in1=gl[0:64, KC + l * H:KC + (l + 1) * H], op=ALU.max), reads=hl_all, writes=hl_all)
        for l in range(DEPTH):
            S.op("dve", lambda e, l=l: e.tensor_tensor(out=hl_e[:, sl(l)], in0=gl[0:64, KC + l * H:KC + (l + 1) * H], in1=hl_m[:], op=ALU.subtract), reads=hl_all, writes=hl_all)
        S.op("act", lambda e: e.activation(out=hl_e[:], in_=hl_e[:], func=AF.Exp), reads=hl_all, writes=hl_all)
        S.op("dve", lambda e: e.tensor_copy(out=hl_s[:], in_=hl_e[:, sl(0)]), reads=hl_all, writes=hl_all)
        for l in range(1, DEPTH):
            S.op("dve", lambda e, l=l: e.tensor_tensor(out=hl_s[:], in0=hl_s[:], in1=hl_e[:, sl(l)], op=ALU.add), reads=hl_all, writes=hl_all)
        S.op("dve", lambda e: e.reciprocal(out=hl_s[:], in_=hl_s[:]), reads=hl_all, writes=hl_all)
        for l in range(DEPTH):
            S.op("dve", lambda e, l=l: e.tensor_tensor(out=hl_e[:, sl(l)], in0=hl_e[:, sl(l)], in1=hl_s[:], op=ALU.mult), reads=hl_all, writes=hl_all)
        S.op("dve", lambda e: e.memset(lbt[:, sl(0)], 0.0), reads=hl_all, writes=hl_all)
        for l in range(1, DEPTH):
            S.op("dve", lambda e, l=l: e.tensor_tensor(out=lbt[:, sl(l)], in0=lbt[:, sl(l - 1)], in1=hl_e[:, sl(l)], op=ALU.add), reads=hl_all, writes=hl_all)
        S.op("dve", lambda e: e.tensor_scalar(out=lbt[:], in0=lbt[:], scalar1=0.0, scalar2=1.0 - 1e-6, op0=ALU.max, op1=ALU.min), reads=hl_all, writes=hl_all)
        S.op("dve", lambda e: e.tensor_scalar(out=omlt[:], in0=lbt[:], scalar1=-1.0, scalar2=1.0, op0=ALU.mult, op1=ALU.add), reads=hl_all, writes=hl_all)
        S.op("dve", lambda e: e.tensor_scalar(out=nomlt[:], in0=omlt[:], scalar1=-1.0, scalar2=None, op0=ALU.mult), reads=hl_all, writes=hl_all)

        nK = max(2 * NKB * 128, 8192)
        nFV = max(NKB * H * 65, 8192)
        nSV = max(NKB * G, 8192)
        arena = S.sb("arena", [128, 2 * nK + nFV + nSV], BF16)
        fKT = T(arena.ap[:, 0:2 * NKB * 128].rearrange("p (c n) -> p c n", c=2), ("sb", "fKT"))
        sKT = T(arena.ap[:, nK:nK + 2 * NKB * 128].rearrange("p (c n) -> p c n", c=2), ("sb", "sKT"))
        fV = T(arena.ap[:, 2 * nK:2 * nK + NKB * H * 65].rearrange("p (n h d) -> p n h d", n=NKB, h=H), ("sb", "fV"))
        sV = T(arena.ap[:, 2 * nK + nFV:2 * nK + nFV + NKB * G].rearrange("p (n g) -> p n g", n=NKB), ("sb", "sV"))
        negC = S.sb("negC", [128, NKB, H], F32)
        Cbase = S.sb("Cbase", [128, H], F32)
        cext = S.sb("cext", [128, 2, 30 + TM], BF16)
        convD = S.sb("convD", [128, 62, 128], BF16)
        hS = S.sb("hS", [64, H, HD], F32)
        hSb = S.sb("hSb", [64, H, HD], BF16)
        mKT = S.sb("mKT", [128, 8, NMEM], BF16)
        mV = S.sb("mV", [128, 2, D], BF16)
        xT = S.sb("xT", [128, KC, TM], F32)
        xn = S.sb("xn", [128, KC, TM], BF16)
        mixT = S.sb("mixT", [128, KC, TM], BF16)
        mqT = mixT
        moT = xn
        assert TM == 256 and PAST // 128 <= 16
        hT = S.sb("hT", [128, DFF // 128, TM], BF16)
        hTflat = hT.ap.rearrange("p a t -> p (a t)")
        ktmp = T(hTflat[:, 0:4096].rearrange("p (n g) -> p n g", g=G), hT.key)
        mtmp = T(hTflat[:, 4096:6144].rearrange("p (n g) -> p n g", g=D), hT.key)
        memT = T(hTflat[:, 4096:6144].rearrange("p (k m) -> p k m", m=NMEM), hT.key)
        xtok = T(hTflat[:, 6144:8192].bitcast(F32), hT.key)
        sq = T(hTflat[:, 0:4096].bitcast(F32).rearrange("p (k t) -> p k t", k=KC), hT.key)
        rstd = S.sb("rstd", [128, TM], F32)
        fQT = S.sb("fQT", [128, 2, TM], BF16)
        sQT = S.sb("sQT", [128, 2, TM], BF16)
        sgT = S.sb("sgT", [128, 2, TM], F32)
        uF = S.sb("uF", [128, 2, 30], F32)
        cY = S.sb("cY", [128, 2, TM], F32)
        cY2 = sgT
        cmean = S.sb("cmean", [128, TM], F32)
        cvar = S.sb("cvar", [128, TM], F32)
        flf = S.sb("flf", [128, H], F32)
        flfp = S.sb("flfp", [128, max(PAST // 128, 1), H], F32)
        Cq = S.sb("Cq", [128, H], F32)
        cs_b = S.sb("cs_b", [128, H], BF16)
        cs_f = S.sb("cs_f", [128, H], F32)
        cs_r = S.sb("cs_r", [128, H], F32)
        Cparts = S.sb("Cparts", [128, H, 3], BF16)
        CTs = S.sb("CTs", [3, H, TM], BF16)
        Pt = Rot([S.sb(f"Pt{i}", [128, TM], BF16) for i in range(4)])
        SPb = Rot([S.sb(f"SPb{i}", [128, TM], BF16) for i in range(4)])
        SPaf = [S.sb(f"SPaf{i}", [128, TM], F32) for i in range(H)]
        SPab = [S.sb(f"SPab{i}", [128, TM], BF16) for i in range(H)]
        rec = S.sb("rec", [128, 4], F32)
        mixtok = S.sb("mixtok", [128, 4, G], BF16)
        rden = cmean
        Pm = S.sb("Pm", [128, 2, TM], BF16)
        rl = Rot([S.sb(f"rl{i}", [128, TM], F32) for i in range(1)])
        Et = rl
        stg = Rot([S.sb(f"stg{i}", [128, 512], F32) for i in range(2)])
        d_stg = [S.dsem(f"d_stg{i}") for i in range(2)]
        stg_i = [0]
        d_xtok = S.dsem("d_xtok")
        d_x = S.dsem("d_x")
        d_xs = S.dsem("d_xs")
        d_misc = S.dsem("d_misc")
        d_oflf = S.dsem("d_oflf")
        d_ouF = S.dsem("d_ouF")
        d_ohS = S.dsem("d_ohS")
        d_cext = S.dsem("d_cext"); d_hS = S.dsem("d_hS"); d_fV = S.dsem("d_fV"); d_sV = S.dsem("d_sV")
        d_ktmp = S.dsem("d_ktmp"); d_flfp = S.dsem("d_flfp"); d_mV = S.dsem("d_mV"); d_mtmp = S.dsem("d_mtmp")
        slab = [S.sb(f"slab{i}", [128, 4128], BF16) for i in range(3)]
        d_slab = [S.dsem(f"d_slab{i}") for i in range(3)]
        slab_i = [0]
        hTf32 = hTflat.bitcast(F32)
        hA = T(hTf32[0:64, 0:1024].rearrange("p (h t) -> p h t", h=H), hT.key)
        hB = T(hTf32[0:64, 1024:2048].rearrange("p (h t) -> p h t", h=H), hT.key)
        hBc = T(hTf32[0:64, 2048:3072].rearrange("p (h t) -> p h t", h=H), hT.key)
        hE = T(hTf32[0:64, 3072:4096].rearrange("p (h t) -> p h t", h=H), hT.key)
        hqt = S.sb("hqt", [64, H, TM], BF16)
        hkt = S.sb("hkt", [64, H, TM], BF16)
        hql = S.sb("hql", [64, H, TM], BF16)
        hkl = S.sb("hkl", [64, H, TM], BF16)
        NCH = max(TM // 64, 1)
        hV = S.sb("hV", [64, NCH, G], BF16)
        hG = S.sb("hG", [64, NCH, G], BF16)
        khT = S.sb("khT", [64, H, 64], BF16)
        kh = S.sb("kh", [64, H, 64], BF16)
        scT = S.sb("scT", [64, H, 64], BF16)
        ofp = S.sb("ofp", [64, H, 64], F32)
        osq = S.sb("osq", [64, H, 64], F32)
        ssq = S.sb("ssq", [64, H], F32)
        hgts = [S.sb(f"hgt{i}", [64, G], BF16) for i in range(2)]
        pA = Rot([S.ps(f"pA{i}", [128, 512], F32) for i in range(2)])
        pS = Rot([S.ps(f"pS{i}", [128, 512], F32) for i in range(2)])
        pO = Rot([S.ps(f"pO{i}", [128, 512], F32) for i in range(2)])
        pM = Rot([S.ps(f"pM{i}", [128, 512], F32) for i in range(2)])
        bank6 = Rot(pS.tiles + pA.tiles + pM.tiles)

        ident_f = cf[:, C_ID:C_ID + 128]
        ident_b = cb[:, C_ID:C_ID + 128]
        ones_f = cf[:, C_ONE:C_ONE + 128]
        ones_b = cb[:, C_ONE:C_ONE + 128]

        def mm(out, lhsT, rhs, start, stop, reads, writes):
            S.op("pe", lambda e: e.matmul(out, lhsT=lhsT, rhs=rhs, start=start, stop=stop), reads=reads, writes=writes)

        def tr(out, in_, ident, reads, writes):
            S.op("pe", lambda e: e.transpose(out, in_, ident), reads=reads, writes=writes)

        def act(out, in_, func, reads, writes, **kw):
            S.op("act", lambda e: e.activation(out=out, in_=in_, func=func, **kw), reads=reads, writes=writes)

        def vcopy(out, in_, reads, writes, eng="dve"):
            S.op(eng, lambda e: e.tensor_copy(out=out, in_=in_), reads=reads, writes=writes)

        def vtt(out, in0, in1, op, reads, writes, eng="dve"):
            S.op(eng, lambda e: e.tensor_tensor(out=out, in0=in0, in1=in1, op=op), reads=reads, writes=writes)

        def vts(out, in0, s1, s2, op0, op1, reads, writes, eng="dve"):
            if op1 is None:
                S.op(eng, lambda e: e.tensor_scalar(out=out, in0=in0, scalar1=s1, scalar2=None, op0=op0), reads=reads, writes=writes)
            else:
                S.op(eng, lambda e: e.tensor_scalar(out=out, in0=in0, scalar1=s1, scalar2=s2, op0=op0, op1=op1), reads=reads, writes=writes)

        def vstt(out, in0, scalar, in1, op0, op1, reads, writes, eng="dve"):
            S.op(eng, lambda e: e.scalar_tensor_tensor(out=out, in0=in0, scalar=scalar, in1=in1, op0=op0, op1=op1), reads=reads, writes=writes)

        sq_default = sq

        def load_slab(wname, l, c0, ncols, kcn=KC):
            i = slab_i[0] % 3
            slab_i[0] += 1
            t = slab[i]
            v = t[:, 0:kcn * ncols].rearrange("p (k n) -> p k n", k=kcn)
            idx = SLABS[wname].index((c0, ncols))
            src = Wb[wname][l, idx, :, 0:kcn * ncols]
            S.dma("sp", t[:, 0:kcn * ncols], src, d_slab[i], reads=[("wb", l, wname)], writes=[t])
            return t, v

        def stage_out(n_rows, width, fill, dsts):
            i = stg_i[0] % 2
            stg_i[0] += 1
            t = stg.tiles[i]
            fill(t)
            for (dap, c0, ncl) in dsts:
                S.dma("sp", dap, t[0:n_rows, c0:c0 + ncl], d_stg[i], reads=[t], writes=[])
            return t

        def rmsnorm_fm(Tn, gcol, out_t, out_dt_is_f32=False, xT=xT, sq=None, rstd=rstd):
            sq = sq_default if sq is None else sq
            S.op("dve", lambda e: e.tensor_tensor(out=sq[:, :, 0:Tn], in0=xT[:, :, 0:Tn], in1=xT[:, :, 0:Tn], op=ALU.mult), reads=[xT], writes=[sq])
            p = pM.next()
            for kc in range(KC):
                mm(p[:, 0:Tn], ones_f, sq[:, kc, 0:Tn], kc == 0, kc == KC - 1, [sq, cf], [p])
            act(rstd[:, 0:Tn], p[:, 0:Tn], AF.Ln, [p, eps_t], [rstd], bias=eps_t[:, 0:1], scale=1.0 / D)
            act(rstd[:, 0:Tn], rstd[:, 0:Tn], AF.Exp, [rstd], [rstd], scale=-0.5)
            for kc in range(KC):
                vstt(out_t[:, kc, 0:Tn], xT[:, kc, 0:Tn], gcol[:, kc:kc + 1], rstd[:, 0:Tn], ALU.mult, ALU.mult, [xT, rstd, pls, gl], [out_t])

        def linear_fm(wname, l, src_t, Tn, nout, evac, kcn=KC, c_base=0):
            if kcn == KC:
                for g0 in range(0, nout, 4):
                    ng = min(4, nout - g0)
                    t, v = load_slab(wname, l, c_base + g0 * 128, ng * 128)
                    for j in range(ng):
                        p = pA.next()
                        for kc in range(KC):
                            mm(p[:, 0:Tn], v[:, kc, j * 128:(j + 1) * 128], src_t[:, kc, 0:Tn], kc == 0, kc == KC - 1, [t, src_t], [p])
                        evac(g0 + j, p)
            else:
                for oc in range(nout):
                    t, v = load_slab(wname, l, c_base + oc * 128, 128, kcn=kcn)
                    p = pA.next()
                    for kc in range(kcn):
                        mm(p[:, 0:Tn], v[:, kc, :], src_t[:, kc, 0:Tn], kc == 0, kc == kcn - 1, [t, src_t], [p])
                    evac(oc, p)

        def add_resid(Tn):
            def ev(oc, p):
                vtt(xT[:, oc, 0:Tn], xT[:, oc, 0:Tn], p[:, 0:Tn], ALU.add, [p, xT], [xT])
            return ev

        def fox_c_block(flf_ap, flf_key, n, blk, want_q, qcol0):
            p = pM.next()
            mm(p[0:n, 0:H], cf[0:n, C_TRI:C_TRI + n], flf_ap, True, True, [cf, flf_key], [p])
            vtt(Cq[0:n, :], p[0:n, 0:H], Cbase[0:n, :], ALU.add, [p, Cbase], [Cq])
            vts(negC[0:n, blk, :], Cq[0:n, :], -1.0, None, ALU.mult, None, [Cq], [negC])
            p2 = pM.next()
            mm(p2[:, 0:H], cf[0:n, C_ONE:C_ONE + 128], flf_ap, True, True, [cf, flf_key], [p2])
            vtt(Cbase[:], Cbase[:], p2[:, 0:H], ALU.add, [p2, Cq], [Cbase])
            if want_q:
                cst = [Cq, cs_b, cs_f, cs_r, Cparts]
                vcopy(cs_b[0:n, :], Cq[0:n, :], cst, cst)
                vcopy(Cparts[0:n, :, 0], cs_b[0:n, :], cst, cst)
                vcopy(cs_f[0:n, :], cs_b[0:n, :], cst, cst)
                vtt(cs_r[0:n, :], Cq[0:n, :], cs_f[0:n, :], ALU.subtract, cst, cst)
                vcopy(cs_b[0:n, :], cs_r[0:n, :], cst, cst)
                vcopy(Cparts[0:n, :, 1], cs_b[0:n, :], cst, cst)
                vcopy(cs_f[0:n, :], cs_b[0:n, :], cst, cst)
                vtt(cs_r[0:n, :], cs_r[0:n, :], cs_f[0:n, :], ALU.subtract, cst, cst)
                vcopy(Cparts[0:n, :, 2], cs_r[0:n, :], cst, cst)
                p3 = pM.next()
                for h in range(H):
                    mm(p3[0:3, h * 128:h * 128 + n], Cparts[0:n, h, :], ident_b[0:n, 0:n], True, True, [Cparts, cb], [p3])
                for h in range(H):
                    vcopy(CTs[0:3, h, qcol0:qcol0 + n], p3[0:3, h * 128:h * 128 + n], [p3], [CTs])

        def layer_tile(sq_, l, ti):
            Tn = sq_["T"]
            SBk = min(128, Tn)
            nsb = Tn // SBk
            nch = Tn // 64
            t0 = ti * Tn
            past = sq_["past"]
            kb0 = (past + t0) // 128
            L = sq_["L"]
            ntiles = L // Tn
            last_tile = ti == ntiles - 1
            xoff = sq_["xoff"] + t0
            PL = pls[:, l, :]
            b = sq_["b"]

            S.mark(f"A{l}_{sq_['name']}_{ti}")
            if l == 0:
                for sbi in range(nsb):
                    S.dma("sp", xtok[0:SBk, :], sq_["x_src"][t0 + sbi * SBk:t0 + (sbi + 1) * SBk, :], d_xtok, writes=[xtok])
                    for g0 in range(0, KC, 4):
                        p = pM.next()
                        for j in range(4):
                            tr(p[:, j * 128:j * 128 + SBk], xtok[0:SBk, (g0 + j) * 128:(g0 + j + 1) * 128], ident_f[0:SBk, 0:SBk], [xtok, cf], [p])
                        for j in range(4):
                            vcopy(xT[:, g0 + j, sbi * SBk:(sbi + 1) * SBk], p[:, j * 128:j * 128 + SBk], [p], [xT])
            else:
                S.dma("sp", xT[:, :, 0:Tn], xs_d[:, xoff:xoff + Tn].rearrange("(k p) t -> p k t", p=128), d_x,
                      reads=[("xs", sq_["name"], ti)], writes=[xT])

            S.mark(f"B{l}_{sq_['name']}_{ti}")
            rmsnorm_fm(Tn, PL[:, O_NMIX:O_NMIX + KC], xn)

            S.mark(f"C{l}_{sq_['name']}_{ti}")
            t, v = load_slab("w_in", l, 0, 512)
            for oc in (2, 3, 0, 1):
                p = pA.next()
                for kc in range(KC):
                    mm(p[:, 0:Tn], v[:, kc, oc * 128:(oc + 1) * 128], xn[:, kc, 0:Tn], kc == 0, kc == KC - 1, [t, xn], [p])
                if oc >= 2:
                    act(sgT[:, oc - 2, 0:Tn], p[:, 0:Tn], AF.Sigmoid, [p], [sgT])
                else:
                    vtt(cext[:, oc, 30:30 + Tn], p[:, 0:Tn], sgT[:, oc, 0:Tn], ALU.mult, [p, sgT], [cext])
                    if last_tile:
                        vtt(uF[:, oc, :], p[:, Tn - 30:Tn], sgT[:, oc, Tn - 30:Tn], ALU.mult, [p, sgT], [uF])
            S.mark(f"C2_{l}_{sq_['name']}_{ti}")
            t, v = load_slab("w_in", l, 512, 512)
            for oc in range(4):
                p = pA.next()
                for kc in range(KC):
                    mm(p[:, 0:Tn], v[:, kc, oc * 128:(oc + 1) * 128], xn[:, kc, 0:Tn], kc == 0, kc == KC - 1, [t, xn], [p])
                if oc < 2:
                    act(fQT[:, oc, 0:Tn], p[:, 0:Tn], AF.Copy, [p], [fQT], scale=0.125)
                else:
                    vcopy(fKT[:, oc - 2, (past + t0):(past + t0) + Tn], p[:, 0:Tn], [p], [fKT])
            S.mark(f"C3_{l}_{sq_['name']}_{ti}")
            t, v = load_slab("w_in", l, 768, 516)
            for sbi in range(nsb):
                blk = kb0 + sbi
                r0 = t0 + sbi * SBk
                p = pA.next()
                p2 = pM.next()
                for kc in range(KC):
                    mm(p[0:SBk, 0:512], xn[:, kc, sbi * SBk:(sbi + 1) * SBk], v[:, kc, 0:512], kc == 0, kc == KC - 1, [t, xn], [p])
                for kc in range(KC):
                    mm(p2[0:SBk, 0:4], xn[:, kc, sbi * SBk:(sbi + 1) * SBk], v[:, kc, 512:516], kc == 0, kc == KC - 1, [t, xn], [p2])

                def fill(tl, p=p):
                    act(tl[0:SBk, 0:512], p[0:SBk, 0:512], AF.Copy, [p], [tl])
                tl = stage_out(SBk, 512, fill, [(sq_["fox_k"][l][r0:r0 + SBk, :], 0, 256), (sq_["fox_v"][l][r0:r0 + SBk, :], 256, 256)])
                vcopy(fV[0:SBk, blk, :, 0:64], tl[0:SBk, 256:512].rearrange("p (h d) -> p h d", h=H), [tl], [fV], eng="pool")
                vtt(flf[0:SBk, :], p2[0:SBk, 0:4], PL[0:SBk, O_BF:O_BF + 4], ALU.add, [p2, pls], [flf])
                act(flf[0:SBk, :], flf[0:SBk, :], AF.Sigmoid, [flf], [flf])
                act(flf[0:SBk, :], flf[0:SBk, :], AF.Ln, [flf], [flf])
                S.dma("sp", sq_["fox_logf"][l][r0:r0 + SBk, :], flf[0:SBk, :], d_oflf, reads=[flf], writes=[])
                fox_c_block(flf[0:SBk, :], flf, SBk, blk, True, sbi * SBk)
            S.mark(f"C4_{l}_{sq_['name']}_{ti}")
            t, v = load_slab("w_in", l, 1284, 512)
            for which in (1,):
                for h in range(H):
                    p = pA.next()
                    for kc in range(KC):
                        mm(p[0:64, 0:Tn], v[:, kc, which * 256 + h * 64:which * 256 + (h + 1) * 64], xn[:, kc, 0:Tn], kc == 0, kc == KC - 1, [t, xn], [p])
                    if which == 0:
                        vtt(hB[:, h, 0:Tn], p[0:64, 0:Tn], hE[:, h, 0:Tn], ALU.mult, [p, hE], [hB])
                    else:
                        act(hA[:, h, 0:Tn], p[0:64, 0:Tn], AF.Sigmoid, [p], [hA])
                if which == 1:
                    for h in range(H):
                        vts(hB[:, h, 0:Tn], hA[:, h, 0:Tn], omlt[:, l * H + h:l * H + h + 1], lbt[:, l * H + h:l * H + h + 1], ALU.mult, ALU.add, [hA, omlt, lbt], [hB])
                    act(hB[:, :, 0:Tn], hB[:, :, 0:Tn], AF.Ln, [hB], [hB])
                    for h in range(H):
                        S.op("dve", lambda e, h=h: e.tensor_tensor_scan(out=hBc[:, h, 0:Tn], data0=cf[0:64, C_SCM:C_SCM + Tn], data1=hB[:, h, 0:Tn], initial=0.0, op0=ALU.mult, op1=ALU.add),
                             reads=[hB, cf], writes=[hBc])
                    vts(hBc[:, :, 0:Tn], hBc[:, :, 0:Tn], -80.0, None, ALU.max, None, [hBc], [hBc])
                    act(hE[:, :, 0:Tn], hBc[:, :, 0:Tn], AF.Exp, [hBc], [hE])
                    act(hB[:, :, 0:Tn], hBc[:, :, 0:Tn], AF.Exp, [hBc], [hB], scale=-1.0)
                    for h in range(H):
                        vts(hA[:, h, 0:Tn], hA[:, h, 0:Tn], nomlt[:, l * H + h:l * H + h + 1], omlt[:, l * H + h:l * H + h + 1], ALU.mult, ALU.add, [hA, omlt, nomlt], [hA])
                    vtt(hA[:, :, 0:Tn], hA[:, :, 0:Tn], hB[:, :, 0:Tn], ALU.mult, [hA, hB], [hA])
                    vcopy(hkt[:, :, 0:Tn], hA[:, :, 0:Tn], [hA], [hkt])
                    vtt(hB[:, :, 0:Tn], hA[:, :, 0:Tn], hkt[:, :, 0:Tn], ALU.subtract, [hA, hkt], [hB])
                    vcopy(hkl[:, :, 0:Tn], hB[:, :, 0:Tn], [hB], [hkl], eng="pool")
                else:
                    vcopy(hqt[:, :, 0:Tn], hB[:, :, 0:Tn], [hB], [hqt])
                    vtt(hA[:, :, 0:Tn], hB[:, :, 0:Tn], hqt[:, :, 0:Tn], ALU.subtract, [hB, hqt], [hA])
                    vcopy(hql[:, :, 0:Tn], hA[:, :, 0:Tn], [hA], [hql], eng="pool")
            S.mark(f"C5_{l}_{sq_['name']}_{ti}")
            t, v = load_slab("w_in", l, 1796, 512)
            for ch in range(nch):
                p = pA.next()
                for kc in range(KC):
                    mm(p[0:64, 0:512], xn[:, kc, ch * 64:(ch + 1) * 64], v[:, kc, 0:512], kc == 0, kc == KC - 1, [t, xn], [p])
                vcopy(hV[:, ch, :], p[0:64, 0:256], [p], [hV])
                act(hG[:, ch, :], p[0:64, 256:512], AF.Silu, [p], [hG])
                vtt(hG[:, ch, :], hG[:, ch, :], PL[0:64, O_HN:O_HN + 256], ALU.mult, [hG, pls], [hG])
            S.mark(f"C6_{l}_{sq_['name']}_{ti}")
            t, v = load_slab("w_in", l, 2308, 512)
            for oc in range(4):
                p = pA.next()
                for kc in range(KC):
                    mm(p[:, 0:Tn], v[:, kc, oc * 128:(oc + 1) * 128], xn[:, kc, 0:Tn], kc == 0, kc == KC - 1, [t, xn], [p])
                if oc < 2:
                    act(sQT[:, oc, 0:Tn], p[:, 0:Tn], AF.Copy, [p], [sQT], scale=0.125)
                else:
                    vcopy(sKT[:, oc - 2, (past + t0):(past + t0) + Tn], p[:, 0:Tn], [p], [sKT])
            S.mark(f"C7_{l}_{sq_['name']}_{ti}")
            t, v = load_slab("w_in", l, 2564, 512)
            for sbi in range(nsb):
                blk = kb0 + sbi
                r0 = t0 + sbi * SBk
                p = pA.next()
                for kc in range(KC):
                    mm(p[0:SBk, 0:512], xn[:, kc, sbi * SBk:(sbi + 1) * SBk], v[:, kc, 0:512], kc == 0, kc == KC - 1, [t, xn], [p])

                def fill(tl, p=p):
                    act(tl[0:SBk, 0:512], p[0:SBk, 0:512], AF.Copy, [p], [tl])
                tl = stage_out(SBk, 512, fill, [(sq_["sb_k"][l][r0:r0 + SBk, :], 0, 256), (sq_["sb_v"][l][r0:r0 + SBk, :], 256, 256)])
                vcopy(sV[0:SBk, blk, :], tl[0:SBk, 256:512], [tl], [sV], eng="pool")

            t, v = load_slab("w_in", l, 1284, 512)
            for which in (0,):
                for h in range(H):
                    p = pA.next()
                    for kc in range(KC):
                        mm(p[0:64, 0:Tn], v[:, kc, which * 256 + h * 64:which * 256 + (h + 1) * 64], xn[:, kc, 0:Tn], kc == 0, kc == KC - 1, [t, xn], [p])
                    if which == 0:
                        vtt(hB[:, h, 0:Tn], p[0:64, 0:Tn], hE[:, h, 0:Tn], ALU.mult, [p, hE], [hB])
                    else:
                        act(hA[:, h, 0:Tn], p[0:64, 0:Tn], AF.Sigmoid, [p], [hA])
                if which == 1:
                    for h in range(H):
                        vts(hB[:, h, 0:Tn], hA[:, h, 0:Tn], omlt[:, l * H + h:l * H + h + 1], lbt[:, l * H + h:l * H + h + 1], ALU.mult, ALU.add, [hA, omlt, lbt], [hB])
                    act(hB[:, :, 0:Tn], hB[:, :, 0:Tn], AF.Ln, [hB], [hB])
                    for h in range(H):
                        S.op("dve", lambda e, h=h: e.tensor_tensor_scan(out=hBc[:, h, 0:Tn], data0=cf[0:64, C_SCM:C_SCM + Tn], data1=hB[:, h, 0:Tn], initial=0.0, op0=ALU.mult, op1=ALU.add),
                             reads=[hB, cf], writes=[hBc])
                    vts(hBc[:, :, 0:Tn], hBc[:, :, 0:Tn], -80.0, None, ALU.max, None, [hBc], [hBc])
                    act(hE[:, :, 0:Tn], hBc[:, :, 0:Tn], AF.Exp, [hBc], [hE])
                    act(hB[:, :, 0:Tn], hBc[:, :, 0:Tn], AF.Exp, [hBc], [hB], scale=-1.0)
                    for h in range(H):
                        vts(hA[:, h, 0:Tn], hA[:, h, 0:Tn], nomlt[:, l * H + h:l * H + h + 1], omlt[:, l * H + h:l * H + h + 1], ALU.mult, ALU.add, [hA, omlt, nomlt], [hA])
                    vtt(hA[:, :, 0:Tn], hA[:, :, 0:Tn], hB[:, :, 0:Tn], ALU.mult, [hA, hB], [hA])
                    vcopy(hkt[:, :, 0:Tn], hA[:, :, 0:Tn], [hA], [hkt])
                    vtt(hB[:, :, 0:Tn], hA[:, :, 0:Tn], hkt[:, :, 0:Tn], ALU.subtract, [hA, hkt], [hB])
                    vcopy(hkl[:, :, 0:Tn], hB[:, :, 0:Tn], [hB], [hkl], eng="pool")
                else:
                    vcopy(hqt[:, :, 0:Tn], hB[:, :, 0:Tn], [hB], [hqt])
                    vtt(hA[:, :, 0:Tn], hB[:, :, 0:Tn], hqt[:, :, 0:Tn], ALU.subtract, [hB, hqt], [hA])
                    vcopy(hql[:, :, 0:Tn], hA[:, :, 0:Tn], [hA], [hql], eng="pool")
            S.mark(f"D{l}_{sq_['name']}_{ti}")
            for chn in range(2):
                p = pA.next()
                for k in range(CW):
                    mm(p[:, 0:Tn], convD[:, chn * CW + k, :], cext[:, chn, k:k + Tn], k == 0, k == CW - 1, [convD, cext], [p])
                vts(cY[:, chn, 0:Tn], p[:, 0:Tn], PL[:, O_CB + chn:O_CB + chn + 1], None, ALU.add, None, [p, pls], [cY])
            vtt(cY2[:, :, 0:Tn], cY[:, :, 0:Tn], cY[:, :, 0:Tn], ALU.mult, [cY], [cY2])
            p = pM.next()
            for chn in range(2):
                mm(p[:, 0:Tn], ones_f, cY[:, chn, 0:Tn], chn == 0, chn == 1, [cY, cf], [p])
            p2 = pM.next()
            for chn in range(2):
                mm(p2[:, 0:Tn], ones_f, cY2[:, chn, 0:Tn], chn == 0, chn == 1, [cY2, cf], [p2])
            vts(cmean[:, 0:Tn], p[:, 0:Tn], 1.0 / G, None, ALU.mult, None, [p], [cmean])
            vtt(cvar[:, 0:Tn], cmean[:, 0:Tn], cmean[:, 0:Tn], ALU.mult, [cmean], [cvar])
            vstt(cvar[:, 0:Tn], p2[:, 0:Tn], 1.0 / G, cvar[:, 0:Tn], ALU.mult, ALU.subtract, [p2, cvar], [cvar])
            act(cvar[:, 0:Tn], cvar[:, 0:Tn], AF.Ln, [cvar, eps_t], [cvar], bias=eps_t[:, 0:1], scale=1.0)
            act(cvar[:, 0:Tn], cvar[:, 0:Tn], AF.Exp, [cvar], [cvar], scale=-0.5)
            for chn in range(2):
                vtt(cY[:, chn, 0:Tn], cY[:, chn, 0:Tn], cmean[:, 0:Tn], ALU.subtract, [cY, cmean], [cY])
                vtt(cY[:, chn, 0:Tn], cY[:, chn, 0:Tn], cvar[:, 0:Tn], ALU.mult, [cY, cvar], [cY])
                vts(cY[:, chn, 0:Tn], cY[:, chn, 0:Tn], PL[:, O_LG + chn:O_LG + chn + 1], PL[:, O_LB + chn:O_LB + chn + 1], ALU.mult, ALU.add, [cY, pls], [cY])
                act(mixT[:, chn, 0:Tn], cY[:, chn, 0:Tn], AF.Silu, [cY], [mixT])
            if last_tile:
                for cc_ in range(2):
                    S.dma("sp", sq_["conv"][l][:, cc_ * 128:(cc_ + 1) * 128].rearrange("r p -> p r"), uF[:, cc_, :], d_ouF, reads=[uF], writes=[],
                          allow_slow_non_contiguous=True)
            else:
                vcopy(cext[:, :, 0:30], cext[:, :, Tn:Tn + 30], [cext], [cext], eng="pool")

            S.mark(f"F{l}_{sq_['name']}_{ti}")
            past_blocks = list(range(kb0))
            diag = [(kb0 + j, SBk, j * SBk) for j in range(nsb)]
            blocks = [(bk, 128, 0, False) for bk in past_blocks] + [(bk, nk, c0, True) for (bk, nk, c0) in diag]
            Ob = pO.tiles
            for ob in Ob:
                mm(ob[0:SBk, 0:2 * nsb * 65], zer[0:128, 0:SBk], zer[0:128, 0:2 * nsb * 65], True, False, [zer], [ob])
            nb = len(blocks)

            def fox_scores(bi):
                bk, nk, c0, isd = blocks[bi]
                res = []
                for h in range(H):
                    c = h // 2
                    pr = slice(64 * (h % 2), 64 * (h % 2) + 64)
                    Sx = bank6.next()
                    mm(Sx[0:nk, c0:Tn], fKT[pr, c, bk * 128:bk * 128 + nk], fQT[pr, c, c0:Tn], True, False, [fKT, fQT], [Sx])
                    mm(Sx[0:nk, c0:Tn], ones_b[0:3, 0:nk], CTs[0:3, h, c0:Tn], False, not isd, [cb, CTs], [Sx])
                    if isd:
                        mm(Sx[0:nk, c0:Tn], ident_b[0:nk, 0:nk], cb[0:nk, C_NTF:C_NTF + Tn - c0], False, True, [cb], [Sx])
                    P_ = Pt.next()
                    act(P_[0:nk, c0:Tn], Sx[0:nk, c0:Tn], AF.Exp, [Sx, negC], [P_], bias=negC[0:nk, bk, h:h + 1], scale=1.0)
                    res.append(P_)
                return res

            def fox_pv(bi, Ps):
                bk, nk, c0, isd = blocks[bi]
                for h in range(H):
                    O_ = Ob[h // 2]
                    P_ = Ps[h]
                    for jq in range(nsb):
                        if jq * SBk < c0:
                            continue
                        o0 = ((h % 2) * nsb + jq) * 65
                        lastq = (bi == nb - 1) and (jq == nsb - 1) and (h % 2 == 1)
                        mm(O_[0:SBk, o0:o0 + 65], P_[0:nk, jq * SBk:(jq + 1) * SBk], fV[0:nk, bk, h, :], False, lastq, [P_, fV], [O_])
            for bi in range(nb):
                fox_pv(bi, fox_scores(bi))
            for h in range(H):
                O_ = Ob[h // 2]
                for jq in range(nsb):
                    o0 = ((h % 2) * nsb + jq) * 65
                    S.op("dve", lambda e, jq=jq, O_=O_, o0=o0: e.reciprocal(out=rec[0:SBk, jq:jq + 1], in_=O_[0:SBk, o0 + 64:o0 + 65]), reads=[O_], writes=[rec])
                    vts(mixtok[0:SBk, jq, h * 64:(h + 1) * 64], O_[0:SBk, o0:o0 + 64], rec[0:SBk, jq:jq + 1], None, ALU.mult, None, [O_, rec], [mixtok])
            for jq in range(nsb):
                pt = pM.next()
                ptb = pt[:].bitcast(BF16)
                for cc in range(2):
                    tr(ptb[:, cc * 128:cc * 128 + SBk], mixtok[0:SBk, jq, cc * 128:(cc + 1) * 128], ident_b[0:SBk, 0:SBk], [mixtok, cb], [pt])
                for cc in range(2):
                    vcopy(mixT[:, 2 + cc, jq * SBk:(jq + 1) * SBk], ptb[:, cc * 128:cc * 128 + SBk], [pt], [mixT])

            S.mark(f"G{l}_{sq_['name']}_{ti}")
            blocks = [(bk, nk, c0, True) for (bk, nk, c0) in reversed(diag)] + [(bk, 128, 0, False) for bk in reversed(past_blocks)]
            O_ = Ob[0]
            mm(O_[0:SBk, 0:H * nsb * 64], zer[0:128, 0:SBk], zer[0:128, 0:H * nsb * 64], True, False, [zer], [O_])
            for h in range(H):
                S.op("pool", lambda e, h=h: e.memset(SPaf[h][:], 0.0), writes=[SPaf[h]])
                S.op("pool", lambda e, h=h: e.memset(SPab[h][:], 0.0), writes=[SPab[h]])
            nb = len(blocks)

            def sb_stage1(bi):
                bk, nk, c0, isd = blocks[bi]
                res = []
                for h in range(H):
                    c = h // 2
                    pr = slice(64 * (h % 2), 64 * (h % 2) + 64)
                    Z = bank6.next()
                    mm(Z[0:nk, c0:Tn], sKT[pr, c, bk * 128:bk * 128 + nk], sQT[pr, c, c0:Tn], True, True, [sKT, sQT], [Z])
                    E_ = Et.next()
                    act(E_[0:nk, c0:Tn], Z[0:nk, c0:Tn], AF.Exp, [Z], [E_])
                    sp_ = SPb.next()
                    act(sp_[0:nk, c0:Tn], E_[0:nk, c0:Tn], AF.Ln, [E_, one_t], [sp_], bias=one_t[0:nk, 0:1], scale=1.0)
                    if isd:
                        vtt(sp_[0:nk, c0:Tn], sp_[0:nk, c0:Tn], cb[0:nk, C_M01:C_M01 + Tn - c0], ALU.mult, [sp_, cb], [sp_])
                    res.append(sp_)
                return res

            def sb_stage23(bi, sps):
                bk, nk, c0, isd = blocks[bi]
                have_acc = bi > 0
                As = []
                for h in range(H):
                    c = h // 2
                    pr = slice(64 * (h % 2), 64 * (h % 2) + 64)
                    sp_ = sps[h]
                    Lg = bank6.next()
                    mm(Lg[0:nk, c0:Tn], sKT[pr, c, bk * 128:bk * 128 + nk], sQT[pr, c, c0:Tn], True, False, [sKT, sQT], [Lg])
                    mm(Lg[0:nk, c0:Tn], cb[0:nk, C_NTI:C_NTI + nk], sp_[0:nk, c0:Tn], False, not (have_acc or isd), [cb, sp_], [Lg])
                    if have_acc:
                        mm(Lg[0:nk, c0:Tn], cb[0:128, C_NEG1:C_NEG1 + nk], SPab[h][0:128, c0:Tn], False, not isd, [cb, SPab[h]], [Lg])
                    if isd:
                        mm(Lg[0:nk, c0:Tn], ident_b[0:nk, 0:nk], cb[0:nk, C_NTS:C_NTS + Tn - c0], False, True, [cb], [Lg])
                    A_ = Pt.next()
                    act(A_[0:nk, c0:Tn], Lg[0:nk, c0:Tn], AF.Exp, [Lg], [A_])
                    As.append(A_)
                    if bi != nb - 1:
                        vtt(SPaf[h][0:nk, c0:Tn], SPaf[h][0:nk, c0:Tn], sp_[0:nk, c0:Tn], ALU.add, [SPaf[h], sp_], [SPaf[h]])
                        vcopy(SPab[h][0:nk, c0:Tn], SPaf[h][0:nk, c0:Tn], [SPaf[h]], [SPab[h]])
                return As

            def sb_stage3(bi, As):
                bk, nk, c0, isd = blocks[bi]
                for h in range(H):
                    A_ = As[h]
                    for jq in range(nsb):
                        if jq * SBk < c0:
                            continue
                        o0 = (h * nsb + jq) * 64
                        mm(O_[0:SBk, o0:o0 + 64], A_[0:nk, jq * SBk:(jq + 1) * SBk], sV[0:nk, bk, h * 64:(h + 1) * 64], False,
                           (bi == nb - 1) and (jq == nsb - 1) and (h == H - 1), [A_, sV], [O_])
            cur = sb_stage1(0)
            for bi in range(nb):
                As = sb_stage23(bi, cur)
                cur = sb_stage1(bi + 1) if bi + 1 < nb else None
                sb_stage3(bi, As)
            for jq in range(nsb):
                vcopy(mixtok[0:SBk, jq, :].rearrange("p (h d) -> p h d", h=H),
                      O_[0:SBk, 0:H * nsb * 64].rearrange("p (h j d) -> p h j d", h=H, j=nsb)[:, :, jq, :], [O_], [mixtok])
            for jq in range(nsb):
                pt = pM.next()
                ptb = pt[:].bitcast(BF16)
                for cc in range(2):
                    tr(ptb[:, cc * 128:cc * 128 + SBk], mixtok[0:SBk, jq, cc * 128:(cc + 1) * 128], ident_b[0:SBk, 0:SBk], [mixtok, cb], [pt])
                for cc in range(2):
                    vcopy(mixT[:, 6 + cc, jq * SBk:(jq + 1) * SBk], ptb[:, cc * 128:cc * 128 + SBk], [pt], [mixT])

            S.mark(f"H{l}_{sq_['name']}_{ti}")
            pend_tr = []

            def emit_hg_tr(hgt_, cs_):
                pt = pM.next()
                ptb = pt[:].bitcast(BF16)
                for cc in range(2):
                    tr(ptb[:, cc * 64:(cc + 1) * 64], hgt_[:, cc * 128:(cc + 1) * 128], ident_b[0:64, 0:64], [hgt_, cb], [pt])
                for cc in range(2):
                    vcopy(mixT[:, 4 + cc, cs_], ptb[:, cc * 64:(cc + 1) * 64], [pt], [mixT])
            for ch in range(nch):
                cs = slice(ch * 64, (ch + 1) * 64)
                cl = ch * 64 + 63
                hgt = hgts[ch % 2]
                for h in range(H):
                    vts(khT[:, h, :], hkt[:, h, cs], hE[:, h, cl:cl + 1], None, ALU.mult, None, [hkt, hE], [khT])
                pt = pM.next()
                ptb = pt[:].bitcast(BF16)
                for h in range(H):
                    tr(ptb[0:64, h * 64:(h + 1) * 64], khT[:, h, :], ident_b[0:64, 0:64], [khT, cb], [pt])
                vcopy(kh[:].rearrange("p h k -> p (h k)"), ptb[0:64, 0:256], [pt], [kh])
                psc = pM.next()
                for h in range(H):
                    mm(psc[0:64, h * 64:(h + 1) * 64], hkt[:, h, cs], hqt[:, h, cs], True, False, [hkt, hqt], [psc])
                    mm(psc[0:64, h * 64:(h + 1) * 64], hkl[:, h, cs], hqt[:, h, cs], False, False, [hkl, hqt], [psc])
                    mm(psc[0:64, h * 64:(h + 1) * 64], hkt[:, h, cs], hql[:, h, cs], False, True, [hkt, hql], [psc])
                vtt(scT[:].rearrange("p h k -> p (h k)"), psc[0:64, 0:256], cf[0:64, C_HM:C_HM + 256], ALU.mult, [psc, cf], [scT])
                po = pO.next()
                for h in range(H):
                    mm(po[0:64, h * 64:(h + 1) * 64], scT[:, h, :], hV[:, ch, h * 64:(h + 1) * 64], True, False, [scT, hV], [po])
                    mm(po[0:64, h * 64:(h + 1) * 64], hqt[:, h, cs], hSb[:, h, :], False, True, [hqt, hSb], [po])
                psu = pM.next()
                for h in range(H):
                    mm(psu[0:64, h * 64:(h + 1) * 64], kh[:, h, :], hV[:, ch, h * 64:(h + 1) * 64], True, True, [kh, hV], [psu])
                for h in range(H):
                    vstt(hS[:, h, :], hS[:, h, :], hE[:, h, cl:cl + 1], psu[0:64, h * 64:(h + 1) * 64], ALU.mult, ALU.add, [hS, hE, psu], [hS])
                vcopy(hSb[:], hS[:], [hS], [hSb])
                act(ofp[:].rearrange("p h k -> p (h k)"), po[0:64, 0:256], AF.Copy, [po], [ofp])
                vtt(osq[:], ofp[:], ofp[:], ALU.mult, [ofp], [osq])
                S.op("dve", lambda e: e.tensor_reduce(out=ssq[:], in_=osq[:], axis=AX.X, op=ALU.add), reads=[osq], writes=[ssq])
                act(ssq[:], ssq[:], AF.Ln, [ssq, eps_t], [ssq], bias=eps_t[0:64, 0:1], scale=1.0 / HD)
                act(ssq[:], ssq[:], AF.Exp, [ssq], [ssq], scale=-0.5)
                for h in range(H):
                    vstt(hgt[:, h * 64:(h + 1) * 64], ofp[:, h, :], ssq[:, h:h + 1], hG[:, ch, h * 64:(h + 1) * 64], ALU.mult, ALU.mult, [ofp, ssq, hG], [hgt])
                pend_tr.append((hgt, cs))
                if len(pend_tr) > 1:
                    emit_hg_tr(*pend_tr.pop(0))
            while pend_tr:
                emit_hg_tr(*pend_tr.pop(0))
            if last_tile:
                S.dma("sp", sq_["hgrn"][l].rearrange("h k v -> k h v"), hS[:], d_ohS, reads=[hS], writes=[])

            S.mark(f"I{l}_{sq_['name']}_{ti}")
            linear_fm("w_out", l, mixT, Tn, 8, add_resid(Tn))
            S.mark(f"J{l}_{sq_['name']}_{ti}")
            rmsnorm_fm(Tn, PL[:, O_NMEM:O_NMEM + KC], xn)

            def ev_q(oc, p):
                act(mqT[:, oc, 0:Tn], p[:, 0:Tn], AF.Copy, [p], [mqT], scale=1.0 / 16.0)
            linear_fm("w_mq", l, xn, Tn, 8, ev_q)
            for h in range(H):
                for mb in range(2):
                    Sx = pS.next()
                    for dc in range(2):
                        mm(Sx[:, 0:Tn], mKT[:, h * 2 + dc, mb * 128:(mb + 1) * 128], mqT[:, h * 2 + dc, 0:Tn], dc == 0, dc == 1, [mKT, mqT], [Sx])
                    act(Pm[:, mb, 0:Tn], Sx[:, 0:Tn], AF.Exp, [Sx], [Pm])
                pd = pM.next()
                for mb in range(2):
                    mm(pd[:, 0:Tn], ones_b, Pm[:, mb, 0:Tn], mb == 0, mb == 1, [cb, Pm], [pd])
                S.op("dve", lambda e, pd=pd: e.reciprocal(out=rden[:, 0:Tn], in_=pd[:, 0:Tn]), reads=[pd], writes=[rden])
                for dc in range(2):
                    po = pO.next()
                    for mb in range(2):
                        mm(po[:, 0:Tn], mV[:, mb, h * 256 + dc * 128:h * 256 + (dc + 1) * 128], Pm[:, mb, 0:Tn], mb == 0, mb == 1, [mV, Pm], [po])
                    vtt(moT[:, h * 2 + dc, 0:Tn], po[:, 0:Tn], rden[:, 0:Tn], ALU.mult, [po, rden], [moT])
            linear_fm("w_mo", l, moT, Tn, 8, add_resid(Tn))
            S.mark(f"K{l}_{sq_['name']}_{ti}")
            S.dma("sp", xs_d[:, xoff:xoff + Tn].rearrange("(k p) t -> p k t", p=128), xT[:, :, 0:Tn], d_xs,
                  reads=[xT], writes=[("xs", sq_["name"], ti)])

        TG = 512
        if FFN_WIDE:
            hT2 = T(arena.ap[:, 0:2 * nK][:, 0:32 * TG].rearrange("p (a t) -> p a t", a=DFF // 128), [fKT.key, sKT.key])
            xT2 = T(arena.ap[:, 2 * nK:2 * nK + KC * TG * 2].bitcast(F32).rearrange("p (k t) -> p k t", k=KC), [fV.key])
            sq2 = T(arena.ap[:, 2 * nK + nFV:2 * nK + nFV + KC * TG * 2].bitcast(F32).rearrange("p (k t) -> p k t", k=KC), [sV.key])
            xn2 = T(hTflat[:, 0:KC * TG].rearrange("p (k t) -> p k t", k=KC), hT.key)
            rstd2 = T(hTflat[:, KC * TG:KC * TG + 2 * TG].bitcast(F32), hT.key)
            r2 = T(mixT.ap.rearrange("p k t -> p (k t)")[:, 0:2 * TG].bitcast(F32), mixT.key)
        d_x2 = S.dsem("d_x2")

        def ffn_pass(l, goff, Tg, xs_keys, outs):
            S.dma("sp", xT2[:, :, 0:Tg], xs_d[:, goff:goff + Tg].rearrange("(k p) t -> p k t", p=128), d_x2, reads=xs_keys, writes=[xT2])
            PL = pls[:, l, :]
            rmsnorm_fm(Tg, PL[:, O_NFFN:O_NFFN + KC], xn2, xT=xT2, sq=sq2, rstd=rstd2)

            def ev_up(oc, p):
                act(r2[:, 0:Tg], p[:, 0:Tg], AF.Relu, [p], [r2])
                vtt(hT2[:, oc, 0:Tg], r2[:, 0:Tg], r2[:, 0:Tg], ALU.mult, [r2], [hT2])
            linear_fm("w_up", l, xn2, Tg, DFF // 128, ev_up)

            def ev_dn(oc, p):
                vtt(xT2[:, oc, 0:Tg], xT2[:, oc, 0:Tg], p[:, 0:Tg], ALU.add, [p, xT2], [xT2])
            linear_fm("w_down", l, hT2, Tg, 8, ev_dn, kcn=DFF // 128)
            if l < DEPTH - 1:
                S.dma("sp", xs_d[:, goff:goff + Tg].rearrange("(k p) t -> p k t", p=128), xT2[:, :, 0:Tg], d_xs, reads=[xT2], writes=xs_keys)
            else:
                rmsnorm_fm(Tg, gl[:, 0:KC], sq2, xT=xT2, sq=sq2, rstd=rstd2)
                for (ydst, c0, nr) in outs:
                    for half in range(2):
                        p = pM.next()
                        for j in range(4):
                            tr(p[0:nr, j * 128:(j + 1) * 128], sq2[:, half * 4 + j, c0:c0 + nr], ident_f, [sq2, cf], [p])

                        def fill(tl, p=p, nr=nr):
                            vcopy(tl[0:nr, 0:512], p[0:nr, 0:512], [p], [tl])
                        stage_out(nr, 512, fill, [(ydst[:, half * 512:(half + 1) * 512], 0, 512)])

        convD_layer = [None]

        def seq_layer_setup(sq_, l):
            S.mark(f"setup{l}_{sq_['name']}")
            past = sq_["past"]
            b = sq_["b"]
            PL = pls[:, l, :]
            if convD_layer[0] != l:
                convD_layer[0] = l
                for j in range(62):
                    vts(convD[:, j, :], ident_f, PL[:, O_CW + j:O_CW + j + 1], None, ALU.mult, None, [cf, pls], [convD])
            S.op("dve", lambda e: e.memset(Cbase[:], 0.0), writes=[Cbase])
            S.op("dve", lambda e: e.memset(fV[:, :, :, 64:65], 1.0), writes=[fV])
            if past == 0:
                S.op("pool", lambda e: e.memset(cext[:, :, 0:30], 0.0), writes=[cext])
                S.op("dve", lambda e: e.memset(hS[:], 0.0), writes=[hS])
                S.op("dve", lambda e: e.memset(hSb[:], 0.0), writes=[hSb])
                for mb in range(2):
                    S.dma("sp", xtok[:, :], I["mem_prompt"][mb * 128:(mb + 1) * 128, :], d_xtok, writes=[xtok])
                    for g0 in range(0, KC, 4):
                        p = pM.next()
                        for j in range(4):
                            tr(p[:, j * 128:(j + 1) * 128], xtok[:, (g0 + j) * 128:(g0 + j + 1) * 128], ident_f, [xtok, cf], [p])
                        for j in range(4):
                            vcopy(memT[:, g0 + j, mb * 128:(mb + 1) * 128], p[:, j * 128:(j + 1) * 128], [p], [memT])


                for wi, wname in enumerate(("w_mk", "w_mv")):
                    for g0 in range(2):
                        t, v = load_slab(wname, l, g0 * 512, 512)
                        if wi == 0:
                            for j in range(4):
                                p = pA.next()
                                for kc in range(KC):
                                    mm(p[:, 0:NMEM], v[:, kc, j * 128:(j + 1) * 128], memT[:, kc, :], kc == 0, kc == KC - 1, [t, memT], [p])
                                vcopy(mKT[:, g0 * 4 + j, :], p[:, 0:NMEM], [p], [mKT])
                        for mb in range(2):
                            p = pA.next()
                            for kc in range(KC):
                                mm(p[:, 0:512], memT[:, kc, mb * 128:(mb + 1) * 128], v[:, kc, :], kc == 0, kc == KC - 1, [t, memT], [p])

                            def fill(tl, p=p):
                                act(tl[:, 0:512], p[:, 0:512], AF.Copy, [p], [tl])
                            dst = sq_["mem_k" if wi == 0 else "mem_v"][l][mb * 128:(mb + 1) * 128, g0 * 512:(g0 + 1) * 512]
                            tl = stage_out(128, 512, fill, [(dst, 0, 512)])
                            if wi == 1:
                                vcopy(mV[:, mb, g0 * 512:(g0 + 1) * 512], tl[:, 0:512], [tl], [mV], eng="pool")
            else:
                npb = past // 128
                for cc_ in range(2):
                    S.dma("pool", cext[:, cc_, 0:30], I["cache_conv"][l, b][:, cc_ * 128:(cc_ + 1) * 128].rearrange("r p -> p r"), d_cext, writes=[cext],
                          allow_slow_non_contiguous=True)
                S.dma("sp", hS[:], I["state_hgrn"][l, b].rearrange("h k v -> k h v"), d_hS, writes=[hS])
                vcopy(hSb[:], hS[:], [hS], [hSb])
                for h_ in range(H):
                    S.dma("sp", fV[:, 0:npb, h_, 0:64], CC["cache_fox_v"][l, b][:, h_ * 64:(h_ + 1) * 64].rearrange("(n p) d -> p n d", p=128), d_fV, reads=[("cc",)], writes=[fV])
                S.dma("sp", sV[:, 0:npb, :], CC["cache_sb_v"][l, b].rearrange("(n p) g -> p n g", p=128), d_sV, reads=[("cc",)], writes=[sV])
                for (cn, dstT) in (("cache_fox_k", fKT), ("cache_sb_k", sKT)):
                    S.dma("sp", ktmp[:, 0:npb, :], CC[cn][l, b].rearrange("(n p) g -> p n g", p=128), d_ktmp, reads=[("cc",)], writes=[ktmp])
                    for c in range(2):
                        for g0 in range(0, npb, 4):
                            pt = pM.next()
                            ptb = pt[:].bitcast(BF16)
                            ng = min(4, npb - g0)
                            for j in range(ng):
                                tr(ptb[:, j * 128:(j + 1) * 128], ktmp[:, g0 + j, c * 128:(c + 1) * 128], ident_b, [ktmp, cb], [pt])
                            vcopy(dstT[:, c, g0 * 128:(g0 + ng) * 128], ptb[:, 0:ng * 128], [pt], [dstT])
                S.dma("sp", flfp[:, 0:npb, :], I["cache_fox_logf"][l, b].rearrange("(n p) h -> p n h", p=128), d_flfp, writes=[flfp])
                for bk in range(npb):
                    fox_c_block(flfp[:, bk, :], flfp, 128, bk, False, 0)
                S.dma("sp", mV[:], CC["cache_mem_v"][l, b].rearrange("(n p) g -> p n g", p=128), d_mV, reads=[("cc",)], writes=[mV])
                S.dma("sp", mtmp[:], CC["cache_mem_k"][l, b].rearrange("(n p) g -> p n g", p=128), d_mtmp, reads=[("cc",)], writes=[mtmp])
                for mb in range(2):
                    for g0 in range(0, 8, 4):
                        pt = pM.next()
                        ptb = pt[:].bitcast(BF16)
                        for j in range(4):
                            tr(ptb[:, j * 128:(j + 1) * 128], mtmp[:, mb, (g0 + j) * 128:(g0 + j + 1) * 128], ident_b, [mtmp, cb], [pt])
                        for j in range(4):
                            vcopy(mKT[:, g0 + j, mb * 128:(mb + 1) * 128], ptb[:, j * 128:(j + 1) * 128], [pt], [mKT])

        seqs = [dict(name="p", L=SEQ, T=TP, past=0, b=0, xoff=0, x_src=I["x_prompt"], y=O["y_prompt"], first_seq=True,
                     conv=[O["conv_p"][l] for l in range(DEPTH)], fox_k=[O["fox_k_p"][l] for l in range(DEPTH)],
                     fox_v=[O["fox_v_p"][l] for l in range(DEPTH)], fox_logf=[O["fox_logf_p"][l] for l in range(DEPTH)],
                     hgrn=[O["hgrn_p"][l] for l in range(DEPTH)], sb_k=[O["sb_k_p"][l] for l in range(DEPTH)],
                     sb_v=[O["sb_v_p"][l] for l in range(DEPTH)], mem_k=[O["mem_k_p"][l] for l in range(DEPTH)],
                     mem_v=[O["mem_v_p"][l] for l in range(DEPTH)])]
        for s_ in range(NS):
            seqs.append(dict(name=f"s{s_}", L=DS, T=DS, past=PAST, b=s_, xoff=SEQ + s_ * DS, x_src=I["x_sample"][s_], y=O["y_sample"][s_], first_seq=False,
                             conv=[O["conv_s"][l, s_] for l in range(DEPTH)], fox_k=[O["fox_k_s"][l, s_] for l in range(DEPTH)],
                             fox_v=[O["fox_v_s"][l, s_] for l in range(DEPTH)], fox_logf=[O["fox_logf_s"][l, s_] for l in range(DEPTH)],
                             hgrn=[O["hgrn_s"][l, s_] for l in range(DEPTH)], sb_k=[O["sb_k_s"][l, s_] for l in range(DEPTH)],
                             sb_v=[O["sb_v_s"][l, s_] for l in range(DEPTH)]))
        for l in range(DEPTH):
            sp_ = seqs[0]
            seq_layer_setup(sp_, l)
            ntl = sp_["L"] // sp_["T"]
            for ti in range(ntl):
                layer_tile(sp_, l, ti)
            tpg = TG // sp_["T"]
            for g in range(sp_["L"] // TG):
                outs = [(sp_["y"][g * TG + j * 128:g * TG + (j + 1) * 128, :], j * 128, 128) for j in range(TG // 128)]
                ffn_pass(l, g * TG, TG, [("xs", "p", g * tpg + j) for j in range(tpg)], outs)
        for l in range(DEPTH):
            for sq_ in seqs[1:]:
                seq_layer_setup(sq_, l)
                layer_tile(sq_, l, 0)
            outs = [(sq_["y"], i * DS, DS) for i, sq_ in enumerate(seqs[1:])]
            ffn_pass(l, SEQ, NS * DS, [("xs", sq_["name"], 0) for sq_ in seqs[1:]], outs)
        S.finalize(final_waits=[d_oflf, d_ouF, d_ohS] + d_stg + [d_xs])
        n_ops = {e: len(S.ops[e]) for e in S.ENGS}
    return nc, n_ops


FULL_CFG = dict(SEQ=4096, T=256, DEPTH=4, PAST=2048, DS=64, NS=2)


def host_pack(inputs, cfg):
    DEPTH = cfg["DEPTH"]
    pl = np.zeros((DEPTH, 128, NPL), np.float32)
    for l in range(DEPTH):
        pl[l, :, O_NMIX:O_NMIX + 8] = inputs["norm_mix"][l].reshape(8, 128).T
        pl[l, :, O_NMEM:O_NMEM + 8] = inputs["norm_mem"][l].reshape(8, 128).T
        pl[l, :, O_NFFN:O_NFFN + 8] = inputs["norm_ffn"][l].reshape(8, 128).T
        cw = inputs["conv_w"][l]
        for ch in range(2):
            pl[l, :, O_CW + ch * 31:O_CW + (ch + 1) * 31] = cw[:, ch * 128:(ch + 1) * 128].T
        pl[l, :, O_CB:O_CB + 2] = inputs["conv_b"][l].reshape(2, 128).T
        pl[l, :, O_LG:O_LG + 2] = inputs["conv_ln_g"][l].reshape(2, 128).T
        pl[l, :, O_LB:O_LB + 2] = inputs["conv_ln_b"][l].reshape(2, 128).T
        pl[l, :, O_BF:O_BF + 4] = inputs["b_fox_f"][l][None, :]
        pl[l, :, O_HN:O_HN + 256] = np.tile(inputs["hgrn_norm"][l], 4)[None, :]
    gl = np.zeros((128, 8 + DEPTH * 4), np.float32)
    gl[:, 0:8] = inputs["norm_final"].reshape(8, 128).T
    gl[0:64, 8:] = inputs["hgrn_lb"].reshape(DEPTH, 4, 64).transpose(2, 0, 1).reshape(64, DEPTH * 4)
    return pl, gl


_NC_CACHE = {}


def make_in_maps(inputs, cfg, n_cores, nb_prompt):
    pl, gl = host_pack(inputs, cfg)
    consts = make_consts(cfg["T"])
    NS = cfg["NS"]
    DEPTH = cfg["DEPTH"]
    maps = []
    f = lambda a: np.ascontiguousarray(a, dtype=np.float32)
    for c in range(n_cores):
        bp = c % nb_prompt
        ss = slice(c * NS, (c + 1) * NS)
        m = {
            "x_prompt": f(inputs["x_prompt"][bp]), "x_sample": f(inputs["x_sample"][ss]), "mem_prompt": f(inputs["mem_prompt"][bp]),
            "cache_conv": f(inputs["cache_conv"][:, ss]),
            "cache_fox_k": f(inputs["cache_fox_k"][:, ss].reshape(DEPTH, NS, -1, G)), "cache_fox_v": f(inputs["cache_fox_v"][:, ss].reshape(DEPTH, NS, -1, G)),
            "cache_sb_k": f(inputs["cache_sb_k"][:, ss].reshape(DEPTH, NS, -1, G)), "cache_sb_v": f(inputs["cache_sb_v"][:, ss].reshape(DEPTH, NS, -1, G)),
            "cache_fox_logf": f(inputs["cache_fox_logf"][:, ss]), "state_hgrn": f(inputs["state_hgrn"][:, ss]),
            "cache_mem_k": f(inputs["cache_mem_k"][:, ss].reshape(DEPTH, NS, NMEM, D)), "cache_mem_v": f(inputs["cache_mem_v"][:, ss].reshape(DEPTH, NS, NMEM, D)),
            "pl": pl, "gl": gl, "consts": consts,
        }
        for n in ("w_in", "w_out", "w_mq", "w_mk", "w_mv", "w_mo", "w_up", "w_down"):
            m[n] = f(inputs[n])
        maps.append(m)
    return maps


def assemble(results, cfg, n_cores, nb_prompt, nb_sample):
    DEPTH, SEQ, DS, NS = cfg["DEPTH"], cfg["SEQ"], cfg["DS"], cfg["NS"]
    R = results
    P = lambda k, shp: np.stack([np.asarray(R[b][k], np.float32) for b in range(nb_prompt)], axis=0 if k.startswith("y_") else 1).reshape(shp)
    y_p = np.stack([R[b]["y_prompt"] for b in range(nb_prompt)], 0)
    y_s = np.concatenate([R[c]["y_sample"] for c in range(n_cores)], 0)

    def pp(k, tail):
        return np.stack([np.asarray(R[b][k]) for b in range(nb_prompt)], 1).reshape((DEPTH, nb_prompt) + tail)

    def ps_(k, tail):
        return np.concatenate([np.asarray(R[c][k]) for c in range(n_cores)], 1).reshape((DEPTH, nb_sample) + tail)
    outs = (y_p, y_s,
            pp("conv_p", (CW - 1, G)), pp("fox_k_p", (SEQ, H, HD)), pp("fox_v_p", (SEQ, H, HD)), pp("fox_logf_p", (SEQ, H)),
            pp("hgrn_p", (H, HD, HD)), pp("sb_k_p", (SEQ, H, HD)), pp("sb_v_p", (SEQ, H, HD)),
            pp("mem_k_p", (NMEM, H, 256)), pp("mem_v_p", (NMEM, H, 256)),
            ps_("conv_s", (CW - 1, G)), ps_("fox_k_s", (DS, H, HD)), ps_("fox_v_s", (DS, H, HD)), ps_("fox_logf_s", (DS, H)),
            ps_("hgrn_s", (H, HD, HD)), ps_("sb_k_s", (DS, H, HD)), ps_("sb_v_s", (DS, H, HD)))
    return tuple(np.ascontiguousarray(o, dtype=np.float32) for o in outs)


def kernel(**inputs):
    cfg = FULL_CFG
    inputs = {k: np.asarray(v) for k, v in inputs.items()}
    if "nc" not in _NC_CACHE:
        _NC_CACHE["nc"] = build(cfg)[0]
    nc = _NC_CACHE["nc"]
    maps = make_in_maps(inputs, cfg, 8, 4)
    res = run_bass_kernel_spmd(nc, maps, core_ids=list(range(8)))
    return assemble(res.results, cfg, 8, 4, 16)
```

```python
import numpy as np
import concourse.bass as bass
import concourse.mybir as mybir
from concourse.bass_utils import run_bass_kernel_spmd
from contextlib import ExitStack

F32 = mybir.dt.float32
BF16 = mybir.dt.bfloat16
ALU = mybir.AluOpType
AF = mybir.ActivationFunctionType
AX = mybir.AxisListType


class T:
    __slots__ = ("ap", "key")

    def __init__(self, ap, key):
        self.ap = ap
        self.key = key

    def __getitem__(self, idx):
        return self.ap[idx]


class DSem:
    __slots__ = ("sem", "val")

    def __init__(self, sem):
        self.sem = sem
        self.val = 0


class Sched:
    ENGS = ("pe", "act", "dve", "pool", "sp")
    SEM_WRAP = 30000

    def __init__(self, nc, stack):
        self.nc = nc
        self.stack = stack
        self.ops = {e: [] for e in self.ENGS}
        self.last_w = {}
        self.readers = {}
        self.nk = 0

    def sem(self, name):
        return self.stack.enter_context(self.nc.semaphore(name))

    def dsem(self, name):
        return DSem(self.sem(name))

    def sb(self, name, shape, dtype):
        h = self.stack.enter_context(self.nc.sbuf_tensor("s_" + name, list(shape), dtype))
        self.nk += 1
        return T(h[:], ("sb", name, self.nk))

    def ps(self, name, shape, dtype):
        h = self.stack.enter_context(self.nc.psum_tensor("q_" + name, list(shape), dtype))
        self.nk += 1
        return T(h[:], ("ps", name, self.nk))

    def _deps(self, reads, writes, ev):
        deps = []
        def expand(lst):
            out = []
            for k in lst:
                k = k.key if isinstance(k, T) else k
                if isinstance(k, list):
                    out.extend(k)
                else:
                    out.append(k)
            return out
        reads = expand(reads)
        writes = expand(writes)
        for k in reads:
            w = self.last_w.get(k)
            if w is not None:
                deps.append(w)
            rl_ = self.readers.setdefault(k, [])
            if isinstance(k, tuple) and k and k[0] == "ps":
                for r in rl_:
                    if r[0] == "e" and r[1] != ev[1]:
                        deps.append(r)
            rl_.append(ev)
        for k in writes:
            w = self.last_w.get(k)
            if w is not None:
                deps.append(w)
            deps.extend(self.readers.get(k, ()))
            self.last_w[k] = ev
            self.readers[k] = []
        return [d for d in deps if d != ev]

    stopped = False

    countdown = None

    def mark(self, name):
        import os
        if os.environ.get("STOP_AT") == name:
            n = int(os.environ.get("STOP_N", "0"))
            if n == 0:
                self.stopped = True
            else:
                self.countdown = n

    def _tick(self):
        if self.countdown is not None:
            self.countdown -= 1
            if self.countdown < 0:
                self.stopped = True

    def op(self, eng, fn, reads=(), writes=()):
        self._tick()
        if self.stopped:
            return None
        lst = self.ops[eng]
        ev = ("e", eng, len(lst))
        deps = self._deps(reads, writes, ev)
        lst.append({"fn": fn, "deps": deps, "sig": False, "dma": None})
        return ev

    def dma(self, q, out_ap, in_ap, dsem, reads=(), writes=(), **kw):
        self._tick()
        if self.stopped:
            return None
        lst = self.ops[q]
        dsem.val += 16
        ev = ("d", dsem, dsem.val)
        deps = self._deps(reads, writes, ev)

        def fn(e, out_ap=out_ap, in_ap=in_ap, kw=kw):
            return e.dma_start(out=out_ap, in_=in_ap, **kw)

        lst.append({"fn": fn, "deps": deps, "sig": False, "dma": dsem})
        return ev

    def finalize(self, final_waits=()):
        nc = self.nc
        for e in self.ENGS:
            for o in self.ops[e]:
                for d in o["deps"]:
                    if d[0] == "e":
                        if d[1] == "pe" and e == "pe":
                            continue
                        self.ops[d[1]][d[2]]["sig"] = True
        val = {}
        for e in self.ENGS:
            c = 0
            gen = 0
            cur = None
            for i, o in enumerate(self.ops[e]):
                if o["sig"] and o["dma"] is None:
                    if cur is None or c >= self.SEM_WRAP:
                        cur = self.sem(f"es_{e}_{gen}")
                        gen += 1
                        c = 0
                    c += 1
                    val[(e, i)] = (cur, c)

        def run(e, eng):
            clock = {}
            for i, o in enumerate(self.ops[e]):
                need = {}
                for d in o["deps"]:
                    if d[0] == "e":
                        if d[1] == "pe" and e == "pe":
                            continue
                        s, v = val[(d[1], d[2])]
                    else:
                        s, v = d[1].sem, d[2]
                    sid = id(s)
                    if clock.get(sid, 0) >= v:
                        continue
                    if sid not in need or need[sid][1] < v:
                        need[sid] = (s, v)
                for sid, (s, v) in need.items():
                    eng.wait_ge(s, v)
                    clock[sid] = v
                ins = o["fn"](eng)
                if o["dma"] is not None:
                    ins.then_inc(o["dma"].sem, 16)
                elif o["sig"]:
                    s, v = val[(e, i)]
                    ins.then_inc(s, 1)
            if e == "sp":
                for ds in final_waits:
                    if ds.val > 0:
                        eng.wait_ge(ds.sem, ds.val)

        with nc.Block() as block:
            @block.tensor
            def _(eng):
                run("pe", eng)

            @block.scalar
            def _(eng):
                run("act", eng)

            @block.vector
            def _(eng):
                run("dve", eng)

            @block.gpsimd
            def _(eng):
                run("pool", eng)

            @block.sync
            def _(eng):
                run("sp", eng)


class Rot:
    def __init__(self, tiles):
        self.tiles = tiles
        self.i = 0

    def next(self):
        t = self.tiles[self.i % len(self.tiles)]
        self.i += 1
        return t


D = 1024
KC = 8
G = 256
H = 4
HD = 64
CW = 31
NMEM = 256
DFF = 4096
PIN = 3076
EPS = 1e-6
O_NMIX, O_NMEM, O_NFFN, O_CW, O_CB, O_LG, O_LB, O_BF, O_HN = 0, 8, 16, 24, 86, 88, 90, 92, 96
NPL = 96 + 256
C_ID, C_TRI, C_ONE, C_HM, C_SCM = 0, 128, 256, 384, 640
NCF = 896
C_NTF, C_NTS, C_M01, C_NTI, C_NEG1 = 896, 1152, 1408, 1664, 1792
NCONST = 1920


def make_consts(TMAX):
    c = np.zeros((128, NCONST), np.float32)
    s = np.arange(128)[:, None]
    t = np.arange(128)[None, :]
    c[:, C_ID:C_ID + 128] = (s == t)
    c[:, C_TRI:C_TRI + 128] = (s <= t)
    c[:, C_ONE:C_ONE + 128] = 1.0
    c[:, C_NTF:C_NTF + 128] = np.where(s <= t, 0.0, -30000.0)
    c[:, C_NTS:C_NTS + 128] = np.where(s < t, 0.0, -30000.0)
    c[:, C_M01:C_M01 + 128] = (s < t)
    c[:, C_M01 + 128:C_M01 + 256] = 1.0
    c[:, C_NTI:C_NTI + 128] = np.where(s >= t, -1.0, 0.0)
    c[:, C_NEG1:C_NEG1 + 128] = -1.0
    hm = (np.arange(64)[:, None] <= np.arange(64)[None, :]).astype(np.float32)
    c[:64, C_HM:C_HM + 256] = np.tile(hm, (1, 4))
    sm = np.ones((256,), np.float32)
    sm[::64] = 0.0
    c[:, C_SCM:C_SCM + 256] = sm[None, :]
    return c


FFN_WIDE = True
KEEP_WARM = 2


def build(cfg):
    SEQ, TP, DEPTH, PAST, DS, NS = cfg["SEQ"], cfg["T"], cfg["DEPTH"], cfg["PAST"], cfg["DS"], cfg["NS"]
    NKB = max(SEQ // 128, PAST // 128 + 1)
    nc = bass.Bass("TRN2", target_bir_lowering=False)

    def din(name, shape, dt=F32):
        return nc.dram_tensor(name, list(shape), dt, kind="ExternalInput").ap()

    def dout(name, shape):
        return nc.dram_tensor(name, list(shape), F32, kind="ExternalOutput").ap()

    def dint(name, shape, dt):
        return nc.dram_tensor(name, list(shape), dt, kind="Internal").ap()

    I = {}
    I["x_prompt"] = din("x_prompt", [SEQ, D])
    I["x_sample"] = din("x_sample", [NS, DS, D])
    I["mem_prompt"] = din("mem_prompt", [NMEM, D])
    I["cache_conv"] = din("cache_conv", [DEPTH, NS, CW - 1, G])
    for n in ("cache_fox_k", "cache_fox_v", "cache_sb_k", "cache_sb_v"):
        I[n] = din(n, [DEPTH, NS, PAST, G])
    I["cache_fox_logf"] = din("cache_fox_logf", [DEPTH, NS, PAST, H])
    I["state_hgrn"] = din("state_hgrn", [DEPTH, NS, H, HD, HD])
    I["cache_mem_k"] = din("cache_mem_k", [DEPTH, NS, NMEM, D])
    I["cache_mem_v"] = din("cache_mem_v", [DEPTH, NS, NMEM, D])
    WSH = {"w_in": (D, PIN), "w_out": (D, D), "w_mq": (D, D), "w_mk": (D, D), "w_mv": (D, D), "w_mo": (D, D),
           "w_up": (D, DFF), "w_down": (DFF, D)}
    Wf = {n: din(n, [DEPTH, s[0], s[1]]) for n, s in WSH.items()}
    SLABS = {"w_in": [(0, 512), (512, 512), (768, 516), (1284, 512), (1796, 512), (2308, 512), (2564, 512)],
             "w_up": [(i * 512, 512) for i in range(8)], "w_down": [(i * 128, 128) for i in range(8)]}
    for n_ in ("w_out", "w_mq", "w_mk", "w_mv", "w_mo"):
        SLABS[n_] = [(0, 512), (512, 512)]
    Wb = {n: dint(n + "_bf", [DEPTH, len(SLABS[n]), 128, 4128], BF16) for n in WSH}
    pl_d = din("pl", [DEPTH, 128, NPL])
    gl_d = din("gl", [128, KC + DEPTH * H])
    const_d = din("consts", [128, NCONST])
    O = {}
    O["y_prompt"] = dout("y_prompt", [SEQ, D])
    O["y_sample"] = dout("y_sample", [NS, DS, D])
    O["conv_p"] = dout("conv_p", [DEPTH, CW - 1, G])
    O["fox_k_p"] = dout("fox_k_p", [DEPTH, SEQ, G])
    O["fox_v_p"] = dout("fox_v_p", [DEPTH, SEQ, G])
    O["fox_logf_p"] = dout("fox_logf_p", [DEPTH, SEQ, H])
    O["hgrn_p"] = dout("hgrn_p", [DEPTH, H, HD, HD])
    O["sb_k_p"] = dout("sb_k_p", [DEPTH, SEQ, G])
    O["sb_v_p"] = dout("sb_v_p", [DEPTH, SEQ, G])
    O["mem_k_p"] = dout("mem_k_p", [DEPTH, NMEM, D])
    O["mem_v_p"] = dout("mem_v_p", [DEPTH, NMEM, D])
    O["conv_s"] = dout("conv_s", [DEPTH, NS, CW - 1, G])
    O["fox_k_s"] = dout("fox_k_s", [DEPTH, NS, DS, G])
    O["fox_v_s"] = dout("fox_v_s", [DEPTH, NS, DS, G])
    O["fox_logf_s"] = dout("fox_logf_s", [DEPTH, NS, DS, H])
    O["hgrn_s"] = dout("hgrn_s", [DEPTH, NS, H, HD, HD])
    O["sb_k_s"] = dout("sb_k_s", [DEPTH, NS, DS, G])
    O["sb_v_s"] = dout("sb_v_s", [DEPTH, NS, DS, G])
    CC = {n: dint(n + "_bf", [DEPTH, NS, PAST, G], BF16) for n in ("cache_fox_k", "cache_fox_v", "cache_sb_k", "cache_sb_v")}
    CC["cache_mem_k"] = dint("cache_mem_k_bf", [DEPTH, NS, NMEM, D], BF16)
    CC["cache_mem_v"] = dint("cache_mem_v_bf", [DEPTH, NS, NMEM, D], BF16)
    LTOT = SEQ + NS * DS
    xs_d = dint("xs", [D, LTOT], F32)

    st = ExitStack()
    with st:
        S = Sched(nc, st)
        TM = max(TP, DS)
        cf = S.sb("cf", [128, NCF], F32)
        cb = S.sb("cb", [128, NCONST], BF16)
        d_c = S.dsem("d_cf")
        d_cb = S.dsem("d_cb")
        d_gl = S.dsem("d_gl")
        d_pls = S.dsem("d_pls")
        S.dma("sp", cf[:], const_d[:, 0:NCF], d_c, writes=[cf])
        S.dma("pool", cb[:], const_d, d_cb, writes=[cb])
        gl = S.sb("gl", [128, KC + DEPTH * H], F32)
        S.dma("sp", gl[:], gl_d, d_gl, writes=[gl])
        pls = S.sb("pls", [128, DEPTH, NPL], F32)
        for l in range(DEPTH):
            S.dma("sp", pls[:, l, :], pl_d[l], d_pls, writes=[pls])
        zer = S.sb("zer", [128, 512], BF16)
        S.op("dve", lambda e: e.memset(zer[:], 0.0), writes=[zer])
        eps_t = S.sb("eps_t", [128, 1], F32)
        S.op("dve", lambda e: e.memset(eps_t[:], EPS), writes=[eps_t])
        one_t = S.sb("one_t", [128, 1], F32)
        S.op("dve", lambda e: e.memset(one_t[:], 1.0), writes=[one_t])

        S.mark("cast")
        d_w = {}
        for l in range(DEPTH):
            for n, (r, ccols) in WSH.items():
                d_w[(l, n)] = S.dsem(f"d_w{l}_{n}")
                kcn_ = r // 128
                for idx, (c0, ncl) in enumerate(SLABS[n]):
                    S.dma("pool", Wb[n][l, idx, :, 0:kcn_ * ncl].rearrange("p (k n) -> p k n", k=kcn_),
                          Wf[n][l, :, c0:c0 + ncl].rearrange("(k p) n -> p k n", p=128), d_w[(l, n)], writes=[("wb", l, n)])

        d_cc = S.dsem("d_cc")
        for l in range(DEPTH):
            for b_ in range(NS):
                for n in ("cache_fox_k", "cache_fox_v", "cache_sb_k", "cache_sb_v"):
                    for r0 in range(0, PAST, 1024):
                        r1 = min(PAST, r0 + 1024)
                        S.dma("pool", CC[n][l, b_, r0:r1, :], I[n][l, b_, r0:r1, :], d_cc, writes=[("cc",)])
                for n in ("cache_mem_k", "cache_mem_v"):
                    S.dma("pool", CC[n][l, b_], I[n][l, b_], d_cc, writes=[("cc",)])
        S.mark("lb")
        lbr = gl[0:64, KC:KC + DEPTH * H]
        hl_e = S.sb("hl_e", [64, DEPTH * H], F32)
        hl_m = S.sb("hl_m", [64, H], F32)
        hl_s = S.sb("hl_s", [64, H], F32)
        lbt = S.sb("lbt", [64, DEPTH * H], F32)
        omlt = S.sb("omlt", [64, DEPTH * H], F32)
        nomlt = S.sb("nomlt", [64, DEPTH * H], F32)
        hl_all = [gl, hl_e, hl_m, hl_s, lbt, omlt, nomlt]

        def sl(l):
            return slice(l * H, (l + 1) * H)
        S.op("dve", lambda e: e.tensor_copy(out=hl_m[:], in_=gl[0:64, KC:KC + H]), reads=hl_all, writes=hl_all)
        for l in range(1, DEPTH):
            S.op("dve", lambda e, l=l: e.tensor_tensor(out=hl_m[:], in0=hl_m[:], in1=gl[0:64, KC + l * H:KC + (l + 1) * H], op=ALU.max), reads=hl_all, writes=hl_all)
        for l in range(DEPTH):
            S.op("dve", lambda e, l=l: e.tensor_tensor(out=hl_e[:, sl(l)], in0=gl[0:64, KC + l * H:KC + (l + 1) * H], in1=hl_m[:], op=ALU.subtract), reads=hl_all, writes=hl_all)
        S.op("act", lambda e: e.activation(out=hl_e[:], in_=hl_e[:], func=AF.Exp), reads=hl_all, writes=hl_all)
        S.op("dve", lambda e: e.tensor_copy(out=hl_s[:], in_=hl_e[:, sl(0)]), reads=hl_all, writes=hl_all)
        for l in range(1, DEPTH):
            S.op("dve", lambda e, l=l: e.tensor_tensor(out=hl_s[:], in0=hl_s[:], in1=hl_e[:, sl(l)], op=ALU.add), reads=hl_all, writes=hl_all)
        S.op("dve", lambda e: e.reciprocal(out=hl_s[:], in_=hl_s[:]), reads=hl_all, writes=hl_all)
        for l in range(DEPTH):
            S.op("dve", lambda e, l=l: e.tensor_tensor(out=hl_e[:, sl(l)], in0=hl_e[:, sl(l)], in1=hl_s[:], op=ALU.mult), reads=hl_all, writes=hl_all)
        S.op("dve", lambda e: e.memset(lbt[:, sl(0)], 0.0), reads=hl_all, writes=hl_all)
        for l in range(1, DEPTH):
            S.op("dve", lambda e, l=l: e.tensor_tensor(out=lbt[:, sl(l)], in0=lbt[:, sl(l - 1)], in1=hl_e[:, sl(l)], op=ALU.add), reads=hl_all, writes=hl_all)
        S.op("dve", lambda e: e.tensor_scalar(out=lbt[:], in0=lbt[:], scalar1=0.0, scalar2=1.0 - 1e-6, op0=ALU.max, op1=ALU.min), reads=hl_all, writes=hl_all)
        S.op("dve", lambda e: e.tensor_scalar(out=omlt[:], in0=lbt[:], scalar1=-1.0, scalar2=1.0, op0=ALU.mult, op1=ALU.add), reads=hl_all, writes=hl_all)
        S.op("dve", lambda e: e.tensor_scalar(out=nomlt[:], in0=omlt[:], scalar1=-1.0, scalar2=None, op0=ALU.mult), reads=hl_all, writes=hl_all)

        nK = max(2 * NKB * 128, 8192)
        nFV = max(NKB * H * 65, 8192)
        nSV = max(NKB * G, 8192)
        arena = S.sb("arena", [128, 2 * nK + nFV + nSV], BF16)
        fKT = T(arena.ap[:, 0:2 * NKB * 128].rearrange("p (c n) -> p c n", c=2), ("sb", "fKT"))
        sKT = T(arena.ap[:, nK:nK + 2 * NKB * 128].rearrange("p (c n) -> p c n", c=2), ("sb", "sKT"))
        fV = T(arena.ap[:, 2 * nK:2 * nK + NKB * H * 65].rearrange("p (n h d) -> p n h d", n=NKB, h=H), ("sb", "fV"))
        sV = T(arena.ap[:, 2 * nK + nFV:2 * nK + nFV + NKB * G].rearrange("p (n g) -> p n g", n=NKB), ("sb", "sV"))
        negC = S.sb("negC", [128, NKB, H], F32)
        Cbase = S.sb("Cbase", [128, H], F32)
        cext = S.sb("cext", [128, 2, 30 + TM], BF16)
        convD = S.sb("convD", [128, 62, 128], BF16)
        hS = S.sb("hS", [64, H, HD], F32)
        hSb = S.sb("hSb", [64, H, HD], BF16)
        mKT = S.sb("mKT", [128, 8, NMEM], BF16)
        mV = S.sb("mV", [128, 2, D], BF16)
        xT = S.sb("xT", [128, KC, TM], F32)
        xn = S.sb("xn", [128, KC, TM], BF16)
        mixT = S.sb("mixT", [128, KC, TM], BF16)
        mqT = mixT
        moT = xn
        assert TM == 256 and PAST // 128 <= 16
        hT = S.sb("hT", [128, DFF // 128, TM], BF16)
        hTflat = hT.ap.rearrange("p a t -> p (a t)")
        ktmp = T(hTflat[:, 0:4096].rearrange("p (n g) -> p n g", g=G), hT.key)
        mtmp = T(hTflat[:, 4096:6144].rearrange("p (n g) -> p n g", g=D), hT.key)
        memT = T(hTflat[:, 4096:6144].rearrange("p (k m) -> p k m", m=NMEM), hT.key)
        xtok = T(hTflat[:, 6144:8192].bitcast(F32), hT.key)
        sq = T(hTflat[:, 0:4096].bitcast(F32).rearrange("p (k t) -> p k t", k=KC), hT.key)
        rstd = S.sb("rstd", [128, TM], F32)
        fQT = S.sb("fQT", [128, 2, TM], BF16)
        sQT = S.sb("sQT", [128, 2, TM], BF16)
        sgT = S.sb("sgT", [128, 2, TM], F32)
        uF = S.sb("uF", [128, 2, 30], F32)
        cY = S.sb("cY", [128, 2, TM], F32)
        cY2 = sgT
        cmean = S.sb("cmean", [128, TM], F32)
        cvar = S.sb("cvar", [128, TM], F32)
        flf = S.sb("flf", [128, H], F32)
        flfp = S.sb("flfp", [128, max(PAST // 128, 1), H], F32)
        Cq = S.sb("Cq", [128, H], F32)
        cs_b = S.sb("cs_b", [128, H], BF16)
        cs_f = S.sb("cs_f", [128, H], F32)
        cs_r = S.sb("cs_r", [128, H], F32)
        Cparts = S.sb("Cparts", [128, H, 3], BF16)
        CTs = S.sb("CTs", [3, H, TM], BF16)
        Pt = Rot([S.sb(f"Pt{i}", [128, TM], BF16) for i in range(4)])
        SPb = Rot([S.sb(f"SPb{i}", [128, TM], BF16) for i in range(4)])
        SPaf = [S.sb(f"SPaf{i}", [128, TM], F32) for i in range(H)]
        SPab = [S.sb(f"SPab{i}", [128, TM], BF16) for i in range(H)]
        rec = S.sb("rec", [128, 4], F32)
        mixtok = S.sb("mixtok", [128, 4, G], BF16)
        rden = cmean
        Pm = S.sb("Pm", [128, 2, TM], BF16)
        rl = Rot([S.sb(f"rl{i}", [128, TM], F32) for i in range(1)])
        Et = rl
        stg = Rot([S.sb(f"stg{i}", [128, 512], F32) for i in range(2)])
        d_stg = [S.dsem(f"d_stg{i}") for i in range(2)]
        stg_i = [0]
        d_xtok = S.dsem("d_xtok")
        d_x = S.dsem("d_x")
        d_xs = S.dsem("d_xs")
        d_misc = S.dsem("d_misc")
        d_oflf = S.dsem("d_oflf")
        d_ouF = S.dsem("d_ouF")
        d_ohS = S.dsem("d_ohS")
        d_cext = S.dsem("d_cext"); d_hS = S.dsem("d_hS"); d_fV = S.dsem("d_fV"); d_sV = S.dsem("d_sV")
        d_ktmp = S.dsem("d_ktmp"); d_flfp = S.dsem("d_flfp"); d_mV = S.dsem("d_mV"); d_mtmp = S.dsem("d_mtmp")
        slab = [S.sb(f"slab{i}", [128, 4128], BF16) for i in range(3)]
        d_slab = [S.dsem(f"d_slab{i}") for i in range(3)]
        slab_i = [0]
        hTf32 = hTflat.bitcast(F32)
        hA = T(hTf32[0:64, 0:1024].rearrange("p (h t) -> p h t", h=H), hT.key)
        hB = T(hTf32[0:64, 1024:2048].rearrange("p (h t) -> p h t", h=H), hT.key)
        hBc = T(hTf32[0:64, 2048:3072].rearrange("p (h t) -> p h t", h=H), hT.key)
        hE = T(hTf32[0:64, 3072:4096].rearrange("p (h t) -> p h t", h=H), hT.key)
        hqt = S.sb("hqt", [64, H, TM], BF16)
        hkt = S.sb("hkt", [64, H, TM], BF16)
        hql = S.sb("hql", [64, H, TM], BF16)
        hkl = S.sb("hkl", [64, H, TM], BF16)
        NCH = max(TM // 64, 1)
        hV = S.sb("hV", [64, NCH, G], BF16)
        hG = S.sb("hG", [64, NCH, G], BF16)
        khT = S.sb("khT", [64, H, 64], BF16)
        kh = S.sb("kh", [64, H, 64], BF16)
        scT = S.sb("scT", [64, H, 64], BF16)
        ofp = S.sb("ofp", [64, H, 64], F32)
        osq = S.sb("osq", [64, H, 64], F32)
        ssq = S.sb("ssq", [64, H], F32)
        hgts = [S.sb(f"hgt{i}", [64, G], BF16) for i in range(2)]
        pA = Rot([S.ps(f"pA{i}", [128, 512], F32) for i in range(2)])
        pS = Rot([S.ps(f"pS{i}", [128, 512], F32) for i in range(2)])
        pO = Rot([S.ps(f"pO{i}", [128, 512], F32) for i in range(2)])
        pM = Rot([S.ps(f"pM{i}", [128, 512], F32) for i in range(2)])
        bank6 = Rot(pS.tiles + pA.tiles + pM.tiles)

        ident_f = cf[:, C_ID:C_ID + 128]
        ident_b = cb[:, C_ID:C_ID + 128]
        ones_f = cf[:, C_ONE:C_ONE + 128]
        ones_b = cb[:, C_ONE:C_ONE + 128]

        def mm(out, lhsT, rhs, start, stop, reads, writes):
            S.op("pe", lambda e: e.matmul(out, lhsT=lhsT, rhs=rhs, start=start, stop=stop), reads=reads, writes=writes)

        def tr(out, in_, ident, reads, writes):
            S.op("pe", lambda e: e.transpose(out, in_, ident), reads=reads, writes=writes)

        def act(out, in_, func, reads, writes, **kw):
            S.op("act", lambda e: e.activation(out=out, in_=in_, func=func, **kw), reads=reads, writes=writes)

        def vcopy(out, in_, reads, writes, eng="dve"):
            S.op(eng, lambda e: e.tensor_copy(out=out, in_=in_), reads=reads, writes=writes)

        def vtt(out, in0, in1, op, reads, writes, eng="dve"):
            S.op(eng, lambda e: e.tensor_tensor(out=out, in0=in0, in1=in1, op=op), reads=reads, writes=writes)

        def vts(out, in0, s1, s2, op0, op1, reads, writes, eng="dve"):
            if op1 is None:
                S.op(eng, lambda e: e.tensor_scalar(out=out, in0=in0, scalar1=s1, scalar2=None, op0=op0), reads=reads, writes=writes)
            else:
                S.op(eng, lambda e: e.tensor_scalar(out=out, in0=in0, scalar1=s1, scalar2=s2, op0=op0, op1=op1), reads=reads, writes=writes)

        def vstt(out, in0, scalar, in1, op0, op1, reads, writes, eng="dve"):
            S.op(eng, lambda e: e.scalar_tensor_tensor(out=out, in0=in0, scalar=scalar, in1=in1, op0=op0, op1=op1), reads=reads, writes=writes)

        sq_default = sq

        def load_slab(wname, l, c0, ncols, kcn=KC):
            i = slab_i[0] % 3
            slab_i[0] += 1
            t = slab[i]
            v = t[:, 0:kcn * ncols].rearrange("p (k n) -> p k n", k=kcn)
            idx = SLABS[wname].index((c0, ncols))
            src = Wb[wname][l, idx, :, 0:kcn * ncols]
            S.dma("sp", t[:, 0:kcn * ncols], src, d_slab[i], reads=[("wb", l, wname)], writes=[t])
            return t, v

        def stage_out(n_rows, width, fill, dsts):
            i = stg_i[0] % 2
            stg_i[0] += 1
            t = stg.tiles[i]
            fill(t)
            for (dap, c0, ncl) in dsts:
                S.dma("sp", dap, t[0:n_rows, c0:c0 + ncl], d_stg[i], reads=[t], writes=[])
            return t

        def rmsnorm_fm(Tn, gcol, out_t, out_dt_is_f32=False, xT=xT, sq=None, rstd=rstd):
            sq = sq_default if sq is None else sq
            S.op("dve", lambda e: e.tensor_tensor(out=sq[:, :, 0:Tn], in0=xT[:, :, 0:Tn], in1=xT[:, :, 0:Tn], op=ALU.mult), reads=[xT], writes=[sq])
            p = pM.next()
            for kc in range(KC):
                mm(p[:, 0:Tn], ones_f, sq[:, kc, 0:Tn], kc == 0, kc == KC - 1, [sq, cf], [p])
            act(rstd[:, 0:Tn], p[:, 0:Tn], AF.Ln, [p, eps_t], [rstd], bias=eps_t[:, 0:1], scale=1.0 / D)
            act(rstd[:, 0:Tn], rstd[:, 0:Tn], AF.Exp, [rstd], [rstd], scale=-0.5)
            for kc in range(KC):
                vstt(out_t[:, kc, 0:Tn], xT[:, kc, 0:Tn], gcol[:, kc:kc + 1], rstd[:, 0:Tn], ALU.mult, ALU.mult, [xT, rstd, pls, gl], [out_t])

        def linear_fm(wname, l, src_t, Tn, nout, evac, kcn=KC, c_base=0):
            if kcn == KC:
                for g0 in range(0, nout, 4):
                    ng = min(4, nout - g0)
                    t, v = load_slab(wname, l, c_base + g0 * 128, ng * 128)
                    for j in range(ng):
                        p = pA.next()
                        for kc in range(KC):
                            mm(p[:, 0:Tn], v[:, kc, j * 128:(j + 1) * 128], src_t[:, kc, 0:Tn], kc == 0, kc == KC - 1, [t, src_t], [p])
                        evac(g0 + j, p)
            else:
                for oc in range(nout):
                    t, v = load_slab(wname, l, c_base + oc * 128, 128, kcn=kcn)
                    p = pA.next()
                    for kc in range(kcn):
                        mm(p[:, 0:Tn], v[:, kc, :], src_t[:, kc, 0:Tn], kc == 0, kc == kcn - 1, [t, src_t], [p])
                    evac(oc, p)

        def add_resid(Tn):
            def ev(oc, p):
                vtt(xT[:, oc, 0:Tn], xT[:, oc, 0:Tn], p[:, 0:Tn], ALU.add, [p, xT], [xT])
            return ev

        def fox_c_block(flf_ap, flf_key, n, blk, want_q, qcol0):
            p = pM.next()
            mm(p[0:n, 0:H], cf[0:n, C_TRI:C_TRI + n], flf_ap, True, True, [cf, flf_key], [p])
            vtt(Cq[0:n, :], p[0:n, 0:H], Cbase[0:n, :], ALU.add, [p, Cbase], [Cq])
            vts(negC[0:n, blk, :], Cq[0:n, :], -1.0, None, ALU.mult, None, [Cq], [negC])
            p2 = pM.next()
            mm(p2[:, 0:H], cf[0:n, C_ONE:C_ONE + 128], flf_ap, True, True, [cf, flf_key], [p2])
            vtt(Cbase[:], Cbase[:], p2[:, 0:H], ALU.add, [p2, Cq], [Cbase])
            if want_q:
                cst = [Cq, cs_b, cs_f, cs_r, Cparts]
                vcopy(cs_b[0:n, :], Cq[0:n, :], cst, cst)
                vcopy(Cparts[0:n, :, 0], cs_b[0:n, :], cst, cst)
                vcopy(cs_f[0:n, :], cs_b[0:n, :], cst, cst)
                vtt(cs_r[0:n, :], Cq[0:n, :], cs_f[0:n, :], ALU.subtract, cst, cst)
                vcopy(cs_b[0:n, :], cs_r[0:n, :], cst, cst)
                vcopy(Cparts[0:n, :, 1], cs_b[0:n, :], cst, cst)
                vcopy(cs_f[0:n, :], cs_b[0:n, :], cst, cst)
                vtt(cs_r[0:n, :], cs_r[0:n, :], cs_f[0:n, :], ALU.subtract, cst, cst)
                vcopy(Cparts[0:n, :, 2], cs_r[0:n, :], cst, cst)
                p3 = pM.next()
                for h in range(H):
                    mm(p3[0:3, h * 128:h * 128 + n], Cparts[0:n, h, :], ident_b[0:n, 0:n], True, True, [Cparts, cb], [p3])
                for h in range(H):
                    vcopy(CTs[0:3, h, qcol0:qcol0 + n], p3[0:3, h * 128:h * 128 + n], [p3], [CTs])

        def layer_tile(sq_, l, ti):
            Tn = sq_["T"]
            SBk = min(128, Tn)
            nsb = Tn // SBk
            nch = Tn // 64
            t0 = ti * Tn
            past = sq_["past"]
            kb0 = (past + t0) // 128
            L = sq_["L"]
            ntiles = L // Tn
            last_tile = ti == ntiles - 1
            xoff = sq_["xoff"] + t0
            PL = pls[:, l, :]
            b = sq_["b"]

            S.mark(f"A{l}_{sq_['name']}_{ti}")
            if l == 0:
                for sbi in range(nsb):
                    S.dma("sp", xtok[0:SBk, :], sq_["x_src"][t0 + sbi * SBk:t0 + (sbi + 1) * SBk, :], d_xtok, writes=[xtok])
                    for g0 in range(0, KC, 4):
                        p = pM.next()
                        for j in range(4):
                            tr(p[:, j * 128:j * 128 + SBk], xtok[0:SBk, (g0 + j) * 128:(g0 + j + 1) * 128], ident_f[0:SBk, 0:SBk], [xtok, cf], [p])
                        for j in range(4):
                            vcopy(xT[:, g0 + j, sbi * SBk:(sbi + 1) * SBk], p[:, j * 128:j * 128 + SBk], [p], [xT])
            else:
                S.dma("sp", xT[:, :, 0:Tn], xs_d[:, xoff:xoff + Tn].rearrange("(k p) t -> p k t", p=128), d_x,
                      reads=[("xs", sq_["name"], ti)], writes=[xT])

            S.mark(f"B{l}_{sq_['name']}_{ti}")
            rmsnorm_fm(Tn, PL[:, O_NMIX:O_NMIX + KC], xn)

            S.mark(f"C{l}_{sq_['name']}_{ti}")
            t, v = load_slab("w_in", l, 0, 512)
            for oc in (2, 3, 0, 1):
                p = pA.next()
                for kc in range(KC):
                    mm(p[:, 0:Tn], v[:, kc, oc * 128:(oc + 1) * 128], xn[:, kc, 0:Tn], kc == 0, kc == KC - 1, [t, xn], [p])
                if oc >= 2:
                    act(sgT[:, oc - 2, 0:Tn], p[:, 0:Tn], AF.Sigmoid, [p], [sgT])
                else:
                    vtt(cext[:, oc, 30:30 + Tn], p[:, 0:Tn], sgT[:, oc, 0:Tn], ALU.mult, [p, sgT], [cext])
                    if last_tile:
                        vtt(uF[:, oc, :], p[:, Tn - 30:Tn], sgT[:, oc, Tn - 30:Tn], ALU.mult, [p, sgT], [uF])
            S.mark(f"C2_{l}_{sq_['name']}_{ti}")
            t, v = load_slab("w_in", l, 512, 512)
            for oc in range(4):
                p = pA.next()
                for kc in range(KC):
                    mm(p[:, 0:Tn], v[:, kc, oc * 128:(oc + 1) * 128], xn[:, kc, 0:Tn], kc == 0, kc == KC - 1, [t, xn], [p])
                if oc < 2:
                    act(fQT[:, oc, 0:Tn], p[:, 0:Tn], AF.Copy, [p], [fQT], scale=0.125)
                else:
                    vcopy(fKT[:, oc - 2, (past + t0):(past + t0) + Tn], p[:, 0:Tn], [p], [fKT])
            S.mark(f"C3_{l}_{sq_['name']}_{ti}")
            t, v = load_slab("w_in", l, 768, 516)
            for sbi in range(nsb):
                blk = kb0 + sbi
                r0 = t0 + sbi * SBk
                p = pA.next()
                p2 = pM.next()
                for kc in range(KC):
                    mm(p[0:SBk, 0:512], xn[:, kc, sbi * SBk:(sbi + 1) * SBk], v[:, kc, 0:512], kc == 0, kc == KC - 1, [t, xn], [p])
                for kc in range(KC):
                    mm(p2[0:SBk, 0:4], xn[:, kc, sbi * SBk:(sbi + 1) * SBk], v[:, kc, 512:516], kc == 0, kc == KC - 1, [t, xn], [p2])

                def fill(tl, p=p):
                    act(tl[0:SBk, 0:512], p[0:SBk, 0:512], AF.Copy, [p], [tl])
                tl = stage_out(SBk, 512, fill, [(sq_["fox_k"][l][r0:r0 + SBk, :], 0, 256), (sq_["fox_v"][l][r0:r0 + SBk, :], 256, 256)])
                vcopy(fV[0:SBk, blk, :, 0:64], tl[0:SBk, 256:512].rearrange("p (h d) -> p h d", h=H), [tl], [fV], eng="pool")
                vtt(flf[0:SBk, :], p2[0:SBk, 0:4], PL[0:SBk, O_BF:O_BF + 4], ALU.add, [p2, pls], [flf])
                act(flf[0:SBk, :], flf[0:SBk, :], AF.Sigmoid, [flf], [flf])
                act(flf[0:SBk, :], flf[0:SBk, :], AF.Ln, [flf], [flf])
                S.dma("sp", sq_["fox_logf"][l][r0:r0 + SBk, :], flf[0:SBk, :], d_oflf, reads=[flf], writes=[])
                fox_c_block(flf[0:SBk, :], flf, SBk, blk, True, sbi * SBk)
            S.mark(f"C4_{l}_{sq_['name']}_{ti}")
            t, v = load_slab("w_in", l, 1284, 512)
            for which in (1,):
                for h in range(H):
                    p = pA.next()
                    for kc in range(KC):
                        mm(p[0:64, 0:Tn], v[:, kc, which * 256 + h * 64:which * 256 + (h + 1) * 64], xn[:, kc, 0:Tn], kc == 0, kc == KC - 1, [t, xn], [p])
                    if which == 0:
                        vtt(hB[:, h, 0:Tn], p[0:64, 0:Tn], hE[:, h, 0:Tn], ALU.mult, [p, hE], [hB])
                    else:
                        act(hA[:, h, 0:Tn], p[0:64, 0:Tn], AF.Sigmoid, [p], [hA])
                if which == 1:
                    for h in range(H):
                        vts(hB[:, h, 0:Tn], hA[:, h, 0:Tn], omlt[:, l * H + h:l * H + h + 1], lbt[:, l * H + h:l * H + h + 1], ALU.mult, ALU.add, [hA, omlt, lbt], [hB])
                    act(hB[:, :, 0:Tn], hB[:, :, 0:Tn], AF.Ln, [hB], [hB])
                    for h in range(H):
                        S.op("dve", lambda e, h=h: e.tensor_tensor_scan(out=hBc[:, h, 0:Tn], data0=cf[0:64, C_SCM:C_SCM + Tn], data1=hB[:, h, 0:Tn], initial=0.0, op0=ALU.mult, op1=ALU.add),
                             reads=[hB, cf], writes=[hBc])
                    vts(hBc[:, :, 0:Tn], hBc[:, :, 0:Tn], -80.0, None, ALU.max, None, [hBc], [hBc])
                    act(hE[:, :, 0:Tn], hBc[:, :, 0:Tn], AF.Exp, [hBc], [hE])
                    act(hB[:, :, 0:Tn], hBc[:, :, 0:Tn], AF.Exp, [hBc], [hB], scale=-1.0)
                    for h in range(H):
                        vts(hA[:, h, 0:Tn], hA[:, h, 0:Tn], nomlt[:, l * H + h:l * H + h + 1], omlt[:, l * H + h:l * H + h + 1], ALU.mult, ALU.add, [hA, omlt, nomlt], [hA])
                    vtt(hA[:, :, 0:Tn], hA[:, :, 0:Tn], hB[:, :, 0:Tn], ALU.mult, [hA, hB], [hA])
                    vcopy(hkt[:, :, 0:Tn], hA[:, :, 0:Tn], [hA], [hkt])
                    vtt(hB[:, :, 0:Tn], hA[:, :, 0:Tn], hkt[:, :, 0:Tn], ALU.subtract, [hA, hkt], [hB])
                    vcopy(hkl[:, :, 0:Tn], hB[:, :, 0:Tn], [hB], [hkl], eng="pool")
                else:
                    vcopy(hqt[:, :, 0:Tn], hB[:, :, 0:Tn], [hB], [hqt])
                    vtt(hA[:, :, 0:Tn], hB[:, :, 0:Tn], hqt[:, :, 0:Tn], ALU.subtract, [hB, hqt], [hA])
                    vcopy(hql[:, :, 0:Tn], hA[:, :, 0:Tn], [hA], [hql], eng="pool")
            S.mark(f"C5_{l}_{sq_['name']}_{ti}")
            t, v = load_slab("w_in", l, 1796, 512)
            for ch in range(nch):
                p = pA.next()
                for kc in range(KC):
                    mm(p[0:64, 0:512], xn[:, kc, ch * 64:(ch + 1) * 64], v[:, kc, 0:512], kc == 0, kc == KC - 1, [t, xn], [p])
                vcopy(hV[:, ch, :], p[0:64, 0:256], [p], [hV])
                act(hG[:, ch, :], p[0:64, 256:512], AF.Silu, [p], [hG])
                vtt(hG[:, ch, :], hG[:, ch, :], PL[0:64, O_HN:O_HN + 256], ALU.mult, [hG, pls], [hG])
            S.mark(f"C6_{l}_{sq_['name']}_{ti}")
            t, v = load_slab("w_in", l, 2308, 512)
            for oc in range(4):
                p = pA.next()
                for kc in range(KC):
                    mm(p[:, 0:Tn], v[:, kc, oc * 128:(oc + 1) * 128], xn[:, kc, 0:Tn], kc == 0, kc == KC - 1, [t, xn], [p])
                if oc < 2:
                    act(sQT[:, oc, 0:Tn], p[:, 0:Tn], AF.Copy, [p], [sQT], scale=0.125)
                else:
                    vcopy(sKT[:, oc - 2, (past + t0):(past + t0) + Tn], p[:, 0:Tn], [p], [sKT])
            S.mark(f"C7_{l}_{sq_['name']}_{ti}")
            t, v = load_slab("w_in", l, 2564, 512)
            for sbi in range(nsb):
                blk = kb0 + sbi
                r0 = t0 + sbi * SBk
                p = pA.next()
                for kc in range(KC):
                    mm(p[0:SBk, 0:512], xn[:, kc, sbi * SBk:(sbi + 1) * SBk], v[:, kc, 0:512], kc == 0, kc == KC - 1, [t, xn], [p])

                def fill(tl, p=p):
                    act(tl[0:SBk, 0:512], p[0:SBk, 0:512], AF.Copy, [p], [tl])
                tl = stage_out(SBk, 512, fill, [(sq_["sb_k"][l][r0:r0 + SBk, :], 0, 256), (sq_["sb_v"][l][r0:r0 + SBk, :], 256, 256)])
                vcopy(sV[0:SBk, blk, :], tl[0:SBk, 256:512], [tl], [sV], eng="pool")

            t, v = load_slab("w_in", l, 1284, 512)
            for which in (0,):
                for h in range(H):
                    p = pA.next()
                    for kc in range(KC):
                        mm(p[0:64, 0:Tn], v[:, kc, which * 256 + h * 64:which * 256 + (h + 1) * 64], xn[:, kc, 0:Tn], kc == 0, kc == KC - 1, [t, xn], [p])
                    if which == 0:
                        vtt(hB[:, h, 0:Tn], p[0:64, 0:Tn], hE[:, h, 0:Tn], ALU.mult, [p, hE], [hB])
                    else:
                        act(hA[:, h, 0:Tn], p[0:64, 0:Tn], AF.Sigmoid, [p], [hA])
                if which == 1:
                    for h in range(H):
                        vts(hB[:, h, 0:Tn], hA[:, h, 0:Tn], omlt[:, l * H + h:l * H + h + 1], lbt[:, l * H + h:l * H + h + 1], ALU.mult, ALU.add, [hA, omlt, lbt], [hB])
                    act(hB[:, :, 0:Tn], hB[:, :, 0:Tn], AF.Ln, [hB], [hB])
                    for h in range(H):
                        S.op("dve", lambda e, h=h: e.tensor_tensor_scan(out=hBc[:, h, 0:Tn], data0=cf[0:64, C_SCM:C_SCM + Tn], data1=hB[:, h, 0:Tn], initial=0.0, op0=ALU.mult, op1=ALU.add),
                             reads=[hB, cf], writes=[hBc])
                    vts(hBc[:, :, 0:Tn], hBc[:, :, 0:Tn], -80.0, None, ALU.max, None, [hBc], [hBc])
                    act(hE[:, :, 0:Tn], hBc[:, :, 0:Tn], AF.Exp, [hBc], [hE])
                    act(hB[:, :, 0:Tn], hBc[:, :, 0:Tn], AF.Exp, [hBc], [hB], scale=-1.0)
                    for h in range(H):
                        vts(hA[:, h, 0:Tn], hA[:, h, 0:Tn], nomlt[:, l * H + h:l * H + h + 1], omlt[:, l * H + h:l * H + h + 1], ALU.mult, ALU.add, [hA, omlt, nomlt], [hA])
                    vtt(hA[:, :, 0:Tn], hA[:, :, 0:Tn], hB[:, :, 0:Tn], ALU.mult, [hA, hB], [hA])
                    vcopy(hkt[:, :, 0:Tn], hA[:, :, 0:Tn], [hA], [hkt])
                    vtt(hB[:, :, 0:Tn], hA[:, :, 0:Tn], hkt[:, :, 0:Tn], ALU.subtract, [hA, hkt], [hB])
                    vcopy(hkl[:, :, 0:Tn], hB[:, :, 0:Tn], [hB], [hkl], eng="pool")
                else:
                    vcopy(hqt[:, :, 0:Tn], hB[:, :, 0:Tn], [hB], [hqt])
                    vtt(hA[:, :, 0:Tn], hB[:, :, 0:Tn], hqt[:, :, 0:Tn], ALU.subtract, [hB, hqt], [hA])
                    vcopy(hql[:, :, 0:Tn], hA[:, :, 0:Tn], [hA], [hql], eng="pool")
            S.mark(f"D{l}_{sq_['name']}_{ti}")
            for chn in range(2):
                p = pA.next()
                for k in range(CW):
                    mm(p[:, 0:Tn], convD[:, chn * CW + k, :], cext[:, chn, k:k + Tn], k == 0, k == CW - 1, [convD, cext], [p])
                vts(cY[:, chn, 0:Tn], p[:, 0:Tn], PL[:, O_CB + chn:O_CB + chn + 1], None, ALU.add, None, [p, pls], [cY])
            vtt(cY2[:, :, 0:Tn], cY[:, :, 0:Tn], cY[:, :, 0:Tn], ALU.mult, [cY], [cY2])
            p = pM.next()
            for chn in range(2):
                mm(p[:, 0:Tn], ones_f, cY[:, chn, 0:Tn], chn == 0, chn == 1, [cY, cf], [p])
            p2 = pM.next()
            for chn in range(2):
                mm(p2[:, 0:Tn], ones_f, cY2[:, chn, 0:Tn], chn == 0, chn == 1, [cY2, cf], [p2])
            vts(cmean[:, 0:Tn], p[:, 0:Tn], 1.0 / G, None, ALU.mult, None, [p], [cmean])
            vtt(cvar[:, 0:Tn], cmean[:, 0:Tn], cmean[:, 0:Tn], ALU.mult, [cmean], [cvar])
            vstt(cvar[:, 0:Tn], p2[:, 0:Tn], 1.0 / G, cvar[:, 0:Tn], ALU.mult, ALU.subtract, [p2, cvar], [cvar])
            act(cvar[:, 0:Tn], cvar[:, 0:Tn], AF.Ln, [cvar, eps_t], [cvar], bias=eps_t[:, 0:1], scale=1.0)
            act(cvar[:, 0:Tn], cvar[:, 0:Tn], AF.Exp, [cvar], [cvar], scale=-0.5)
            for chn in range(2):
                vtt(cY[:, chn, 0:Tn], cY[:, chn, 0:Tn], cmean[:, 0:Tn], ALU.subtract, [cY, cmean], [cY])
                vtt(cY[:, chn, 0:Tn], cY[:, chn, 0:Tn], cvar[:, 0:Tn], ALU.mult, [cY, cvar], [cY])
                vts(cY[:, chn, 0:Tn], cY[:, chn, 0:Tn], PL[:, O_LG + chn:O_LG + chn + 1], PL[:, O_LB + chn:O_LB + chn + 1], ALU.mult, ALU.add, [cY, pls], [cY])
                act(mixT[:, chn, 0:Tn], cY[:, chn, 0:Tn], AF.Silu, [cY], [mixT])
            if last_tile:
                for cc_ in range(2):
                    S.dma("sp", sq_["conv"][l][:, cc_ * 128:(cc_ + 1) * 128].rearrange("r p -> p r"), uF[:, cc_, :], d_ouF, reads=[uF], writes=[],
                          allow_slow_non_contiguous=True)
            else:
                vcopy(cext[:, :, 0:30], cext[:, :, Tn:Tn + 30], [cext], [cext], eng="pool")

            S.mark(f"F{l}_{sq_['name']}_{ti}")
            past_blocks = list(range(kb0))
            diag = [(kb0 + j, SBk, j * SBk) for j in range(nsb)]
            blocks = [(bk, 128, 0, False) for bk in past_blocks] + [(bk, nk, c0, True) for (bk, nk, c0) in diag]
            Ob = pO.tiles
            for ob in Ob:
                mm(ob[0:SBk, 0:2 * nsb * 65], zer[0:128, 0:SBk], zer[0:128, 0:2 * nsb * 65], True, False, [zer], [ob])
            nb = len(blocks)

            def fox_scores(bi):
                bk, nk, c0, isd = blocks[bi]
                res = []
                for h in range(H):
                    c = h // 2
                    pr = slice(64 * (h % 2), 64 * (h % 2) + 64)
                    Sx = bank6.next()
                    mm(Sx[0:nk, c0:Tn], fKT[pr, c, bk * 128:bk * 128 + nk], fQT[pr, c, c0:Tn], True, False, [fKT, fQT], [Sx])
                    mm(Sx[0:nk, c0:Tn], ones_b[0:3, 0:nk], CTs[0:3, h, c0:Tn], False, not isd, [cb, CTs], [Sx])
                    if isd:
                        mm(Sx[0:nk, c0:Tn], ident_b[0:nk, 0:nk], cb[0:nk, C_NTF:C_NTF + Tn - c0], False, True, [cb], [Sx])
                    P_ = Pt.next()
                    act(P_[0:nk, c0:Tn], Sx[0:nk, c0:Tn], AF.Exp, [Sx, negC], [P_], bias=negC[0:nk, bk, h:h + 1], scale=1.0)
                    res.append(P_)
                return res

            def fox_pv(bi, Ps):
                bk, nk, c0, isd = blocks[bi]
                for h in range(H):
                    O_ = Ob[h // 2]
                    P_ = Ps[h]
                    for jq in range(nsb):
                        if jq * SBk < c0:
                            continue
                        o0 = ((h % 2) * nsb + jq) * 65
                        lastq = (bi == nb - 1) and (jq == nsb - 1) and (h % 2 == 1)
                        mm(O_[0:SBk, o0:o0 + 65], P_[0:nk, jq * SBk:(jq + 1) * SBk], fV[0:nk, bk, h, :], False, lastq, [P_, fV], [O_])
            for bi in range(nb):
                fox_pv(bi, fox_scores(bi))
            for h in range(H):
                O_ = Ob[h // 2]
                for jq in range(nsb):
                    o0 = ((h % 2) * nsb + jq) * 65
                    S.op("dve", lambda e, jq=jq, O_=O_, o0=o0: e.reciprocal(out=rec[0:SBk, jq:jq + 1], in_=O_[0:SBk, o0 + 64:o0 + 65]), reads=[O_], writes=[rec])
                    vts(mixtok[0:SBk, jq, h * 64:(h + 1) * 64], O_[0:SBk, o0:o0 + 64], rec[0:SBk, jq:jq + 1], None, ALU.mult, None, [O_, rec], [mixtok])
            for jq in range(nsb):
                pt = pM.next()
                ptb = pt[:].bitcast(BF16)
                for cc in range(2):
                    tr(ptb[:, cc * 128:cc * 128 + SBk], mixtok[0:SBk, jq, cc * 128:(cc + 1) * 128], ident_b[0:SBk, 0:SBk], [mixtok, cb], [pt])
                for cc in range(2):
                    vcopy(mixT[:, 2 + cc, jq * SBk:(jq + 1) * SBk], ptb[:, cc * 128:cc * 128 + SBk], [pt], [mixT])

            S.mark(f"G{l}_{sq_['name']}_{ti}")
            blocks = [(bk, nk, c0, True) for (bk, nk, c0) in reversed(diag)] + [(bk, 128, 0, False) for bk in reversed(past_blocks)]
            O_ = Ob[0]
            mm(O_[0:SBk, 0:H * nsb * 64], zer[0:128, 0:SBk], zer[0:128, 0:H * nsb * 64], True, False, [zer], [O_])
            for h in range(H):
                S.op("pool", lambda e, h=h: e.memset(SPaf[h][:], 0.0), writes=[SPaf[h]])
                S.op("pool", lambda e, h=h: e.memset(SPab[h][:], 0.0), writes=[SPab[h]])
            nb = len(blocks)

            def sb_stage1(bi):
                bk, nk, c0, isd = blocks[bi]
                res = []
                for h in range(H):
                    c = h // 2
                    pr = slice(64 * (h % 2), 64 * (h % 2) + 64)
                    Z = bank6.next()
                    mm(Z[0:nk, c0:Tn], sKT[pr, c, bk * 128:bk * 128 + nk], sQT[pr, c, c0:Tn], True, True, [sKT, sQT], [Z])
                    if KEEP_WARM:
                        for _ in range(KEEP_WARM):
                            mm(Ob[1][:, 0:512], zer[0:128, 0:128], zer[0:128, 0:512], True, True, [zer], [Ob[1]])
                    E_ = Et.next()
                    act(E_[0:nk, c0:Tn], Z[0:nk, c0:Tn], AF.Exp, [Z], [E_])
                    sp_ = SPb.next()
                    act(sp_[0:nk, c0:Tn], E_[0:nk, c0:Tn], AF.Ln, [E_, one_t], [sp_], bias=one_t[0:nk, 0:1], scale=1.0)
                    if isd:
                        vtt(sp_[0:nk, c0:Tn], sp_[0:nk, c0:Tn], cb[0:nk, C_M01:C_M01 + Tn - c0], ALU.mult, [sp_, cb], [sp_])
                    res.append(sp_)
                return res

            def sb_stage23(bi, sps):
                bk, nk, c0, isd = blocks[bi]
                have_acc = bi > 0
                As = []
                for h in range(H):
                    c = h // 2
                    pr = slice(64 * (h % 2), 64 * (h % 2) + 64)
                    sp_ = sps[h]
                    Lg = bank6.next()
                    mm(Lg[0:nk, c0:Tn], sKT[pr, c, bk * 128:bk * 128 + nk], sQT[pr, c, c0:Tn], True, False, [sKT, sQT], [Lg])
                    mm(Lg[0:nk, c0:Tn], cb[0:nk, C_NTI:C_NTI + nk], sp_[0:nk, c0:Tn], False, not (have_acc or isd), [cb, sp_], [Lg])
                    if have_acc:
                        mm(Lg[0:nk, c0:Tn], cb[0:128, C_NEG1:C_NEG1 + nk], SPab[h][0:128, c0:Tn], False, not isd, [cb, SPab[h]], [Lg])
                    if isd:
                        mm(Lg[0:nk, c0:Tn], ident_b[0:nk, 0:nk], cb[0:nk, C_NTS:C_NTS + Tn - c0], False, True, [cb], [Lg])
                    A_ = Pt.next()
                    act(A_[0:nk, c0:Tn], Lg[0:nk, c0:Tn], AF.Exp, [Lg], [A_])
                    As.append(A_)
                    if bi != nb - 1:
                        vtt(SPaf[h][0:nk, c0:Tn], SPaf[h][0:nk, c0:Tn], sp_[0:nk, c0:Tn], ALU.add, [SPaf[h], sp_], [SPaf[h]])
                        vcopy(SPab[h][0:nk, c0:Tn], SPaf[h][0:nk, c0:Tn], [SPaf[h]], [SPab[h]])
                return As

            def sb_stage3(bi, As):
                bk, nk, c0, isd = blocks[bi]
                for h in range(H):
                    A_ = As[h]
                    for jq in range(nsb):
                        if jq * SBk < c0:
                            continue
                        o0 = (h * nsb + jq) * 64
                        mm(O_[0:SBk, o0:o0 + 64], A_[0:nk, jq * SBk:(jq + 1) * SBk], sV[0:nk, bk, h * 64:(h + 1) * 64], False,
                           (bi == nb - 1) and (jq == nsb - 1) and (h == H - 1), [A_, sV], [O_])
            cur = sb_stage1(0)
            for bi in range(nb):
                As = sb_stage23(bi, cur)
                cur = sb_stage1(bi + 1) if bi + 1 < nb else None
                sb_stage3(bi, As)
            for jq in range(nsb):
                vcopy(mixtok[0:SBk, jq, :].rearrange("p (h d) -> p h d", h=H),
                      O_[0:SBk, 0:H * nsb * 64].rearrange("p (h j d) -> p h j d", h=H, j=nsb)[:, :, jq, :], [O_], [mixtok])
            for jq in range(nsb):
                pt = pM.next()
                ptb = pt[:].bitcast(BF16)
                for cc in range(2):
                    tr(ptb[:, cc * 128:cc * 128 + SBk], mixtok[0:SBk, jq, cc * 128:(cc + 1) * 128], ident_b[0:SBk, 0:SBk], [mixtok, cb], [pt])
                for cc in range(2):
                    vcopy(mixT[:, 6 + cc, jq * SBk:(jq + 1) * SBk], ptb[:, cc * 128:cc * 128 + SBk], [pt], [mixT])

            S.mark(f"H{l}_{sq_['name']}_{ti}")
            pend_tr = []

            def emit_hg_tr(hgt_, cs_):
                pt = pM.next()
                ptb = pt[:].bitcast(BF16)
                for cc in range(2):
                    tr(ptb[:, cc * 64:(cc + 1) * 64], hgt_[:, cc * 128:(cc + 1) * 128], ident_b[0:64, 0:64], [hgt_, cb], [pt])
                for cc in range(2):
                    vcopy(mixT[:, 4 + cc, cs_], ptb[:, cc * 64:(cc + 1) * 64], [pt], [mixT])
            for ch in range(nch):
                cs = slice(ch * 64, (ch + 1) * 64)
                cl = ch * 64 + 63
                hgt = hgts[ch % 2]
                for h in range(H):
                    vts(khT[:, h, :], hkt[:, h, cs], hE[:, h, cl:cl + 1], None, ALU.mult, None, [hkt, hE], [khT])
                pt = pM.next()
                ptb = pt[:].bitcast(BF16)
                for h in range(H):
                    tr(ptb[0:64, h * 64:(h + 1) * 64], khT[:, h, :], ident_b[0:64, 0:64], [khT, cb], [pt])
                vcopy(kh[:].rearrange("p h k -> p (h k)"), ptb[0:64, 0:256], [pt], [kh])
                psc = pM.next()
                for h in range(H):
                    mm(psc[0:64, h * 64:(h + 1) * 64], hkt[:, h, cs], hqt[:, h, cs], True, False, [hkt, hqt], [psc])
                    mm(psc[0:64, h * 64:(h + 1) * 64], hkl[:, h, cs], hqt[:, h, cs], False, False, [hkl, hqt], [psc])
                    mm(psc[0:64, h * 64:(h + 1) * 64], hkt[:, h, cs], hql[:, h, cs], False, True, [hkt, hql], [psc])
                vtt(scT[:].rearrange("p h k -> p (h k)"), psc[0:64, 0:256], cf[0:64, C_HM:C_HM + 256], ALU.mult, [psc, cf], [scT])
                po = pO.next()
                for h in range(H):
                    mm(po[0:64, h * 64:(h + 1) * 64], scT[:, h, :], hV[:, ch, h * 64:(h + 1) * 64], True, False, [scT, hV], [po])
                    mm(po[0:64, h * 64:(h + 1) * 64], hqt[:, h, cs], hSb[:, h, :], False, True, [hqt, hSb], [po])
                psu = pM.next()
                for h in range(H):
                    mm(psu[0:64, h * 64:(h + 1) * 64], kh[:, h, :], hV[:, ch, h * 64:(h + 1) * 64], True, True, [kh, hV], [psu])
                for h in range(H):
                    vstt(hS[:, h, :], hS[:, h, :], hE[:, h, cl:cl + 1], psu[0:64, h * 64:(h + 1) * 64], ALU.mult, ALU.add, [hS, hE, psu], [hS])
                vcopy(hSb[:], hS[:], [hS], [hSb])
                act(ofp[:].rearrange("p h k -> p (h k)"), po[0:64, 0:256], AF.Copy, [po], [ofp])
                vtt(osq[:], ofp[:], ofp[:], ALU.mult, [ofp], [osq])
                S.op("dve", lambda e: e.tensor_reduce(out=ssq[:], in_=osq[:], axis=AX.X, op=ALU.add), reads=[osq], writes=[ssq])
                act(ssq[:], ssq[:], AF.Ln, [ssq, eps_t], [ssq], bias=eps_t[0:64, 0:1], scale=1.0 / HD)
                act(ssq[:], ssq[:], AF.Exp, [ssq], [ssq], scale=-0.5)
                for h in range(H):
                    vstt(hgt[:, h * 64:(h + 1) * 64], ofp[:, h, :], ssq[:, h:h + 1], hG[:, ch, h * 64:(h + 1) * 64], ALU.mult, ALU.mult, [ofp, ssq, hG], [hgt])
                pend_tr.append((hgt, cs))
                if len(pend_tr) > 1:
                    emit_hg_tr(*pend_tr.pop(0))
            while pend_tr:
                emit_hg_tr(*pend_tr.pop(0))
            if last_tile:
                S.dma("sp", sq_["hgrn"][l].rearrange("h k v -> k h v"), hS[:], d_ohS, reads=[hS], writes=[])

            S.mark(f"I{l}_{sq_['name']}_{ti}")
            linear_fm("w_out", l, mixT, Tn, 8, add_resid(Tn))
            S.mark(f"J{l}_{sq_['name']}_{ti}")
            rmsnorm_fm(Tn, PL[:, O_NMEM:O_NMEM + KC], xn)

            def ev_q(oc, p):
                act(mqT[:, oc, 0:Tn], p[:, 0:Tn], AF.Copy, [p], [mqT], scale=1.0 / 16.0)
            linear_fm("w_mq", l, xn, Tn, 8, ev_q)
            for h in range(H):
                for mb in range(2):
                    Sx = pS.next()
                    for dc in range(2):
                        mm(Sx[:, 0:Tn], mKT[:, h * 2 + dc, mb * 128:(mb + 1) * 128], mqT[:, h * 2 + dc, 0:Tn], dc == 0, dc == 1, [mKT, mqT], [Sx])
                    act(Pm[:, mb, 0:Tn], Sx[:, 0:Tn], AF.Exp, [Sx], [Pm])
                pd = pM.next()
                for mb in range(2):
                    mm(pd[:, 0:Tn], ones_b, Pm[:, mb, 0:Tn], mb == 0, mb == 1, [cb, Pm], [pd])
                S.op("dve", lambda e, pd=pd: e.reciprocal(out=rden[:, 0:Tn], in_=pd[:, 0:Tn]), reads=[pd], writes=[rden])
                for dc in range(2):
                    po = pO.next()
                    for mb in range(2):
                        mm(po[:, 0:Tn], mV[:, mb, h * 256 + dc * 128:h * 256 + (dc + 1) * 128], Pm[:, mb, 0:Tn], mb == 0, mb == 1, [mV, Pm], [po])
                    vtt(moT[:, h * 2 + dc, 0:Tn], po[:, 0:Tn], rden[:, 0:Tn], ALU.mult, [po, rden], [moT])
            linear_fm("w_mo", l, moT, Tn, 8, add_resid(Tn))
            S.mark(f"K{l}_{sq_['name']}_{ti}")
            S.dma("sp", xs_d[:, xoff:xoff + Tn].rearrange("(k p) t -> p k t", p=128), xT[:, :, 0:Tn], d_xs,
                  reads=[xT], writes=[("xs", sq_["name"], ti)])

        TG = 512
        if FFN_WIDE:
            hT2 = T(arena.ap[:, 0:2 * nK][:, 0:32 * TG].rearrange("p (a t) -> p a t", a=DFF // 128), [fKT.key, sKT.key])
            xT2 = T(arena.ap[:, 2 * nK:2 * nK + KC * TG * 2].bitcast(F32).rearrange("p (k t) -> p k t", k=KC), [fV.key])
            sq2 = T(arena.ap[:, 2 * nK + nFV:2 * nK + nFV + KC * TG * 2].bitcast(F32).rearrange("p (k t) -> p k t", k=KC), [sV.key])
            xn2 = T(hTflat[:, 0:KC * TG].rearrange("p (k t) -> p k t", k=KC), hT.key)
            rstd2 = T(hTflat[:, KC * TG:KC * TG + 2 * TG].bitcast(F32), hT.key)
            r2 = T(mixT.ap.rearrange("p k t -> p (k t)")[:, 0:2 * TG].bitcast(F32), mixT.key)
        d_x2 = S.dsem("d_x2")

        def ffn_pass(l, goff, Tg, xs_keys, outs):
            S.dma("sp", xT2[:, :, 0:Tg], xs_d[:, goff:goff + Tg].rearrange("(k p) t -> p k t", p=128), d_x2, reads=xs_keys, writes=[xT2])
            PL = pls[:, l, :]
            rmsnorm_fm(Tg, PL[:, O_NFFN:O_NFFN + KC], xn2, xT=xT2, sq=sq2, rstd=rstd2)

            def ev_up(oc, p):
                act(r2[:, 0:Tg], p[:, 0:Tg], AF.Relu, [p], [r2])
                vtt(hT2[:, oc, 0:Tg], r2[:, 0:Tg], r2[:, 0:Tg], ALU.mult, [r2], [hT2])
            linear_fm("w_up", l, xn2, Tg, DFF // 128, ev_up)

            def ev_dn(oc, p):
                vtt(xT2[:, oc, 0:Tg], xT2[:, oc, 0:Tg], p[:, 0:Tg], ALU.add, [p, xT2], [xT2])
            linear_fm("w_down", l, hT2, Tg, 8, ev_dn, kcn=DFF // 128)
            if l < DEPTH - 1:
                S.dma("sp", xs_d[:, goff:goff + Tg].rearrange("(k p) t -> p k t", p=128), xT2[:, :, 0:Tg], d_xs, reads=[xT2], writes=xs_keys)
            else:
                rmsnorm_fm(Tg, gl[:, 0:KC], sq2, xT=xT2, sq=sq2, rstd=rstd2)
                for (ydst, c0, nr) in outs:
                    for half in range(2):
                        p = pM.next()
                        for j in range(4):
                            tr(p[0:nr, j * 128:(j + 1) * 128], sq2[:, half * 4 + j, c0:c0 + nr], ident_f, [sq2, cf], [p])

                        def fill(tl, p=p, nr=nr):
                            vcopy(tl[0:nr, 0:512], p[0:nr, 0:512], [p], [tl])
                        stage_out(nr, 512, fill, [(ydst[:, half * 512:(half + 1) * 512], 0, 512)])

        convD_layer = [None]

        def seq_layer_setup(sq_, l):
            S.mark(f"setup{l}_{sq_['name']}")
            past = sq_["past"]
            b = sq_["b"]
            PL = pls[:, l, :]
            if convD_layer[0] != l:
                convD_layer[0] = l
                for j in range(62):
                    vts(convD[:, j, :], ident_f, PL[:, O_CW + j:O_CW + j + 1], None, ALU.mult, None, [cf, pls], [convD])
            S.op("dve", lambda e: e.memset(Cbase[:], 0.0), writes=[Cbase])
            S.op("dve", lambda e: e.memset(fV[:, :, :, 64:65], 1.0), writes=[fV])
            if past == 0:
                S.op("pool", lambda e: e.memset(cext[:, :, 0:30], 0.0), writes=[cext])
                S.op("dve", lambda e: e.memset(hS[:], 0.0), writes=[hS])
                S.op("dve", lambda e: e.memset(hSb[:], 0.0), writes=[hSb])
                for mb in range(2):
                    S.dma("sp", xtok[:, :], I["mem_prompt"][mb * 128:(mb + 1) * 128, :], d_xtok, writes=[xtok])
                    for g0 in range(0, KC, 4):
                        p = pM.next()
                        for j in range(4):
                            tr(p[:, j * 128:(j + 1) * 128], xtok[:, (g0 + j) * 128:(g0 + j + 1) * 128], ident_f, [xtok, cf], [p])
                        for j in range(4):
                            vcopy(memT[:, g0 + j, mb * 128:(mb + 1) * 128], p[:, j * 128:(j + 1) * 128], [p], [memT])


                for wi, wname in enumerate(("w_mk", "w_mv")):
                    for g0 in range(2):
                        t, v = load_slab(wname, l, g0 * 512, 512)
                        if wi == 0:
                            for j in range(4):
                                p = pA.next()
                                for kc in range(KC):
                                    mm(p[:, 0:NMEM], v[:, kc, j * 128:(j + 1) * 128], memT[:, kc, :], kc == 0, kc == KC - 1, [t, memT], [p])
                                vcopy(mKT[:, g0 * 4 + j, :], p[:, 0:NMEM], [p], [mKT])
                        for mb in range(2):
                            p = pA.next()
                            for kc in range(KC):
                                mm(p[:, 0:512], memT[:, kc, mb * 128:(mb + 1) * 128], v[:, kc, :], kc == 0, kc == KC - 1, [t, memT], [p])

                            def fill(tl, p=p):
                                act(tl[:, 0:512], p[:, 0:512], AF.Copy, [p], [tl])
                            dst = sq_["mem_k" if wi == 0 else "mem_v"][l][mb * 128:(mb + 1) * 128, g0 * 512:(g0 + 1) * 512]
                            tl = stage_out(128, 512, fill, [(dst, 0, 512)])
                            if wi == 1:
                                vcopy(mV[:, mb, g0 * 512:(g0 + 1) * 512], tl[:, 0:512], [tl], [mV], eng="pool")
            else:
                npb = past // 128
                for cc_ in range(2):
                    S.dma("pool", cext[:, cc_, 0:30], I["cache_conv"][l, b][:, cc_ * 128:(cc_ + 1) * 128].rearrange("r p -> p r"), d_cext, writes=[cext],
                          allow_slow_non_contiguous=True)
                S.dma("sp", hS[:], I["state_hgrn"][l, b].rearrange("h k v -> k h v"), d_hS, writes=[hS])
                vcopy(hSb[:], hS[:], [hS], [hSb])
                for h_ in range(H):
                    S.dma("sp", fV[:, 0:npb, h_, 0:64], CC["cache_fox_v"][l, b][:, h_ * 64:(h_ + 1) * 64].rearrange("(n p) d -> p n d", p=128), d_fV, reads=[("cc",)], writes=[fV])
                S.dma("sp", sV[:, 0:npb, :], CC["cache_sb_v"][l, b].rearrange("(n p) g -> p n g", p=128), d_sV, reads=[("cc",)], writes=[sV])
                for (cn, dstT) in (("cache_fox_k", fKT), ("cache_sb_k", sKT)):
                    S.dma("sp", ktmp[:, 0:npb, :], CC[cn][l, b].rearrange("(n p) g -> p n g", p=128), d_ktmp, reads=[("cc",)], writes=[ktmp])
                    for c in range(2):
                        for g0 in range(0, npb, 4):
                            pt = pM.next()
                            ptb = pt[:].bitcast(BF16)
                            ng = min(4, npb - g0)
                            for j in range(ng):
                                tr(ptb[:, j * 128:(j + 1) * 128], ktmp[:, g0 + j, c * 128:(c + 1) * 128], ident_b, [ktmp, cb], [pt])
                            vcopy(dstT[:, c, g0 * 128:(g0 + ng) * 128], ptb[:, 0:ng * 128], [pt], [dstT])
                S.dma("sp", flfp[:, 0:npb, :], I["cache_fox_logf"][l, b].rearrange("(n p) h -> p n h", p=128), d_flfp, writes=[flfp])
                for bk in range(npb):
                    fox_c_block(flfp[:, bk, :], flfp, 128, bk, False, 0)
                S.dma("sp", mV[:], CC["cache_mem_v"][l, b].rearrange("(n p) g -> p n g", p=128), d_mV, reads=[("cc",)], writes=[mV])
                S.dma("sp", mtmp[:], CC["cache_mem_k"][l, b].rearrange("(n p) g -> p n g", p=128), d_mtmp, reads=[("cc",)], writes=[mtmp])
                for mb in range(2):
                    for g0 in range(0, 8, 4):
                        pt = pM.next()
                        ptb = pt[:].bitcast(BF16)
                        for j in range(4):
                            tr(ptb[:, j * 128:(j + 1) * 128], mtmp[:, mb, (g0 + j) * 128:(g0 + j + 1) * 128], ident_b, [mtmp, cb], [pt])
                        for j in range(4):
                            vcopy(mKT[:, g0 + j, mb * 128:(mb + 1) * 128], ptb[:, j * 128:(j + 1) * 128], [pt], [mKT])

        seqs = [dict(name="p", L=SEQ, T=TP, past=0, b=0, xoff=0, x_src=I["x_prompt"], y=O["y_prompt"], first_seq=True,
                     conv=[O["conv_p"][l] for l in range(DEPTH)], fox_k=[O["fox_k_p"][l] for l in range(DEPTH)],
                     fox_v=[O["fox_v_p"][l] for l in range(DEPTH)], fox_logf=[O["fox_logf_p"][l] for l in range(DEPTH)],
                     hgrn=[O["hgrn_p"][l] for l in range(DEPTH)], sb_k=[O["sb_k_p"][l] for l in range(DEPTH)],
                     sb_v=[O["sb_v_p"][l] for l in range(DEPTH)], mem_k=[O["mem_k_p"][l] for l in range(DEPTH)],
                     mem_v=[O["mem_v_p"][l] for l in range(DEPTH)])]
        for s_ in range(NS):
            seqs.append(dict(name=f"s{s_}", L=DS, T=DS, past=PAST, b=s_, xoff=SEQ + s_ * DS, x_src=I["x_sample"][s_], y=O["y_sample"][s_], first_seq=False,
                             conv=[O["conv_s"][l, s_] for l in range(DEPTH)], fox_k=[O["fox_k_s"][l, s_] for l in range(DEPTH)],
                             fox_v=[O["fox_v_s"][l, s_] for l in range(DEPTH)], fox_logf=[O["fox_logf_s"][l, s_] for l in range(DEPTH)],
                             hgrn=[O["hgrn_s"][l, s_] for l in range(DEPTH)], sb_k=[O["sb_k_s"][l, s_] for l in range(DEPTH)],
                             sb_v=[O["sb_v_s"][l, s_] for l in range(DEPTH)]))
        for l in range(DEPTH):
            sp_ = seqs[0]
            seq_layer_setup(sp_, l)
            ntl = sp_["L"] // sp_["T"]
            for ti in range(ntl):
                layer_tile(sp_, l, ti)
            tpg = TG // sp_["T"]
            for g in range(sp_["L"] // TG):
                outs = [(sp_["y"][g * TG + j * 128:g * TG + (j + 1) * 128, :], j * 128, 128) for j in range(TG // 128)]
                ffn_pass(l, g * TG, TG, [("xs", "p", g * tpg + j) for j in range(tpg)], outs)
        for l in range(DEPTH):
            for sq_ in seqs[1:]:
                seq_layer_setup(sq_, l)
                layer_tile(sq_, l, 0)
            outs = [(sq_["y"], i * DS, DS) for i, sq_ in enumerate(seqs[1:])]
            ffn_pass(l, SEQ, NS * DS, [("xs", sq_["name"], 0) for sq_ in seqs[1:]], outs)
        S.finalize(final_waits=[d_oflf, d_ouF, d_ohS] + d_stg + [d_xs])
        n_ops = {e: len(S.ops[e]) for e in S.ENGS}
    return nc, n_ops


FULL_CFG = dict(SEQ=4096, T=256, DEPTH=4, PAST=2048, DS=64, NS=2)


def host_pack(inputs, cfg):
    DEPTH = cfg["DEPTH"]
    pl = np.zeros((DEPTH, 128, NPL), np.float32)
    for l in range(DEPTH):
        pl[l, :, O_NMIX:O_NMIX + 8] = inputs["norm_mix"][l].reshape(8, 128).T
        pl[l, :, O_NMEM:O_NMEM + 8] = inputs["norm_mem"][l].reshape(8, 128).T
        pl[l, :, O_NFFN:O_NFFN + 8] = inputs["norm_ffn"][l].reshape(8, 128).T
        cw = inputs["conv_w"][l]
        for ch in range(2):
            pl[l, :, O_CW + ch * 31:O_CW + (ch + 1) * 31] = cw[:, ch * 128:(ch + 1) * 128].T
        pl[l, :, O_CB:O_CB + 2] = inputs["conv_b"][l].reshape(2, 128).T
        pl[l, :, O_LG:O_LG + 2] = inputs["conv_ln_g"][l].reshape(2, 128).T
        pl[l, :, O_LB:O_LB + 2] = inputs["conv_ln_b"][l].reshape(2, 128).T
        pl[l, :, O_BF:O_BF + 4] = inputs["b_fox_f"][l][None, :]
        pl[l, :, O_HN:O_HN + 256] = np.tile(inputs["hgrn_norm"][l], 4)[None, :]
    gl = np.zeros((128, 8 + DEPTH * 4), np.float32)
    gl[:, 0:8] = inputs["norm_final"].reshape(8, 128).T
    gl[0:64, 8:] = inputs["hgrn_lb"].reshape(DEPTH, 4, 64).transpose(2, 0, 1).reshape(64, DEPTH * 4)
    return pl, gl


_NC_CACHE = {}


def make_in_maps(inputs, cfg, n_cores, nb_prompt):
    pl, gl = host_pack(inputs, cfg)
    consts = make_consts(cfg["T"])
    NS = cfg["NS"]
    DEPTH = cfg["DEPTH"]
    maps = []
    f = lambda a: np.ascontiguousarray(a, dtype=np.float32)
    for c in range(n_cores):
        bp = c % nb_prompt
        ss = slice(c * NS, (c + 1) * NS)
        m = {
            "x_prompt": f(inputs["x_prompt"][bp]), "x_sample": f(inputs["x_sample"][ss]), "mem_prompt": f(inputs["mem_prompt"][bp]),
            "cache_conv": f(inputs["cache_conv"][:, ss]),
            "cache_fox_k": f(inputs["cache_fox_k"][:, ss].reshape(DEPTH, NS, -1, G)), "cache_fox_v": f(inputs["cache_fox_v"][:, ss].reshape(DEPTH, NS, -1, G)),
            "cache_sb_k": f(inputs["cache_sb_k"][:, ss].reshape(DEPTH, NS, -1, G)), "cache_sb_v": f(inputs["cache_sb_v"][:, ss].reshape(DEPTH, NS, -1, G)),
            "cache_fox_logf": f(inputs["cache_fox_logf"][:, ss]), "state_hgrn": f(inputs["state_hgrn"][:, ss]),
            "cache_mem_k": f(inputs["cache_mem_k"][:, ss].reshape(DEPTH, NS, NMEM, D)), "cache_mem_v": f(inputs["cache_mem_v"][:, ss].reshape(DEPTH, NS, NMEM, D)),
            "pl": pl, "gl": gl, "consts": consts,
        }
        for n in ("w_in", "w_out", "w_mq", "w_mk", "w_mv", "w_mo", "w_up", "w_down"):
            m[n] = f(inputs[n])
        maps.append(m)
    return maps


def assemble(results, cfg, n_cores, nb_prompt, nb_sample):
    DEPTH, SEQ, DS, NS = cfg["DEPTH"], cfg["SEQ"], cfg["DS"], cfg["NS"]
    R = results
    P = lambda k, shp: np.stack([np.asarray(R[b][k], np.float32) for b in range(nb_prompt)], axis=0 if k.startswith("y_") else 1).reshape(shp)
    y_p = np.stack([R[b]["y_prompt"] for b in range(nb_prompt)], 0)
    y_s = np.concatenate([R[c]["y_sample"] for c in range(n_cores)], 0)

    def pp(k, tail):
        return np.stack([np.asarray(R[b][k]) for b in range(nb_prompt)], 1).reshape((DEPTH, nb_prompt) + tail)

    def ps_(k, tail):
        return np.concatenate([np.asarray(R[c][k]) for c in range(n_cores)], 1).reshape((DEPTH, nb_sample) + tail)
    outs = (y_p, y_s,
            pp("conv_p", (CW - 1, G)), pp("fox_k_p", (SEQ, H, HD)), pp("fox_v_p", (SEQ, H, HD)), pp("fox_logf_p", (SEQ, H)),
            pp("hgrn_p", (H, HD, HD)), pp("sb_k_p", (SEQ, H, HD)), pp("sb_v_p", (SEQ, H, HD)),
            pp("mem_k_p", (NMEM, H, 256)), pp("mem_v_p", (NMEM, H, 256)),
            ps_("conv_s", (CW - 1, G)), ps_("fox_k_s", (DS, H, HD)), ps_("fox_v_s", (DS, H, HD)), ps_("fox_logf_s", (DS, H)),
            ps_("hgrn_s", (H, HD, HD)), ps_("sb_k_s", (DS, H, HD)), ps_("sb_v_s", (DS, H, HD)))
    return tuple(np.ascontiguousarray(o, dtype=np.float32) for o in outs)


def kernel(**inputs):
    cfg = FULL_CFG
    inputs = {k: np.asarray(v) for k, v in inputs.items()}
    if "nc" not in _NC_CACHE:
        _NC_CACHE["nc"] = build(cfg)[0]
    nc = _NC_CACHE["nc"]
    maps = make_in_maps(inputs, cfg, 8, 4)
    res = run_bass_kernel_spmd(nc, maps, core_ids=list(range(8)))
    return assemble(res.results, cfg, 8, 4, 16)
```

```python
import numpy as np
import concourse.bass as bass
import concourse.mybir as mybir
from concourse.bass_utils import run_bass_kernel_spmd
from contextlib import ExitStack

F32 = mybir.dt.float32
BF16 = mybir.dt.bfloat16
ALU = mybir.AluOpType
AF = mybir.ActivationFunctionType
AX = mybir.AxisListType


class T:
    __slots__ = ("ap", "key")

    def __init__(self, ap, key):
        self.ap = ap
        self.key = key

    def __getitem__(self, idx):
        return self.ap[idx]


class DSem:
    __slots__ = ("sem", "val")

    def __init__(self, sem):
        self.sem = sem
        self.val = 0


class Sched:
    ENGS = ("pe", "act", "dve", "pool", "sp")
    SEM_WRAP = 30000

    def __init__(self, nc, stack):
        self.nc = nc
        self.stack = stack
        self.ops = {e: [] for e in self.ENGS}
        self.last_w = {}
        self.readers = {}
        self.nk = 0

    def sem(self, name):
        return self.stack.enter_context(self.nc.semaphore(name))

    def dsem(self, name):
        return DSem(self.sem(name))

    def sb(self, name, shape, dtype):
        h = self.stack.enter_context(self.nc.sbuf_tensor("s_" + name, list(shape), dtype))
        self.nk += 1
        return T(h[:], ("sb", name, self.nk))

    def ps(self, name, shape, dtype):
        h = self.stack.enter_context(self.nc.psum_tensor("q_" + name, list(shape), dtype))
        self.nk += 1
        return T(h[:], ("ps", name, self.nk))

    def _deps(self, reads, writes, ev):
        deps = []
        def expand(lst):
            out = []
            for k in lst:
                k = k.key if isinstance(k, T) else k
                if isinstance(k, list):
                    out.extend(k)
                else:
                    out.append(k)
            return out
        reads = expand(reads)
        writes = expand(writes)
        for k in reads:
            w = self.last_w.get(k)
            if w is not None:
                deps.append(w)
            rl_ = self.readers.setdefault(k, [])
            if isinstance(k, tuple) and k and k[0] == "ps":
                for r in rl_:
                    if r[0] == "e" and r[1] != ev[1]:
                        deps.append(r)
            rl_.append(ev)
        for k in writes:
            w = self.last_w.get(k)
            if w is not None:
                deps.append(w)
            deps.extend(self.readers.get(k, ()))
            self.last_w[k] = ev
            self.readers[k] = []
        return [d for d in deps if d != ev]

    stopped = False

    countdown = None

    def mark(self, name):
        import os
        if os.environ.get("STOP_AT") == name:
            n = int(os.environ.get("STOP_N", "0"))
            if n == 0:
                self.stopped = True
            else:
                self.countdown = n

    def _tick(self):
        if self.countdown is not None:
            self.countdown -= 1
            if self.countdown < 0:
                self.stopped = True

    def op(self, eng, fn, reads=(), writes=()):
        self._tick()
        if self.stopped:
            return None
        lst = self.ops[eng]
        ev = ("e", eng, len(lst))
        deps = self._deps(reads, writes, ev)
        lst.append({"fn": fn, "deps": deps, "sig": False, "dma": None})
        return ev

    def dma(self, q, out_ap, in_ap, dsem, reads=(), writes=(), **kw):
        self._tick()
        if self.stopped:
            return None
        lst = self.ops[q]
        dsem.val += 16
        ev = ("d", dsem, dsem.val)
        deps = self._deps(reads, writes, ev)

        def fn(e, out_ap=out_ap, in_ap=in_ap, kw=kw):
            return e.dma_start(out=out_ap, in_=in_ap, **kw)

        lst.append({"fn": fn, "deps": deps, "sig": False, "dma": dsem})
        return ev

    def finalize(self, final_waits=()):
        nc = self.nc
        for e in self.ENGS:
            for o in self.ops[e]:
                for d in o["deps"]:
                    if d[0] == "e":
                        if d[1] == "pe" and e == "pe":
                            continue
                        self.ops[d[1]][d[2]]["sig"] = True
        val = {}
        for e in self.ENGS:
            c = 0
            gen = 0
            cur = None
            for i, o in enumerate(self.ops[e]):
                if o["sig"] and o["dma"] is None:
                    if cur is None or c >= self.SEM_WRAP:
                        cur = self.sem(f"es_{e}_{gen}")
                        gen += 1
                        c = 0
                    c += 1
                    val[(e, i)] = (cur, c)

        def run(e, eng):
            clock = {}
            for i, o in enumerate(self.ops[e]):
                need = {}
                for d in o["deps"]:
                    if d[0] == "e":
                        if d[1] == "pe" and e == "pe":
                            continue
                        s, v = val[(d[1], d[2])]
                    else:
                        s, v = d[1].sem, d[2]
                    sid = id(s)
                    if clock.get(sid, 0) >= v:
                        continue
                    if sid not in need or need[sid][1] < v:
                        need[sid] = (s, v)
                for sid, (s, v) in need.items():
                    eng.wait_ge(s, v)
                    clock[sid] = v
                ins = o["fn"](eng)
                if o["dma"] is not None:
                    ins.then_inc(o["dma"].sem, 16)
                elif o["sig"]:
                    s, v = val[(e, i)]
                    ins.then_inc(s, 1)
            if e == "sp":
                for ds in final_waits:
                    if ds.val > 0:
                        eng.wait_ge(ds.sem, ds.val)

        with nc.Block() as block:
            @block.tensor
            def _(eng):
                run("pe", eng)

            @block.scalar
            def _(eng):
                run("act", eng)

            @block.vector
            def _(eng):
                run("dve", eng)

            @block.gpsimd
            def _(eng):
                run("pool", eng)

            @block.sync
            def _(eng):
                run("sp", eng)


class Rot:
    def __init__(self, tiles):
        self.tiles = tiles
        self.i = 0

    def next(self):
        t = self.tiles[self.i % len(self.tiles)]
        self.i += 1
        return t


D = 1024
KC = 8
G = 256
H = 4
HD = 64
CW = 31
NMEM = 256
DFF = 4096
PIN = 3076
EPS = 1e-6
O_NMIX, O_NMEM, O_NFFN, O_CW, O_CB, O_LG, O_LB, O_BF, O_HN = 0, 8, 16, 24, 86, 88, 90, 92, 96
NPL = 96 + 256
C_ID, C_TRI, C_ONE, C_HM, C_SCM = 0, 128, 256, 384, 640
NCF = 896
C_NTF, C_NTS, C_M01, C_NTI, C_NEG1 = 896, 1152, 1408, 1664, 1792
NCONST = 1920


def make_consts(TMAX):
    c = np.zeros((128, NCONST), np.float32)
    s = np.arange(128)[:, None]
    t = np.arange(128)[None, :]
    c[:, C_ID:C_ID + 128] = (s == t)
    c[:, C_TRI:C_TRI + 128] = (s <= t)
    c[:, C_ONE:C_ONE + 128] = 1.0
    c[:, C_NTF:C_NTF + 128] = np.where(s <= t, 0.0, -30000.0)
    c[:, C_NTS:C_NTS + 128] = np.where(s < t, 0.0, -30000.0)
    c[:, C_M01:C_M01 + 128] = (s < t)
    c[:, C_M01 + 128:C_M01 + 256] = 1.0
    c[:, C_NTI:C_NTI + 128] = np.where(s >= t, -1.0, 0.0)
    c[:, C_NEG1:C_NEG1 + 128] = -1.0
    hm = (np.arange(64)[:, None] <= np.arange(64)[None, :]).astype(np.float32)
    c[:64, C_HM:C_HM + 256] = np.tile(hm, (1, 4))
    sm = np.ones((256,), np.float32)
    sm[::64] = 0.0
    c[:, C_SCM:C_SCM + 256] = sm[None, :]
    return c


FFN_WIDE = True
KEEP_WARM = 2


def build(cfg):
    SEQ, TP, DEPTH, PAST, DS, NS = cfg["SEQ"], cfg["T"], cfg["DEPTH"], cfg["PAST"], cfg["DS"], cfg["NS"]
    NKB = max(SEQ // 128, PAST // 128 + 1)
    nc = bass.Bass("TRN2", target_bir_lowering=False)

    def din(name, shape, dt=F32):
        return nc.dram_tensor(name, list(shape), dt, kind="ExternalInput").ap()

    def dout(name, shape):
        return nc.dram_tensor(name, list(shape), F32, kind="ExternalOutput").ap()

    def dint(name, shape, dt):
        return nc.dram_tensor(name, list(shape), dt, kind="Internal").ap()

    I = {}
    I["x_prompt"] = din("x_prompt", [SEQ, D])
    I["x_sample"] = din("x_sample", [NS, DS, D])
    I["mem_prompt"] = din("mem_prompt", [NMEM, D])
    I["cache_conv"] = din("cache_conv", [DEPTH, NS, CW - 1, G])
    for n in ("cache_fox_k", "cache_fox_v", "cache_sb_k", "cache_sb_v"):
        I[n] = din(n, [DEPTH, NS, PAST, G])
    I["cache_fox_logf"] = din("cache_fox_logf", [DEPTH, NS, PAST, H])
    I["state_hgrn"] = din("state_hgrn", [DEPTH, NS, H, HD, HD])
    I["cache_mem_k"] = din("cache_mem_k", [DEPTH, NS, NMEM, D])
    I["cache_mem_v"] = din("cache_mem_v", [DEPTH, NS, NMEM, D])
    WSH = {"w_in": (D, PIN), "w_out": (D, D), "w_mq": (D, D), "w_mk": (D, D), "w_mv": (D, D), "w_mo": (D, D),
           "w_up": (D, DFF), "w_down": (DFF, D)}
    Wf = {n: din(n, [DEPTH, s[0], s[1]]) for n, s in WSH.items()}
    SLABS = {"w_in": [(0, 512), (512, 512), (768, 516), (1284, 512), (1796, 512), (2308, 512), (2564, 512)],
             "w_up": [(i * 512, 512) for i in range(8)], "w_down": [(i * 128, 128) for i in range(8)]}
    for n_ in ("w_out", "w_mq", "w_mk", "w_mv", "w_mo"):
        SLABS[n_] = [(0, 512), (512, 512)]
    Wb = {n: dint(n + "_bf", [DEPTH, len(SLABS[n]), 128, 4128], BF16) for n in WSH}
    pl_d = din("pl", [DEPTH, 128, NPL])
    gl_d = din("gl", [128, KC + DEPTH * H])
    const_d = din("consts", [128, NCONST])
    O = {}
    O["y_prompt"] = dout("y_prompt", [SEQ, D])
    O["y_sample"] = dout("y_sample", [NS, DS, D])
    O["conv_p"] = dout("conv_p", [DEPTH, CW - 1, G])
    O["fox_k_p"] = dout("fox_k_p", [DEPTH, SEQ, G])
    O["fox_v_p"] = dout("fox_v_p", [DEPTH, SEQ, G])
    O["fox_logf_p"] = dout("fox_logf_p", [DEPTH, SEQ, H])
    O["hgrn_p"] = dout("hgrn_p", [DEPTH, H, HD, HD])
    O["sb_k_p"] = dout("sb_k_p", [DEPTH, SEQ, G])
    O["sb_v_p"] = dout("sb_v_p", [DEPTH, SEQ, G])
    O["mem_k_p"] = dout("mem_k_p", [DEPTH, NMEM, D])
    O["mem_v_p"] = dout("mem_v_p", [DEPTH, NMEM, D])
    O["conv_s"] = dout("conv_s", [DEPTH, NS, CW - 1, G])
    O["fox_k_s"] = dout("fox_k_s", [DEPTH, NS, DS, G])
    O["fox_v_s"] = dout("fox_v_s", [DEPTH, NS, DS, G])
    O["fox_logf_s"] = dout("fox_logf_s", [DEPTH, NS, DS, H])
    O["hgrn_s"] = dout("hgrn_s", [DEPTH, NS, H, HD, HD])
    O["sb_k_s"] = dout("sb_k_s", [DEPTH, NS, DS, G])
    O["sb_v_s"] = dout("sb_v_s", [DEPTH, NS, DS, G])
    CC = {n: dint(n + "_bf", [DEPTH, NS, PAST, G], BF16) for n in ("cache_fox_k", "cache_fox_v", "cache_sb_k", "cache_sb_v")}
    CC["cache_mem_k"] = dint("cache_mem_k_bf", [DEPTH, NS, NMEM, D], BF16)
    CC["cache_mem_v"] = dint("cache_mem_v_bf", [DEPTH, NS, NMEM, D], BF16)
    LTOT = SEQ + NS * DS
    xs_d = dint("xs", [D, LTOT], F32)

    st = ExitStack()
    with st:
        S = Sched(nc, st)
        TM = max(TP, DS)
        cf = S.sb("cf", [128, NCF], F32)
        cb = S.sb("cb", [128, NCONST], BF16)
        d_c = S.dsem("d_cf")
        d_cb = S.dsem("d_cb")
        d_gl = S.dsem("d_gl")
        d_pls = S.dsem("d_pls")
        S.dma("sp", cf[:], const_d[:, 0:NCF], d_c, writes=[cf])
        S.dma("pool", cb[:], const_d, d_cb, writes=[cb])
        gl = S.sb("gl", [128, KC + DEPTH * H], F32)
        S.dma("sp", gl[:], gl_d, d_gl, writes=[gl])
        pls = S.sb("pls", [128, DEPTH, NPL], F32)
        for l in range(DEPTH):
            S.dma("sp", pls[:, l, :], pl_d[l], d_pls, writes=[pls])
        zer = S.sb("zer", [128, 512], BF16)
        S.op("dve", lambda e: e.memset(zer[:], 0.0), writes=[zer])
        eps_t = S.sb("eps_t", [128, 1], F32)
        S.op("dve", lambda e: e.memset(eps_t[:], EPS), writes=[eps_t])
        one_t = S.sb("one_t", [128, 1], F32)
        S.op("dve", lambda e: e.memset(one_t[:], 1.0), writes=[one_t])

        S.mark("cast")
        d_w = {}
        for l in range(DEPTH):
            for n, (r, ccols) in WSH.items():
                d_w[(l, n)] = S.dsem(f"d_w{l}_{n}")
                kcn_ = r // 128
                for idx, (c0, ncl) in enumerate(SLABS[n]):
                    S.dma("pool", Wb[n][l, idx, :, 0:kcn_ * ncl].rearrange("p (k n) -> p k n", k=kcn_),
                          Wf[n][l, :, c0:c0 + ncl].rearrange("(k p) n -> p k n", p=128), d_w[(l, n)], writes=[("wb", l, n)])

        d_cc = S.dsem("d_cc")
        for l in range(DEPTH):
            for b_ in range(NS):
                for n in ("cache_fox_k", "cache_fox_v", "cache_sb_k", "cache_sb_v"):
                    for r0 in range(0, PAST, 1024):
                        r1 = min(PAST, r0 + 1024)
                        S.dma("pool", CC[n][l, b_, r0:r1, :], I[n][l, b_, r0:r1, :], d_cc, writes=[("cc",)])
                for n in ("cache_mem_k", "cache_mem_v"):
                    S.dma("pool", CC[n][l, b_], I[n][l, b_], d_cc, writes=[("cc",)])
        S.mark("lb")
        lbr = gl[0:64, KC:KC + DEPTH * H]
        hl_e = S.sb("hl_e", [64, DEPTH * H], F32)
        hl_m = S.sb("hl_m", [64, H], F32)
        hl_s = S.sb("hl_s", [64, H], F32)
        lbt = S.sb("lbt", [64, DEPTH * H], F32)
        omlt = S.sb("omlt", [64, DEPTH * H], F32)
        nomlt = S.sb("nomlt", [64, DEPTH * H], F32)
        hl_all = [gl, hl_e, hl_m, hl_s, lbt, omlt, nomlt]

        def sl(l):
            return slice(l * H, (l + 1) * H)
        S.op("dve", lambda e: e.tensor_copy(out=hl_m[:], in_=gl[0:64, KC:KC + H]), reads=hl_all, writes=hl_all)
        for l in range(1, DEPTH):
            S.op("dve", lambda e, l=l: e.tensor_tensor(out=hl_m[:], in0=hl_m[:], in1=gl[0:64, KC + l * H:KC + (l + 1) * H], op=ALU.max), reads=hl_all, writes=hl_all)
        for l in range(DEPTH):
            S.op("dve", lambda e, l=l: e.tensor_tensor(out=hl_e[:, sl(l)], in0=gl[0:64, KC + l * H:KC + (l + 1) * H], in1=hl_m[:], op=ALU.subtract), reads=hl_all, writes=hl_all)
        S.op("act", lambda e: e.activation(out=hl_e[:], in_=hl_e[:], func=AF.Exp), reads=hl_all, writes=hl_all)
        S.op("dve", lambda e: e.tensor_copy(out=hl_s[:], in_=hl_e[:, sl(0)]), reads=hl_all, writes=hl_all)
        for l in range(1, DEPTH):
            S.op("dve", lambda e, l=l: e.tensor_tensor(out=hl_s[:], in0=hl_s[:], in1=hl_e[:, sl(l)], op=ALU.add), reads=hl_all, writes=hl_all)
        S.op("dve", lambda e: e.reciprocal(out=hl_s[:], in_=hl_s[:]), reads=hl_all, writes=hl_all)
        for l in range(DEPTH):
            S.op("dve", lambda e, l=l: e.tensor_tensor(out=hl_e[:, sl(l)], in0=hl_e[:, sl(l)], in1=hl_s[:], op=ALU.mult), reads=hl_all, writes=hl_all)
        S.op("dve", lambda e: e.memset(lbt[:, sl(0)], 0.0), reads=hl_all, writes=hl_all)
        for l in range(1, DEPTH):
            S.op("dve", lambda e, l=l: e.tensor_tensor(out=lbt[:, sl(l)], in0=lbt[:, sl(l - 1)], in1=hl_e[:, sl(l)], op=ALU.add), reads=hl_all, writes=hl_all)
        S.op("dve", lambda e: e.tensor_scalar(out=lbt[:], in0=lbt[:], scalar1=0.0, scalar2=1.0 - 1e-6, op0=ALU.max, op1=ALU.min), reads=hl_all, writes=hl_all)
        S.op("dve", lambda e: e.tensor_scalar(out=omlt[:], in0=lbt[:], scalar1=-1.0, scalar2=1.0, op0=ALU.mult, op1=ALU.add), reads=hl_all, writes=hl_all)
        S.op("dve", lambda e: e.tensor_scalar(out=nomlt[:], in0=omlt[:], scalar1=-1.0, scalar2=None, op0=ALU.mult), reads=hl_all, writes=hl_all)

        nK = max(2 * NKB * 128, 8192)
        nFV = max(NKB * H * 65, 8192)
        nSV = max(NKB * G, 8192)
        arena = S.sb("arena", [128, 2 * nK + nFV + nSV], BF16)
        fKT = T(arena.ap[:, 0:2 * NKB * 128].rearrange("p (c n) -> p c n", c=2), ("sb", "fKT"))
        sKT = T(arena.ap[:, nK:nK + 2 * NKB * 128].rearrange("p (c n) -> p c n", c=2), ("sb", "sKT"))
        fV = T(arena.ap[:, 2 * nK:2 * nK + NKB * H * 65].rearrange("p (n h d) -> p n h d", n=NKB, h=H), ("sb", "fV"))
        sV = T(arena.ap[:, 2 * nK + nFV:2 * nK + nFV + NKB * G].rearrange("p (n g) -> p n g", n=NKB), ("sb", "sV"))
        negC = S.sb("negC", [128, NKB, H], F32)
        Cbase = S.sb("Cbase", [128, H], F32)
        cext = S.sb("cext", [128, 2, 30 + TM], BF16)
        convD = S.sb("convD", [128, 62, 128], BF16)
        hS = S.sb("hS", [64, H, HD], F32)
        hSb = S.sb("hSb", [64, H, HD], BF16)
        mKT = S.sb("mKT", [128, 8, NMEM], BF16)
        mV = S.sb("mV", [128, 2, D], BF16)
        xT = S.sb("xT", [128, KC, TM], F32)
        xn = S.sb("xn", [128, KC, TM], BF16)
        mixT = S.sb("mixT", [128, KC, TM], BF16)
        mqT = mixT
        w32 = T(mixT.ap.rearrange("p k t -> p (k t)").bitcast(F32).rearrange("p (k n) -> p k n", k=KC), mixT.key)
        xn0 = S.sb("xn0", [128, KC], F32)
        d_w32 = S.dsem("d_w32")
        moT = xn
        assert TM == 256 and PAST // 128 <= 16
        hT = S.sb("hT", [128, DFF // 128, TM], BF16)
        hTflat = hT.ap.rearrange("p a t -> p (a t)")
        ktmp = T(hTflat[:, 0:4096].rearrange("p (n g) -> p n g", g=G), hT.key)
        mtmp = T(hTflat[:, 4096:6144].rearrange("p (n g) -> p n g", g=D), hT.key)
        memT = T(hTflat[:, 4096:6144].rearrange("p (k m) -> p k m", m=NMEM), hT.key)
        xtok = T(hTflat[:, 6144:8192].bitcast(F32), hT.key)
        sq = T(hTflat[:, 0:4096].bitcast(F32).rearrange("p (k t) -> p k t", k=KC), hT.key)
        rstd = S.sb("rstd", [128, TM], F32)
        fQT = S.sb("fQT", [128, 2, TM], BF16)
        sQT = S.sb("sQT", [128, 2, TM], BF16)
        sgT = S.sb("sgT", [128, 2, TM], F32)
        uF = S.sb("uF", [128, 2, 30], F32)
        cY = S.sb("cY", [128, 2, TM], F32)
        cY2 = sgT
        cmean = S.sb("cmean", [128, TM], F32)
        cvar = S.sb("cvar", [128, TM], F32)
        flf = S.sb("flf", [128, H], F32)
        flfp = S.sb("flfp", [128, max(PAST // 128, 1), H], F32)
        Cq = S.sb("Cq", [128, H], F32)
        cs_b = S.sb("cs_b", [128, H], BF16)
        cs_f = S.sb("cs_f", [128, H], F32)
        cs_r = S.sb("cs_r", [128, H], F32)
        Cparts = S.sb("Cparts", [128, H, 3], BF16)
        CTs = S.sb("CTs", [3, H, TM], BF16)
        Pt = Rot([S.sb(f"Pt{i}", [128, TM], BF16) for i in range(4)])
        SPb = Rot([S.sb(f"SPb{i}", [128, TM], BF16) for i in range(4)])
        SPaf = [S.sb(f"SPaf{i}", [128, TM], F32) for i in range(H)]
        SPab = [S.sb(f"SPab{i}", [128, TM], BF16) for i in range(H)]
        rec = S.sb("rec", [128, 4], F32)
        mixtok = S.sb("mixtok", [128, 4, G], BF16)
        rden = cmean
        Pm = S.sb("Pm", [128, 2, TM], BF16)
        rl = Rot([S.sb(f"rl{i}", [128, TM], F32) for i in range(1)])
        Et = rl
        stg = Rot([S.sb(f"stg{i}", [128, 512], F32) for i in range(2)])
        d_stg = [S.dsem(f"d_stg{i}") for i in range(2)]
        stg_i = [0]
        d_xtok = S.dsem("d_xtok")
        d_x = S.dsem("d_x")
        d_xs = S.dsem("d_xs")
        d_misc = S.dsem("d_misc")
        d_oflf = S.dsem("d_oflf")
        d_ouF = S.dsem("d_ouF")
        d_ohS = S.dsem("d_ohS")
        d_cext = S.dsem("d_cext"); d_hS = S.dsem("d_hS"); d_fV = S.dsem("d_fV"); d_sV = S.dsem("d_sV")
        d_ktmp = S.dsem("d_ktmp"); d_flfp = S.dsem("d_flfp"); d_mV = S.dsem("d_mV"); d_mtmp = S.dsem("d_mtmp")
        slab = [S.sb(f"slab{i}", [128, 4128], BF16) for i in range(3)]
        d_slab = [S.dsem(f"d_slab{i}") for i in range(3)]
        slab_i = [0]
        hTf32 = hTflat.bitcast(F32)
        hA = T(hTf32[0:64, 0:1024].rearrange("p (h t) -> p h t", h=H), hT.key)
        hB = T(hTf32[0:64, 1024:2048].rearrange("p (h t) -> p h t", h=H), hT.key)
        hBc = T(hTf32[0:64, 2048:3072].rearrange("p (h t) -> p h t", h=H), hT.key)
        hE = T(hTf32[0:64, 3072:4096].rearrange("p (h t) -> p h t", h=H), hT.key)
        hqt = S.sb("hqt", [64, H, TM], BF16)
        hkt = S.sb("hkt", [64, H, TM], BF16)
        hql = S.sb("hql", [64, H, TM], BF16)
        hkl = S.sb("hkl", [64, H, TM], BF16)
        NCH = max(TM // 64, 1)
        hV = S.sb("hV", [64, NCH, G], BF16)
        hG = S.sb("hG", [64, NCH, G], BF16)
        khT = S.sb("khT", [64, H, 64], BF16)
        kh = S.sb("kh", [64, H, 64], BF16)
        scT = S.sb("scT", [64, H, 64], BF16)
        ofp = S.sb("ofp", [64, H, 64], F32)
        osq = S.sb("osq", [64, H, 64], F32)
        ssq = S.sb("ssq", [64, H], F32)
        hgts = [S.sb(f"hgt{i}", [64, G], BF16) for i in range(2)]
        pA = Rot([S.ps(f"pA{i}", [128, 512], F32) for i in range(2)])
        pS = Rot([S.ps(f"pS{i}", [128, 512], F32) for i in range(2)])
        pO = Rot([S.ps(f"pO{i}", [128, 512], F32) for i in range(2)])
        pM = Rot([S.ps(f"pM{i}", [128, 512], F32) for i in range(2)])
        bank6 = Rot(pS.tiles + pA.tiles + pM.tiles)

        ident_f = cf[:, C_ID:C_ID + 128]
        ident_b = cb[:, C_ID:C_ID + 128]
        ones_f = cf[:, C_ONE:C_ONE + 128]
        ones_b = cb[:, C_ONE:C_ONE + 128]

        def mm(out, lhsT, rhs, start, stop, reads, writes):
            S.op("pe", lambda e: e.matmul(out, lhsT=lhsT, rhs=rhs, start=start, stop=stop), reads=reads, writes=writes)

        def tr(out, in_, ident, reads, writes):
            S.op("pe", lambda e: e.transpose(out, in_, ident), reads=reads, writes=writes)

        def act(out, in_, func, reads, writes, **kw):
            S.op("act", lambda e: e.activation(out=out, in_=in_, func=func, **kw), reads=reads, writes=writes)

        def vcopy(out, in_, reads, writes, eng="dve"):
            S.op(eng, lambda e: e.tensor_copy(out=out, in_=in_), reads=reads, writes=writes)

        def vtt(out, in0, in1, op, reads, writes, eng="dve"):
            S.op(eng, lambda e: e.tensor_tensor(out=out, in0=in0, in1=in1, op=op), reads=reads, writes=writes)

        def vts(out, in0, s1, s2, op0, op1, reads, writes, eng="dve"):
            if op1 is None:
                S.op(eng, lambda e: e.tensor_scalar(out=out, in0=in0, scalar1=s1, scalar2=None, op0=op0), reads=reads, writes=writes)
            else:
                S.op(eng, lambda e: e.tensor_scalar(out=out, in0=in0, scalar1=s1, scalar2=s2, op0=op0, op1=op1), reads=reads, writes=writes)

        def vstt(out, in0, scalar, in1, op0, op1, reads, writes, eng="dve"):
            S.op(eng, lambda e: e.scalar_tensor_tensor(out=out, in0=in0, scalar=scalar, in1=in1, op0=op0, op1=op1), reads=reads, writes=writes)

        sq_default = sq

        def load_slab(wname, l, c0, ncols, kcn=KC):
            i = slab_i[0] % 3
            slab_i[0] += 1
            t = slab[i]
            v = t[:, 0:kcn * ncols].rearrange("p (k n) -> p k n", k=kcn)
            idx = SLABS[wname].index((c0, ncols))
            src = Wb[wname][l, idx, :, 0:kcn * ncols]
            S.dma("sp", t[:, 0:kcn * ncols], src, d_slab[i], reads=[("wb", l, wname)], writes=[t])
            return t, v

        def stage_out(n_rows, width, fill, dsts):
            i = stg_i[0] % 2
            stg_i[0] += 1
            t = stg.tiles[i]
            fill(t)
            for (dap, c0, ncl) in dsts:
                S.dma("sp", dap, t[0:n_rows, c0:c0 + ncl], d_stg[i], reads=[t], writes=[])
            return t

        def rmsnorm_fm(Tn, gcol, out_t, out_dt_is_f32=False, xT=xT, sq=None, rstd=rstd):
            sq = sq_default if sq is None else sq
            S.op("dve", lambda e: e.tensor_tensor(out=sq[:, :, 0:Tn], in0=xT[:, :, 0:Tn], in1=xT[:, :, 0:Tn], op=ALU.mult), reads=[xT], writes=[sq])
            p = pM.next()
            for kc in range(KC):
                mm(p[:, 0:Tn], ones_f, sq[:, kc, 0:Tn], kc == 0, kc == KC - 1, [sq, cf], [p])
            act(rstd[:, 0:Tn], p[:, 0:Tn], AF.Ln, [p, eps_t], [rstd], bias=eps_t[:, 0:1], scale=1.0 / D)
            act(rstd[:, 0:Tn], rstd[:, 0:Tn], AF.Exp, [rstd], [rstd], scale=-0.5)
            for kc in range(KC):
                vstt(out_t[:, kc, 0:Tn], xT[:, kc, 0:Tn], gcol[:, kc:kc + 1], rstd[:, 0:Tn], ALU.mult, ALU.mult, [xT, rstd, pls, gl], [out_t])

        def linear_fm(wname, l, src_t, Tn, nout, evac, kcn=KC, c_base=0):
            if kcn == KC:
                for g0 in range(0, nout, 4):
                    ng = min(4, nout - g0)
                    t, v = load_slab(wname, l, c_base + g0 * 128, ng * 128)
                    for j in range(ng):
                        p = pA.next()
                        for kc in range(KC):
                            mm(p[:, 0:Tn], v[:, kc, j * 128:(j + 1) * 128], src_t[:, kc, 0:Tn], kc == 0, kc == KC - 1, [t, src_t], [p])
                        evac(g0 + j, p)
            else:
                for oc in range(nout):
                    t, v = load_slab(wname, l, c_base + oc * 128, 128, kcn=kcn)
                    p = pA.next()
                    for kc in range(kcn):
                        mm(p[:, 0:Tn], v[:, kc, :], src_t[:, kc, 0:Tn], kc == 0, kc == kcn - 1, [t, src_t], [p])
                    evac(oc, p)

        def add_resid(Tn):
            def ev(oc, p):
                vtt(xT[:, oc, 0:Tn], xT[:, oc, 0:Tn], p[:, 0:Tn], ALU.add, [p, xT], [xT])
            return ev

        def fox_c_block(flf_ap, flf_key, n, blk, want_q, qcol0):
            p = pM.next()
            mm(p[0:n, 0:H], cf[0:n, C_TRI:C_TRI + n], flf_ap, True, True, [cf, flf_key], [p])
            vtt(Cq[0:n, :], p[0:n, 0:H], Cbase[0:n, :], ALU.add, [p, Cbase], [Cq])
            vts(negC[0:n, blk, :], Cq[0:n, :], -1.0, None, ALU.mult, None, [Cq], [negC])
            p2 = pM.next()
            mm(p2[:, 0:H], cf[0:n, C_ONE:C_ONE + 128], flf_ap, True, True, [cf, flf_key], [p2])
            vtt(Cbase[:], Cbase[:], p2[:, 0:H], ALU.add, [p2, Cq], [Cbase])
            if want_q:
                cst = [Cq, cs_b, cs_f, cs_r, Cparts]
                vcopy(cs_b[0:n, :], Cq[0:n, :], cst, cst)
                vcopy(Cparts[0:n, :, 0], cs_b[0:n, :], cst, cst)
                vcopy(cs_f[0:n, :], cs_b[0:n, :], cst, cst)
                vtt(cs_r[0:n, :], Cq[0:n, :], cs_f[0:n, :], ALU.subtract, cst, cst)
                vcopy(cs_b[0:n, :], cs_r[0:n, :], cst, cst)
                vcopy(Cparts[0:n, :, 1], cs_b[0:n, :], cst, cst)
                vcopy(cs_f[0:n, :], cs_b[0:n, :], cst, cst)
                vtt(cs_r[0:n, :], cs_r[0:n, :], cs_f[0:n, :], ALU.subtract, cst, cst)
                vcopy(Cparts[0:n, :, 2], cs_r[0:n, :], cst, cst)
                p3 = pM.next()
                for h in range(H):
                    mm(p3[0:3, h * 128:h * 128 + n], Cparts[0:n, h, :], ident_b[0:n, 0:n], True, True, [Cparts, cb], [p3])
                for h in range(H):
                    vcopy(CTs[0:3, h, qcol0:qcol0 + n], p3[0:3, h * 128:h * 128 + n], [p3], [CTs])

        def layer_tile(sq_, l, ti):
            Tn = sq_["T"]
            SBk = min(128, Tn)
            nsb = Tn // SBk
            nch = Tn // 64
            t0 = ti * Tn
            past = sq_["past"]
            kb0 = (past + t0) // 128
            L = sq_["L"]
            ntiles = L // Tn
            last_tile = ti == ntiles - 1
            xoff = sq_["xoff"] + t0
            PL = pls[:, l, :]
            b = sq_["b"]

            S.mark(f"A{l}_{sq_['name']}_{ti}")
            if l == 0:
                for sbi in range(nsb):
                    S.dma("sp", xtok[0:SBk, :], sq_["x_src"][t0 + sbi * SBk:t0 + (sbi + 1) * SBk, :], d_xtok, writes=[xtok])
                    for g0 in range(0, KC, 4):
                        p = pM.next()
                        for j in range(4):
                            tr(p[:, j * 128:j * 128 + SBk], xtok[0:SBk, (g0 + j) * 128:(g0 + j + 1) * 128], ident_f[0:SBk, 0:SBk], [xtok, cf], [p])
                        for j in range(4):
                            vcopy(xT[:, g0 + j, sbi * SBk:(sbi + 1) * SBk], p[:, j * 128:j * 128 + SBk], [p], [xT])
            else:
                S.dma("sp", xT[:, :, 0:Tn], xs_d[:, xoff:xoff + Tn].rearrange("(k p) t -> p k t", p=128), d_x,
                      reads=[("xs", sq_["name"], ti)], writes=[xT])

            S.mark(f"B{l}_{sq_['name']}_{ti}")
            rmsnorm_fm(Tn, PL[:, O_NMIX:O_NMIX + KC], xn)
            if past == 0 and ti == 0:
                vtt(xn0[:], xT[:, :, 0], PL[:, O_NMIX:O_NMIX + KC], ALU.mult, [xT, pls], [xn0])
                vts(xn0[:], xn0[:], rstd[:, 0:1], None, ALU.mult, None, [xn0, rstd], [xn0])

            S.mark(f"C{l}_{sq_['name']}_{ti}")
            t, v = load_slab("w_in", l, 0, 512)
            for oc in (2, 3, 0, 1):
                p = pA.next()
                for kc in range(KC):
                    mm(p[:, 0:Tn], v[:, kc, oc * 128:(oc + 1) * 128], xn[:, kc, 0:Tn], kc == 0, kc == KC - 1, [t, xn], [p])
                if oc >= 2:
                    act(sgT[:, oc - 2, 0:Tn], p[:, 0:Tn], AF.Sigmoid, [p], [sgT])
                else:
                    vtt(cext[:, oc, 30:30 + Tn], p[:, 0:Tn], sgT[:, oc, 0:Tn], ALU.mult, [p, sgT], [cext])
                    if last_tile:
                        vtt(uF[:, oc, :], p[:, Tn - 30:Tn], sgT[:, oc, Tn - 30:Tn], ALU.mult, [p, sgT], [uF])
            S.mark(f"C2_{l}_{sq_['name']}_{ti}")
            t, v = load_slab("w_in", l, 512, 512)
            for oc in range(4):
                p = pA.next()
                for kc in range(KC):
                    mm(p[:, 0:Tn], v[:, kc, oc * 128:(oc + 1) * 128], xn[:, kc, 0:Tn], kc == 0, kc == KC - 1, [t, xn], [p])
                if oc < 2:
                    act(fQT[:, oc, 0:Tn], p[:, 0:Tn], AF.Copy, [p], [fQT], scale=0.125)
                else:
                    vcopy(fKT[:, oc - 2, (past + t0):(past + t0) + Tn], p[:, 0:Tn], [p], [fKT])
            S.mark(f"C3_{l}_{sq_['name']}_{ti}")
            t, v = load_slab("w_in", l, 768, 516)
            for sbi in range(nsb):
                blk = kb0 + sbi
                r0 = t0 + sbi * SBk
                p = pA.next()
                p2 = pM.next()
                for kc in range(KC):
                    mm(p[0:SBk, 0:512], xn[:, kc, sbi * SBk:(sbi + 1) * SBk], v[:, kc, 0:512], kc == 0, kc == KC - 1, [t, xn], [p])
                for kc in range(KC):
                    mm(p2[0:SBk, 0:4], xn[:, kc, sbi * SBk:(sbi + 1) * SBk], v[:, kc, 512:516], kc == 0, kc == KC - 1, [t, xn], [p2])

                def fill(tl, p=p):
                    act(tl[0:SBk, 0:512], p[0:SBk, 0:512], AF.Copy, [p], [tl])
                tl = stage_out(SBk, 512, fill, [(sq_["fox_k"][l][r0:r0 + SBk, :], 0, 256), (sq_["fox_v"][l][r0:r0 + SBk, :], 256, 256)])
                vcopy(fV[0:SBk, blk, :, 0:64], tl[0:SBk, 256:512].rearrange("p (h d) -> p h d", h=H), [tl], [fV], eng="pool")
                vtt(flf[0:SBk, :], p2[0:SBk, 0:4], PL[0:SBk, O_BF:O_BF + 4], ALU.add, [p2, pls], [flf])
                act(flf[0:SBk, :], flf[0:SBk, :], AF.Sigmoid, [flf], [flf])
                act(flf[0:SBk, :], flf[0:SBk, :], AF.Ln, [flf], [flf])
                S.dma("sp", sq_["fox_logf"][l][r0:r0 + SBk, :], flf[0:SBk, :], d_oflf, reads=[flf], writes=[])
                fox_c_block(flf[0:SBk, :], flf, SBk, blk, True, sbi * SBk)
            S.mark(f"C4_{l}_{sq_['name']}_{ti}")
            t, v = load_slab("w_in", l, 1284, 512)
            for which in (1,):
                for h in range(H):
                    p = pA.next()
                    for kc in range(KC):
                        mm(p[0:64, 0:Tn], v[:, kc, which * 256 + h * 64:which * 256 + (h + 1) * 64], xn[:, kc, 0:Tn], kc == 0, kc == KC - 1, [t, xn], [p])
                    if which == 0:
                        vtt(hB[:, h, 0:Tn], p[0:64, 0:Tn], hE[:, h, 0:Tn], ALU.mult, [p, hE], [hB])
                    else:
                        act(hA[:, h, 0:Tn], p[0:64, 0:Tn], AF.Sigmoid, [p], [hA])
                if past == 0 and ti == 0:
                    for hp in range(2):
                        c0w = 1284 + which * 256 + hp * 128
                        S.dma("sp", w32[:], Wf["w_in"][l, :, c0w:c0w + 128].rearrange("(k p) n -> p k n", p=128), d_w32, writes=[w32])
                        for hh in range(2):
                            h = hp * 2 + hh
                            p = pM.next()
                            for kc in range(KC):
                                mm(p[0:64, 0:1], w32[:, kc, hh * 64:(hh + 1) * 64], xn0[:, kc:kc + 1], kc == 0, kc == KC - 1, [w32, xn0], [p])
                            if which == 0:
                                vtt(hB[:, h, 0:1], p[0:64, 0:1], hE[:, h, 0:1], ALU.mult, [p, hE], [hB])
                            else:
                                act(hA[:, h, 0:1], p[0:64, 0:1], AF.Sigmoid, [p], [hA])
                if which == 1:
                    for h in range(H):
                        vts(hB[:, h, 0:Tn], hA[:, h, 0:Tn], omlt[:, l * H + h:l * H + h + 1], lbt[:, l * H + h:l * H + h + 1], ALU.mult, ALU.add, [hA, omlt, lbt], [hB])
                    act(hB[:, :, 0:Tn], hB[:, :, 0:Tn], AF.Ln, [hB], [hB])
                    for h in range(H):
                        S.op("dve", lambda e, h=h: e.tensor_tensor_scan(out=hBc[:, h, 0:Tn], data0=cf[0:64, C_SCM:C_SCM + Tn], data1=hB[:, h, 0:Tn], initial=0.0, op0=ALU.mult, op1=ALU.add),
                             reads=[hB, cf], writes=[hBc])
                    vts(hBc[:, :, 0:Tn], hBc[:, :, 0:Tn], -80.0, None, ALU.max, None, [hBc], [hBc])
                    act(hE[:, :, 0:Tn], hBc[:, :, 0:Tn], AF.Exp, [hBc], [hE])
                    act(hB[:, :, 0:Tn], hBc[:, :, 0:Tn], AF.Exp, [hBc], [hB], scale=-1.0)
                    for h in range(H):
                        vts(hA[:, h, 0:Tn], hA[:, h, 0:Tn], nomlt[:, l * H + h:l * H + h + 1], omlt[:, l * H + h:l * H + h + 1], ALU.mult, ALU.add, [hA, omlt, nomlt], [hA])
                    vtt(hA[:, :, 0:Tn], hA[:, :, 0:Tn], hB[:, :, 0:Tn], ALU.mult, [hA, hB], [hA])
                    vcopy(hkt[:, :, 0:Tn], hA[:, :, 0:Tn], [hA], [hkt])
                    vtt(hB[:, :, 0:Tn], hA[:, :, 0:Tn], hkt[:, :, 0:Tn], ALU.subtract, [hA, hkt], [hB])
                    vcopy(hkl[:, :, 0:Tn], hB[:, :, 0:Tn], [hB], [hkl], eng="pool")
                else:
                    vcopy(hqt[:, :, 0:Tn], hB[:, :, 0:Tn], [hB], [hqt])
                    vtt(hA[:, :, 0:Tn], hB[:, :, 0:Tn], hqt[:, :, 0:Tn], ALU.subtract, [hB, hqt], [hA])
                    vcopy(hql[:, :, 0:Tn], hA[:, :, 0:Tn], [hA], [hql], eng="pool")
            S.mark(f"C5_{l}_{sq_['name']}_{ti}")
            t, v = load_slab("w_in", l, 1796, 512)
            for ch in range(nch):
                p = pA.next()
                for kc in range(KC):
                    mm(p[0:64, 0:512], xn[:, kc, ch * 64:(ch + 1) * 64], v[:, kc, 0:512], kc == 0, kc == KC - 1, [t, xn], [p])
                vcopy(hV[:, ch, :], p[0:64, 0:256], [p], [hV])
                act(hG[:, ch, :], p[0:64, 256:512], AF.Silu, [p], [hG])
                vtt(hG[:, ch, :], hG[:, ch, :], PL[0:64, O_HN:O_HN + 256], ALU.mult, [hG, pls], [hG])
            S.mark(f"C6_{l}_{sq_['name']}_{ti}")
            t, v = load_slab("w_in", l, 2308, 512)
            for oc in range(4):
                p = pA.next()
                for kc in range(KC):
                    mm(p[:, 0:Tn], v[:, kc, oc * 128:(oc + 1) * 128], xn[:, kc, 0:Tn], kc == 0, kc == KC - 1, [t, xn], [p])
                if oc < 2:
                    act(sQT[:, oc, 0:Tn], p[:, 0:Tn], AF.Copy, [p], [sQT], scale=0.125)
                else:
                    vcopy(sKT[:, oc - 2, (past + t0):(past + t0) + Tn], p[:, 0:Tn], [p], [sKT])
            S.mark(f"C7_{l}_{sq_['name']}_{ti}")
            t, v = load_slab("w_in", l, 2564, 512)
            for sbi in range(nsb):
                blk = kb0 + sbi
                r0 = t0 + sbi * SBk
                p = pA.next()
                for kc in range(KC):
                    mm(p[0:SBk, 0:512], xn[:, kc, sbi * SBk:(sbi + 1) * SBk], v[:, kc, 0:512], kc == 0, kc == KC - 1, [t, xn], [p])

                def fill(tl, p=p):
                    act(tl[0:SBk, 0:512], p[0:SBk, 0:512], AF.Copy, [p], [tl])
                tl = stage_out(SBk, 512, fill, [(sq_["sb_k"][l][r0:r0 + SBk, :], 0, 256), (sq_["sb_v"][l][r0:r0 + SBk, :], 256, 256)])
                vcopy(sV[0:SBk, blk, :], tl[0:SBk, 256:512], [tl], [sV], eng="pool")

            t, v = load_slab("w_in", l, 1284, 512)
            for which in (0,):
                for h in range(H):
                    p = pA.next()
                    for kc in range(KC):
                        mm(p[0:64, 0:Tn], v[:, kc, which * 256 + h * 64:which * 256 + (h + 1) * 64], xn[:, kc, 0:Tn], kc == 0, kc == KC - 1, [t, xn], [p])
                    if which == 0:
                        vtt(hB[:, h, 0:Tn], p[0:64, 0:Tn], hE[:, h, 0:Tn], ALU.mult, [p, hE], [hB])
                    else:
                        act(hA[:, h, 0:Tn], p[0:64, 0:Tn], AF.Sigmoid, [p], [hA])
                if past == 0 and ti == 0:
                    for hp in range(2):
                        c0w = 1284 + which * 256 + hp * 128
                        S.dma("sp", w32[:], Wf["w_in"][l, :, c0w:c0w + 128].rearrange("(k p) n -> p k n", p=128), d_w32, writes=[w32])
                        for hh in range(2):
                            h = hp * 2 + hh
                            p = pM.next()
                            for kc in range(KC):
                                mm(p[0:64, 0:1], w32[:, kc, hh * 64:(hh + 1) * 64], xn0[:, kc:kc + 1], kc == 0, kc == KC - 1, [w32, xn0], [p])
                            if which == 0:
                                vtt(hB[:, h, 0:1], p[0:64, 0:1], hE[:, h, 0:1], ALU.mult, [p, hE], [hB])
                            else:
                                act(hA[:, h, 0:1], p[0:64, 0:1], AF.Sigmoid, [p], [hA])
                if which == 1:
                    for h in range(H):
                        vts(hB[:, h, 0:Tn], hA[:, h, 0:Tn], omlt[:, l * H + h:l * H + h + 1], lbt[:, l * H + h:l * H + h + 1], ALU.mult, ALU.add, [hA, omlt, lbt], [hB])
                    act(hB[:, :, 0:Tn], hB[:, :, 0:Tn], AF.Ln, [hB], [hB])
                    for h in range(H):
                        S.op("dve", lambda e, h=h: e.tensor_tensor_scan(out=hBc[:, h, 0:Tn], data0=cf[0:64, C_SCM:C_SCM + Tn], data1=hB[:, h, 0:Tn], initial=0.0, op0=ALU.mult, op1=ALU.add),
                             reads=[hB, cf], writes=[hBc])
                    vts(hBc[:, :, 0:Tn], hBc[:, :, 0:Tn], -80.0, None, ALU.max, None, [hBc], [hBc])
                    act(hE[:, :, 0:Tn], hBc[:, :, 0:Tn], AF.Exp, [hBc], [hE])
                    act(hB[:, :, 0:Tn], hBc[:, :, 0:Tn], AF.Exp, [hBc], [hB], scale=-1.0)
                    for h in range(H):
                        vts(hA[:, h, 0:Tn], hA[:, h, 0:Tn], nomlt[:, l * H + h:l * H + h + 1], omlt[:, l * H + h:l * H + h + 1], ALU.mult, ALU.add, [hA, omlt, nomlt], [hA])
                    vtt(hA[:, :, 0:Tn], hA[:, :, 0:Tn], hB[:, :, 0:Tn], ALU.mult, [hA, hB], [hA])
                    vcopy(hkt[:, :, 0:Tn], hA[:, :, 0:Tn], [hA], [hkt])
                    vtt(hB[:, :, 0:Tn], hA[:, :, 0:Tn], hkt[:, :, 0:Tn], ALU.subtract, [hA, hkt], [hB])
                    vcopy(hkl[:, :, 0:Tn], hB[:, :, 0:Tn], [hB], [hkl], eng="pool")
                else:
                    vcopy(hqt[:, :, 0:Tn], hB[:, :, 0:Tn], [hB], [hqt])
                    vtt(hA[:, :, 0:Tn], hB[:, :, 0:Tn], hqt[:, :, 0:Tn], ALU.subtract, [hB, hqt], [hA])
                    vcopy(hql[:, :, 0:Tn], hA[:, :, 0:Tn], [hA], [hql], eng="pool")
            S.mark(f"D{l}_{sq_['name']}_{ti}")
            for chn in range(2):
                p = pA.next()
                for k in range(CW):
                    mm(p[:, 0:Tn], convD[:, chn * CW + k, :], cext[:, chn, k:k + Tn], k == 0, k == CW - 1, [convD, cext], [p])
                vts(cY[:, chn, 0:Tn], p[:, 0:Tn], PL[:, O_CB + chn:O_CB + chn + 1], None, ALU.add, None, [p, pls], [cY])
            vtt(cY2[:, :, 0:Tn], cY[:, :, 0:Tn], cY[:, :, 0:Tn], ALU.mult, [cY], [cY2])
            p = pM.next()
            for chn in range(2):
                mm(p[:, 0:Tn], ones_f, cY[:, chn, 0:Tn], chn == 0, chn == 1, [cY, cf], [p])
            p2 = pM.next()
            for chn in range(2):
                mm(p2[:, 0:Tn], ones_f, cY2[:, chn, 0:Tn], chn == 0, chn == 1, [cY2, cf], [p2])
            vts(cmean[:, 0:Tn], p[:, 0:Tn], 1.0 / G, None, ALU.mult, None, [p], [cmean])
            vtt(cvar[:, 0:Tn], cmean[:, 0:Tn], cmean[:, 0:Tn], ALU.mult, [cmean], [cvar])
            vstt(cvar[:, 0:Tn], p2[:, 0:Tn], 1.0 / G, cvar[:, 0:Tn], ALU.mult, ALU.subtract, [p2, cvar], [cvar])
            act(cvar[:, 0:Tn], cvar[:, 0:Tn], AF.Ln, [cvar, eps_t], [cvar], bias=eps_t[:, 0:1], scale=1.0)
            act(cvar[:, 0:Tn], cvar[:, 0:Tn], AF.Exp, [cvar], [cvar], scale=-0.5)
            for chn in range(2):
                vtt(cY[:, chn, 0:Tn], cY[:, chn, 0:Tn], cmean[:, 0:Tn], ALU.subtract, [cY, cmean], [cY])
                vtt(cY[:, chn, 0:Tn], cY[:, chn, 0:Tn], cvar[:, 0:Tn], ALU.mult, [cY, cvar], [cY])
                vts(cY[:, chn, 0:Tn], cY[:, chn, 0:Tn], PL[:, O_LG + chn:O_LG + chn + 1], PL[:, O_LB + chn:O_LB + chn + 1], ALU.mult, ALU.add, [cY, pls], [cY])
                act(mixT[:, chn, 0:Tn], cY[:, chn, 0:Tn], AF.Silu, [cY], [mixT])
            if last_tile:
                for cc_ in range(2):
                    S.dma("sp", sq_["conv"][l][:, cc_ * 128:(cc_ + 1) * 128].rearrange("r p -> p r"), uF[:, cc_, :], d_ouF, reads=[uF], writes=[],
                          allow_slow_non_contiguous=True)
            else:
                vcopy(cext[:, :, 0:30], cext[:, :, Tn:Tn + 30], [cext], [cext], eng="pool")

            S.mark(f"F{l}_{sq_['name']}_{ti}")
            past_blocks = list(range(kb0))
            diag = [(kb0 + j, SBk, j * SBk) for j in range(nsb)]
            blocks = [(bk, 128, 0, False) for bk in past_blocks] + [(bk, nk, c0, True) for (bk, nk, c0) in diag]
            Ob = pO.tiles
            for ob in Ob:
                mm(ob[0:SBk, 0:2 * nsb * 65], zer[0:128, 0:SBk], zer[0:128, 0:2 * nsb * 65], True, False, [zer], [ob])
            nb = len(blocks)

            def fox_scores(bi):
                bk, nk, c0, isd = blocks[bi]
                res = []
                for h in range(H):
                    c = h // 2
                    pr = slice(64 * (h % 2), 64 * (h % 2) + 64)
                    Sx = bank6.next()
                    mm(Sx[0:nk, c0:Tn], fKT[pr, c, bk * 128:bk * 128 + nk], fQT[pr, c, c0:Tn], True, False, [fKT, fQT], [Sx])
                    mm(Sx[0:nk, c0:Tn], ones_b[0:3, 0:nk], CTs[0:3, h, c0:Tn], False, not isd, [cb, CTs], [Sx])
                    if isd:
                        mm(Sx[0:nk, c0:Tn], ident_b[0:nk, 0:nk], cb[0:nk, C_NTF:C_NTF + Tn - c0], False, True, [cb], [Sx])
                    P_ = Pt.next()
                    act(P_[0:nk, c0:Tn], Sx[0:nk, c0:Tn], AF.Exp, [Sx, negC], [P_], bias=negC[0:nk, bk, h:h + 1], scale=1.0)
                    res.append(P_)
                return res

            def fox_pv(bi, Ps):
                bk, nk, c0, isd = blocks[bi]
                for h in range(H):
                    O_ = Ob[h // 2]
                    P_ = Ps[h]
                    for jq in range(nsb):
                        if jq * SBk < c0:
                            continue
                        o0 = ((h % 2) * nsb + jq) * 65
                        lastq = (bi == nb - 1) and (jq == nsb - 1) and (h % 2 == 1)
                        mm(O_[0:SBk, o0:o0 + 65], P_[0:nk, jq * SBk:(jq + 1) * SBk], fV[0:nk, bk, h, :], False, lastq, [P_, fV], [O_])
            for bi in range(nb):
                fox_pv(bi, fox_scores(bi))
            for h in range(H):
                O_ = Ob[h // 2]
                for jq in range(nsb):
                    o0 = ((h % 2) * nsb + jq) * 65
                    S.op("dve", lambda e, jq=jq, O_=O_, o0=o0: e.reciprocal(out=rec[0:SBk, jq:jq + 1], in_=O_[0:SBk, o0 + 64:o0 + 65]), reads=[O_], writes=[rec])
                    vts(mixtok[0:SBk, jq, h * 64:(h + 1) * 64], O_[0:SBk, o0:o0 + 64], rec[0:SBk, jq:jq + 1], None, ALU.mult, None, [O_, rec], [mixtok])
            for jq in range(nsb):
                pt = pM.next()
                ptb = pt[:].bitcast(BF16)
                for cc in range(2):
                    tr(ptb[:, cc * 128:cc * 128 + SBk], mixtok[0:SBk, jq, cc * 128:(cc + 1) * 128], ident_b[0:SBk, 0:SBk], [mixtok, cb], [pt])
                for cc in range(2):
                    vcopy(mixT[:, 2 + cc, jq * SBk:(jq + 1) * SBk], ptb[:, cc * 128:cc * 128 + SBk], [pt], [mixT])

            S.mark(f"G{l}_{sq_['name']}_{ti}")
            blocks = [(bk, nk, c0, True) for (bk, nk, c0) in reversed(diag)] + [(bk, 128, 0, False) for bk in reversed(past_blocks)]
            O_ = Ob[0]
            mm(O_[0:SBk, 0:H * nsb * 64], zer[0:128, 0:SBk], zer[0:128, 0:H * nsb * 64], True, False, [zer], [O_])
            for h in range(H):
                S.op("pool", lambda e, h=h: e.memset(SPaf[h][:], 0.0), writes=[SPaf[h]])
                S.op("pool", lambda e, h=h: e.memset(SPab[h][:], 0.0), writes=[SPab[h]])
            nb = len(blocks)

            def sb_stage1(bi):
                bk, nk, c0, isd = blocks[bi]
                res = []
                for h in range(H):
                    c = h // 2
                    pr = slice(64 * (h % 2), 64 * (h % 2) + 64)
                    Z = bank6.next()
                    mm(Z[0:nk, c0:Tn], sKT[pr, c, bk * 128:bk * 128 + nk], sQT[pr, c, c0:Tn], True, True, [sKT, sQT], [Z])
                    if KEEP_WARM:
                        for _ in range(KEEP_WARM):
                            mm(Ob[1][:, 0:512], zer[0:128, 0:128], zer[0:128, 0:512], True, True, [zer], [Ob[1]])
                    E_ = Et.next()
                    act(E_[0:nk, c0:Tn], Z[0:nk, c0:Tn], AF.Exp, [Z], [E_])
                    sp_ = SPb.next()
                    act(sp_[0:nk, c0:Tn], E_[0:nk, c0:Tn], AF.Ln, [E_, one_t], [sp_], bias=one_t[0:nk, 0:1], scale=1.0)
                    if isd:
                        vtt(sp_[0:nk, c0:Tn], sp_[0:nk, c0:Tn], cb[0:nk, C_M01:C_M01 + Tn - c0], ALU.mult, [sp_, cb], [sp_])
                    res.append(sp_)
                return res

            def sb_stage23(bi, sps):
                bk, nk, c0, isd = blocks[bi]
                have_acc = bi > 0
                As = []
                for h in range(H):
                    c = h // 2
                    pr = slice(64 * (h % 2), 64 * (h % 2) + 64)
                    sp_ = sps[h]
                    Lg = bank6.next()
                    mm(Lg[0:nk, c0:Tn], sKT[pr, c, bk * 128:bk * 128 + nk], sQT[pr, c, c0:Tn], True, False, [sKT, sQT], [Lg])
                    mm(Lg[0:nk, c0:Tn], cb[0:nk, C_NTI:C_NTI + nk], sp_[0:nk, c0:Tn], False, not (have_acc or isd), [cb, sp_], [Lg])
                    if have_acc:
                        mm(Lg[0:nk, c0:Tn], cb[0:128, C_NEG1:C_NEG1 + nk], SPab[h][0:128, c0:Tn], False, not isd, [cb, SPab[h]], [Lg])
                    if isd:
                        mm(Lg[0:nk, c0:Tn], ident_b[0:nk, 0:nk], cb[0:nk, C_NTS:C_NTS + Tn - c0], False, True, [cb], [Lg])
                    A_ = Pt.next()
                    act(A_[0:nk, c0:Tn], Lg[0:nk, c0:Tn], AF.Exp, [Lg], [A_])
                    As.append(A_)
                    if bi != nb - 1:
                        vtt(SPaf[h][0:nk, c0:Tn], SPaf[h][0:nk, c0:Tn], sp_[0:nk, c0:Tn], ALU.add, [SPaf[h], sp_], [SPaf[h]])
                        vcopy(SPab[h][0:nk, c0:Tn], SPaf[h][0:nk, c0:Tn], [SPaf[h]], [SPab[h]])
                return As

            def sb_stage3(bi, As):
                bk, nk, c0, isd = blocks[bi]
                for h in range(H):
                    A_ = As[h]
                    for jq in range(nsb):
                        if jq * SBk < c0:
                            continue
                        o0 = (h * nsb + jq) * 64
                        mm(O_[0:SBk, o0:o0 + 64], A_[0:nk, jq * SBk:(jq + 1) * SBk], sV[0:nk, bk, h * 64:(h + 1) * 64], False,
                           (bi == nb - 1) and (jq == nsb - 1) and (h == H - 1), [A_, sV], [O_])
            cur = sb_stage1(0)
            for bi in range(nb):
                As = sb_stage23(bi, cur)
                cur = sb_stage1(bi + 1) if bi + 1 < nb else None
                sb_stage3(bi, As)
            for jq in range(nsb):
                vcopy(mixtok[0:SBk, jq, :].rearrange("p (h d) -> p h d", h=H),
                      O_[0:SBk, 0:H * nsb * 64].rearrange("p (h j d) -> p h j d", h=H, j=nsb)[:, :, jq, :], [O_], [mixtok])
            for jq in range(nsb):
                pt = pM.next()
                ptb = pt[:].bitcast(BF16)
                for cc in range(2):
                    tr(ptb[:, cc * 128:cc * 128 + SBk], mixtok[0:SBk, jq, cc * 128:(cc + 1) * 128], ident_b[0:SBk, 0:SBk], [mixtok, cb], [pt])
                for cc in range(2):
                    vcopy(mixT[:, 6 + cc, jq * SBk:(jq + 1) * SBk], ptb[:, cc * 128:cc * 128 + SBk], [pt], [mixT])

            S.mark(f"H{l}_{sq_['name']}_{ti}")
            pend_tr = []

            def emit_hg_tr(hgt_, cs_):
                pt = pM.next()
                ptb = pt[:].bitcast(BF16)
                for cc in range(2):
                    tr(ptb[:, cc * 64:(cc + 1) * 64], hgt_[:, cc * 128:(cc + 1) * 128], ident_b[0:64, 0:64], [hgt_, cb], [pt])
                for cc in range(2):
                    vcopy(mixT[:, 4 + cc, cs_], ptb[:, cc * 64:(cc + 1) * 64], [pt], [mixT])
            for ch in range(nch):
                cs = slice(ch * 64, (ch + 1) * 64)
                cl = ch * 64 + 63
                hgt = hgts[ch % 2]
                for h in range(H):
                    vts(khT[:, h, :], hkt[:, h, cs], hE[:, h, cl:cl + 1], None, ALU.mult, None, [hkt, hE], [khT])
                pt = pM.next()
                ptb = pt[:].bitcast(BF16)
                for h in range(H):
                    tr(ptb[0:64, h * 64:(h + 1) * 64], khT[:, h, :], ident_b[0:64, 0:64], [khT, cb], [pt])
                vcopy(kh[:].rearrange("p h k -> p (h k)"), ptb[0:64, 0:256], [pt], [kh])
                psc = pM.next()
                for h in range(H):
                    mm(psc[0:64, h * 64:(h + 1) * 64], hkt[:, h, cs], hqt[:, h, cs], True, False, [hkt, hqt], [psc])
                    mm(psc[0:64, h * 64:(h + 1) * 64], hkl[:, h, cs], hqt[:, h, cs], False, False, [hkl, hqt], [psc])
                    mm(psc[0:64, h * 64:(h + 1) * 64], hkt[:, h, cs], hql[:, h, cs], False, True, [hkt, hql], [psc])
                vtt(scT[:].rearrange("p h k -> p (h k)"), psc[0:64, 0:256], cf[0:64, C_HM:C_HM + 256], ALU.mult, [psc, cf], [scT])
                po = pO.next()
                for h in range(H):
                    mm(po[0:64, h * 64:(h + 1) * 64], scT[:, h, :], hV[:, ch, h * 64:(h + 1) * 64], True, False, [scT, hV], [po])
                    mm(po[0:64, h * 64:(h + 1) * 64], hqt[:, h, cs], hSb[:, h, :], False, True, [hqt, hSb], [po])
                psu = pM.next()
                for h in range(H):
                    mm(psu[0:64, h * 64:(h + 1) * 64], kh[:, h, :], hV[:, ch, h * 64:(h + 1) * 64], True, True, [kh, hV], [psu])
                for h in range(H):
                    vstt(hS[:, h, :], hS[:, h, :], hE[:, h, cl:cl + 1], psu[0:64, h * 64:(h + 1) * 64], ALU.mult, ALU.add, [hS, hE, psu], [hS])
                vcopy(hSb[:], hS[:], [hS], [hSb])
                act(ofp[:].rearrange("p h k -> p (h k)"), po[0:64, 0:256], AF.Copy, [po], [ofp])
                vtt(osq[:], ofp[:], ofp[:], ALU.mult, [ofp], [osq])
                S.op("dve", lambda e: e.tensor_reduce(out=ssq[:], in_=osq[:], axis=AX.X, op=ALU.add), reads=[osq], writes=[ssq])
                act(ssq[:], ssq[:], AF.Ln, [ssq, eps_t], [ssq], bias=eps_t[0:64, 0:1], scale=1.0 / HD)
                act(ssq[:], ssq[:], AF.Exp, [ssq], [ssq], scale=-0.5)
                for h in range(H):
                    vstt(hgt[:, h * 64:(h + 1) * 64], ofp[:, h, :], ssq[:, h:h + 1], hG[:, ch, h * 64:(h + 1) * 64], ALU.mult, ALU.mult, [ofp, ssq, hG], [hgt])
                pend_tr.append((hgt, cs))
                if len(pend_tr) > 1:
                    emit_hg_tr(*pend_tr.pop(0))
            while pend_tr:
                emit_hg_tr(*pend_tr.pop(0))
            if last_tile:
                S.dma("sp", sq_["hgrn"][l].rearrange("h k v -> k h v"), hS[:], d_ohS, reads=[hS], writes=[])

            S.mark(f"I{l}_{sq_['name']}_{ti}")
            linear_fm("w_out", l, mixT, Tn, 8, add_resid(Tn))
            S.mark(f"J{l}_{sq_['name']}_{ti}")
            rmsnorm_fm(Tn, PL[:, O_NMEM:O_NMEM + KC], xn)

            def ev_q(oc, p):
                act(mqT[:, oc, 0:Tn], p[:, 0:Tn], AF.Copy, [p], [mqT], scale=1.0 / 16.0)
            linear_fm("w_mq", l, xn, Tn, 8, ev_q)
            for h in range(H):
                for mb in range(2):
                    Sx = pS.next()
                    for dc in range(2):
                        mm(Sx[:, 0:Tn], mKT[:, h * 2 + dc, mb * 128:(mb + 1) * 128], mqT[:, h * 2 + dc, 0:Tn], dc == 0, dc == 1, [mKT, mqT], [Sx])
                    act(Pm[:, mb, 0:Tn], Sx[:, 0:Tn], AF.Exp, [Sx], [Pm])
                pd = pM.next()
                for mb in range(2):
                    mm(pd[:, 0:Tn], ones_b, Pm[:, mb, 0:Tn], mb == 0, mb == 1, [cb, Pm], [pd])
                S.op("dve", lambda e, pd=pd: e.reciprocal(out=rden[:, 0:Tn], in_=pd[:, 0:Tn]), reads=[pd], writes=[rden])
                for dc in range(2):
                    po = pO.next()
                    for mb in range(2):
                        mm(po[:, 0:Tn], mV[:, mb, h * 256 + dc * 128:h * 256 + (dc + 1) * 128], Pm[:, mb, 0:Tn], mb == 0, mb == 1, [mV, Pm], [po])
                    vtt(moT[:, h * 2 + dc, 0:Tn], po[:, 0:Tn], rden[:, 0:Tn], ALU.mult, [po, rden], [moT])
            linear_fm("w_mo", l, moT, Tn, 8, add_resid(Tn))
            S.mark(f"K{l}_{sq_['name']}_{ti}")
            S.dma("sp", xs_d[:, xoff:xoff + Tn].rearrange("(k p) t -> p k t", p=128), xT[:, :, 0:Tn], d_xs,
                  reads=[xT], writes=[("xs", sq_["name"], ti)])

        TG = 512
        if FFN_WIDE:
            hT2 = T(arena.ap[:, 0:2 * nK][:, 0:32 * TG].rearrange("p (a t) -> p a t", a=DFF // 128), [fKT.key, sKT.key])
            xT2 = T(arena.ap[:, 2 * nK:2 * nK + KC * TG * 2].bitcast(F32).rearrange("p (k t) -> p k t", k=KC), [fV.key])
            sq2 = T(arena.ap[:, 2 * nK + nFV:2 * nK + nFV + KC * TG * 2].bitcast(F32).rearrange("p (k t) -> p k t", k=KC), [sV.key])
            xn2 = T(hTflat[:, 0:KC * TG].rearrange("p (k t) -> p k t", k=KC), hT.key)
            rstd2 = T(hTflat[:, KC * TG:KC * TG + 2 * TG].bitcast(F32), hT.key)
            r2 = T(mixT.ap.rearrange("p k t -> p (k t)")[:, 0:2 * TG].bitcast(F32), mixT.key)
        d_x2 = S.dsem("d_x2")

        def ffn_pass(l, goff, Tg, xs_keys, outs):
            S.dma("sp", xT2[:, :, 0:Tg], xs_d[:, goff:goff + Tg].rearrange("(k p) t -> p k t", p=128), d_x2, reads=xs_keys, writes=[xT2])
            PL = pls[:, l, :]
            rmsnorm_fm(Tg, PL[:, O_NFFN:O_NFFN + KC], xn2, xT=xT2, sq=sq2, rstd=rstd2)

            def ev_up(oc, p):
                act(r2[:, 0:Tg], p[:, 0:Tg], AF.Relu, [p], [r2])
                vtt(hT2[:, oc, 0:Tg], r2[:, 0:Tg], r2[:, 0:Tg], ALU.mult, [r2], [hT2])
            linear_fm("w_up", l, xn2, Tg, DFF // 128, ev_up)

            def ev_dn(oc, p):
                vtt(xT2[:, oc, 0:Tg], xT2[:, oc, 0:Tg], p[:, 0:Tg], ALU.add, [p, xT2], [xT2])
            linear_fm("w_down", l, hT2, Tg, 8, ev_dn, kcn=DFF // 128)
            if l < DEPTH - 1:
                S.dma("sp", xs_d[:, goff:goff + Tg].rearrange("(k p) t -> p k t", p=128), xT2[:, :, 0:Tg], d_xs, reads=[xT2], writes=xs_keys)
            else:
                rmsnorm_fm(Tg, gl[:, 0:KC], sq2, xT=xT2, sq=sq2, rstd=rstd2)
                for (ydst, c0, nr) in outs:
                    for half in range(2):
                        p = pM.next()
                        for j in range(4):
                            tr(p[0:nr, j * 128:(j + 1) * 128], sq2[:, half * 4 + j, c0:c0 + nr], ident_f, [sq2, cf], [p])

                        def fill(tl, p=p, nr=nr):
                            vcopy(tl[0:nr, 0:512], p[0:nr, 0:512], [p], [tl])
                        stage_out(nr, 512, fill, [(ydst[:, half * 512:(half + 1) * 512], 0, 512)])

        convD_layer = [None]

        def seq_layer_setup(sq_, l):
            S.mark(f"setup{l}_{sq_['name']}")
            past = sq_["past"]
            b = sq_["b"]
            PL = pls[:, l, :]
            if convD_layer[0] != l:
                convD_layer[0] = l
                for j in range(62):
                    vts(convD[:, j, :], ident_f, PL[:, O_CW + j:O_CW + j + 1], None, ALU.mult, None, [cf, pls], [convD])
            S.op("dve", lambda e: e.memset(Cbase[:], 0.0), writes=[Cbase])
            S.op("dve", lambda e: e.memset(fV[:, :, :, 64:65], 1.0), writes=[fV])
            if past == 0:
                S.op("pool", lambda e: e.memset(cext[:, :, 0:30], 0.0), writes=[cext])
                S.op("dve", lambda e: e.memset(hS[:], 0.0), writes=[hS])
                S.op("dve", lambda e: e.memset(hSb[:], 0.0), writes=[hSb])
                for mb in range(2):
                    S.dma("sp", xtok[:, :], I["mem_prompt"][mb * 128:(mb + 1) * 128, :], d_xtok, writes=[xtok])
                    for g0 in range(0, KC, 4):
                        p = pM.next()
                        for j in range(4):
                            tr(p[:, j * 128:(j + 1) * 128], xtok[:, (g0 + j) * 128:(g0 + j + 1) * 128], ident_f, [xtok, cf], [p])
                        for j in range(4):
                            vcopy(memT[:, g0 + j, mb * 128:(mb + 1) * 128], p[:, j * 128:(j + 1) * 128], [p], [memT])


                for wi, wname in enumerate(("w_mk", "w_mv")):
                    for g0 in range(2):
                        t, v = load_slab(wname, l, g0 * 512, 512)
                        if wi == 0:
                            for j in range(4):
                                p = pA.next()
                                for kc in range(KC):
                                    mm(p[:, 0:NMEM], v[:, kc, j * 128:(j + 1) * 128], memT[:, kc, :], kc == 0, kc == KC - 1, [t, memT], [p])
                                vcopy(mKT[:, g0 * 4 + j, :], p[:, 0:NMEM], [p], [mKT])
                        for mb in range(2):
                            p = pA.next()
                            for kc in range(KC):
                                mm(p[:, 0:512], memT[:, kc, mb * 128:(mb + 1) * 128], v[:, kc, :], kc == 0, kc == KC - 1, [t, memT], [p])

                            def fill(tl, p=p):
                                act(tl[:, 0:512], p[:, 0:512], AF.Copy, [p], [tl])
                            dst = sq_["mem_k" if wi == 0 else "mem_v"][l][mb * 128:(mb + 1) * 128, g0 * 512:(g0 + 1) * 512]
                            tl = stage_out(128, 512, fill, [(dst, 0, 512)])
                            if wi == 1:
                                vcopy(mV[:, mb, g0 * 512:(g0 + 1) * 512], tl[:, 0:512], [tl], [mV], eng="pool")
            else:
                npb = past // 128
                for cc_ in range(2):
                    S.dma("pool", cext[:, cc_, 0:30], I["cache_conv"][l, b][:, cc_ * 128:(cc_ + 1) * 128].rearrange("r p -> p r"), d_cext, writes=[cext],
                          allow_slow_non_contiguous=True)
                S.dma("sp", hS[:], I["state_hgrn"][l, b].rearrange("h k v -> k h v"), d_hS, writes=[hS])
                vcopy(hSb[:], hS[:], [hS], [hSb])
                for h_ in range(H):
                    S.dma("sp", fV[:, 0:npb, h_, 0:64], CC["cache_fox_v"][l, b][:, h_ * 64:(h_ + 1) * 64].rearrange("(n p) d -> p n d", p=128), d_fV, reads=[("cc",)], writes=[fV])
                S.dma("sp", sV[:, 0:npb, :], CC["cache_sb_v"][l, b].rearrange("(n p) g -> p n g", p=128), d_sV, reads=[("cc",)], writes=[sV])
                for (cn, dstT) in (("cache_fox_k", fKT), ("cache_sb_k", sKT)):
                    S.dma("sp", ktmp[:, 0:npb, :], CC[cn][l, b].rearrange("(n p) g -> p n g", p=128), d_ktmp, reads=[("cc",)], writes=[ktmp])
                    for c in range(2):
                        for g0 in range(0, npb, 4):
                            pt = pM.next()
                            ptb = pt[:].bitcast(BF16)
                            ng = min(4, npb - g0)
                            for j in range(ng):
                                tr(ptb[:, j * 128:(j + 1) * 128], ktmp[:, g0 + j, c * 128:(c + 1) * 128], ident_b, [ktmp, cb], [pt])
                            vcopy(dstT[:, c, g0 * 128:(g0 + ng) * 128], ptb[:, 0:ng * 128], [pt], [dstT])
                S.dma("sp", flfp[:, 0:npb, :], I["cache_fox_logf"][l, b].rearrange("(n p) h -> p n h", p=128), d_flfp, writes=[flfp])
                for bk in range(npb):
                    fox_c_block(flfp[:, bk, :], flfp, 128, bk, False, 0)
                S.dma("sp", mV[:], CC["cache_mem_v"][l, b].rearrange("(n p) g -> p n g", p=128), d_mV, reads=[("cc",)], writes=[mV])
                S.dma("sp", mtmp[:], CC["cache_mem_k"][l, b].rearrange("(n p) g -> p n g", p=128), d_mtmp, reads=[("cc",)], writes=[mtmp])
                for mb in range(2):
                    for g0 in range(0, 8, 4):
                        pt = pM.next()
                        ptb = pt[:].bitcast(BF16)
                        for j in range(4):
                            tr(ptb[:, j * 128:(j + 1) * 128], mtmp[:, mb, (g0 + j) * 128:(g0 + j + 1) * 128], ident_b, [mtmp, cb], [pt])
                        for j in range(4):
                            vcopy(mKT[:, g0 + j, mb * 128:(mb + 1) * 128], ptb[:, j * 128:(j + 1) * 128], [pt], [mKT])

        seqs = [dict(name="p", L=SEQ, T=TP, past=0, b=0, xoff=0, x_src=I["x_prompt"], y=O["y_prompt"], first_seq=True,
                     conv=[O["conv_p"][l] for l in range(DEPTH)], fox_k=[O["fox_k_p"][l] for l in range(DEPTH)],
                     fox_v=[O["fox_v_p"][l] for l in range(DEPTH)], fox_logf=[O["fox_logf_p"][l] for l in range(DEPTH)],
                     hgrn=[O["hgrn_p"][l] for l in range(DEPTH)], sb_k=[O["sb_k_p"][l] for l in range(DEPTH)],
                     sb_v=[O["sb_v_p"][l] for l in range(DEPTH)], mem_k=[O["mem_k_p"][l] for l in range(DEPTH)],
                     mem_v=[O["mem_v_p"][l] for l in range(DEPTH)])]
        for s_ in range(NS):
            seqs.append(dict(name=f"s{s_}", L=DS, T=DS, past=PAST, b=s_, xoff=SEQ + s_ * DS, x_src=I["x_sample"][s_], y=O["y_sample"][s_], first_seq=False,
                             conv=[O["conv_s"][l, s_] for l in range(DEPTH)], fox_k=[O["fox_k_s"][l, s_] for l in range(DEPTH)],
                             fox_v=[O["fox_v_s"][l, s_] for l in range(DEPTH)], fox_logf=[O["fox_logf_s"][l, s_] for l in range(DEPTH)],
                             hgrn=[O["hgrn_s"][l, s_] for l in range(DEPTH)], sb_k=[O["sb_k_s"][l, s_] for l in range(DEPTH)],
                             sb_v=[O["sb_v_s"][l, s_] for l in range(DEPTH)]))
        for l in range(DEPTH):
            sp_ = seqs[0]
            seq_layer_setup(sp_, l)
            ntl = sp_["L"] // sp_["T"]
            for ti in range(ntl):
                layer_tile(sp_, l, ti)
            tpg = TG // sp_["T"]
            for g in range(sp_["L"] // TG):
                outs = [(sp_["y"][g * TG + j * 128:g * TG + (j + 1) * 128, :], j * 128, 128) for j in range(TG // 128)]
                ffn_pass(l, g * TG, TG, [("xs", "p", g * tpg + j) for j in range(tpg)], outs)
        for l in range(DEPTH):
            for sq_ in seqs[1:]:
                seq_layer_setup(sq_, l)
                layer_tile(sq_, l, 0)
            outs = [(sq_["y"], i * DS, DS) for i, sq_ in enumerate(seqs[1:])]
            ffn_pass(l, SEQ, NS * DS, [("xs", sq_["name"], 0) for sq_ in seqs[1:]], outs)
        S.finalize(final_waits=[d_oflf, d_ouF, d_ohS] + d_stg + [d_xs])
        n_ops = {e: len(S.ops[e]) for e in S.ENGS}
    return nc, n_ops


FULL_CFG = dict(SEQ=4096, T=256, DEPTH=4, PAST=2048, DS=64, NS=2)


def host_pack(inputs, cfg):
    DEPTH = cfg["DEPTH"]
    pl = np.zeros((DEPTH, 128, NPL), np.float32)
    for l in range(DEPTH):
        pl[l, :, O_NMIX:O_NMIX + 8] = inputs["norm_mix"][l].reshape(8, 128).T
        pl[l, :, O_NMEM:O_NMEM + 8] = inputs["norm_mem"][l].reshape(8, 128).T
        pl[l, :, O_NFFN:O_NFFN + 8] = inputs["norm_ffn"][l].reshape(8, 128).T
        cw = inputs["conv_w"][l]
        for ch in range(2):
            pl[l, :, O_CW + ch * 31:O_CW + (ch + 1) * 31] = cw[:, ch * 128:(ch + 1) * 128].T
        pl[l, :, O_CB:O_CB + 2] = inputs["conv_b"][l].reshape(2, 128).T
        pl[l, :, O_LG:O_LG + 2] = inputs["conv_ln_g"][l].reshape(2, 128).T
        pl[l, :, O_LB:O_LB + 2] = inputs["conv_ln_b"][l].reshape(2, 128).T
        pl[l, :, O_BF:O_BF + 4] = inputs["b_fox_f"][l][None, :]
        pl[l, :, O_HN:O_HN + 256] = np.tile(inputs["hgrn_norm"][l], 4)[None, :]
    gl = np.zeros((128, 8 + DEPTH * 4), np.float32)
    gl[:, 0:8] = inputs["norm_final"].reshape(8, 128).T
    gl[0:64, 8:] = inputs["hgrn_lb"].reshape(DEPTH, 4, 64).transpose(2, 0, 1).reshape(64, DEPTH * 4)
    return pl, gl


_NC_CACHE = {}


def make_in_maps(inputs, cfg, n_cores, nb_prompt):
    pl, gl = host_pack(inputs, cfg)
    consts = make_consts(cfg["T"])
    NS = cfg["NS"]
    DEPTH = cfg["DEPTH"]
    maps = []
    f = lambda a: np.ascontiguousarray(a, dtype=np.float32)
    for c in range(n_cores):
        bp = c % nb_prompt
        ss = slice(c * NS, (c + 1) * NS)
        m = {
            "x_prompt": f(inputs["x_prompt"][bp]), "x_sample": f(inputs["x_sample"][ss]), "mem_prompt": f(inputs["mem_prompt"][bp]),
            "cache_conv": f(inputs["cache_conv"][:, ss]),
            "cache_fox_k": f(inputs["cache_fox_k"][:, ss].reshape(DEPTH, NS, -1, G)), "cache_fox_v": f(inputs["cache_fox_v"][:, ss].reshape(DEPTH, NS, -1, G)),
            "cache_sb_k": f(inputs["cache_sb_k"][:, ss].reshape(DEPTH, NS, -1, G)), "cache_sb_v": f(inputs["cache_sb_v"][:, ss].reshape(DEPTH, NS, -1, G)),
            "cache_fox_logf": f(inputs["cache_fox_logf"][:, ss]), "state_hgrn": f(inputs["state_hgrn"][:, ss]),
            "cache_mem_k": f(inputs["cache_mem_k"][:, ss].reshape(DEPTH, NS, NMEM, D)), "cache_mem_v": f(inputs["cache_mem_v"][:, ss].reshape(DEPTH, NS, NMEM, D)),
            "pl": pl, "gl": gl, "consts": consts,
        }
        for n in ("w_in", "w_out", "w_mq", "w_mk", "w_mv", "w_mo", "w_up", "w_down"):
            m[n] = f(inputs[n])
        maps.append(m)
    return maps


def assemble(results, cfg, n_cores, nb_prompt, nb_sample):
    DEPTH, SEQ, DS, NS = cfg["DEPTH"], cfg["SEQ"], cfg["DS"], cfg["NS"]
    R = results
    P = lambda k, shp: np.stack([np.asarray(R[b][k], np.float32) for b in range(nb_prompt)], axis=0 if k.startswith("y_") else 1).reshape(shp)
    y_p = np.stack([R[b]["y_prompt"] for b in range(nb_prompt)], 0)
    y_s = np.concatenate([R[c]["y_sample"] for c in range(n_cores)], 0)

    def pp(k, tail):
        return np.stack([np.asarray(R[b][k]) for b in range(nb_prompt)], 1).reshape((DEPTH, nb_prompt) + tail)

    def ps_(k, tail):
        return np.concatenate([np.asarray(R[c][k]) for c in range(n_cores)], 1).reshape((DEPTH, nb_sample) + tail)
    outs = (y_p, y_s,
            pp("conv_p", (CW - 1, G)), pp("fox_k_p", (SEQ, H, HD)), pp("fox_v_p", (SEQ, H, HD)), pp("fox_logf_p", (SEQ, H)),
            pp("hgrn_p", (H, HD, HD)), pp("sb_k_p", (SEQ, H, HD)), pp("sb_v_p", (SEQ, H, HD)),
            pp("mem_k_p", (NMEM, H, 256)), pp("mem_v_p", (NMEM, H, 256)),
            ps_("conv_s", (CW - 1, G)), ps_("fox_k_s", (DS, H, HD)), ps_("fox_v_s", (DS, H, HD)), ps_("fox_logf_s", (DS, H)),
            ps_("hgrn_s", (H, HD, HD)), ps_("sb_k_s", (DS, H, HD)), ps_("sb_v_s", (DS, H, HD)))
    return tuple(np.ascontiguousarray(o, dtype=np.float32) for o in outs)


def kernel(**inputs):
    cfg = FULL_CFG
    inputs = {k: np.asarray(v) for k, v in inputs.items()}
    if "nc" not in _NC_CACHE:
        _NC_CACHE["nc"] = build(cfg)[0]
    nc = _NC_CACHE["nc"]
    maps = make_in_maps(inputs, cfg, 8, 4)
    res = run_bass_kernel_spmd(nc, maps, core_ids=list(range(8)))
    return assemble(res.results, cfg, 8, 4, 16)
```
